# Optimizing a Trainium2 kernel written in Bass

```python
import math
import jax, jax.numpy as jnp
from jax import lax
import numpy as np

D_MODEL = 2048
BATCH = 32
SEQ = 256
DEPTH = 1
DEC_BATCH = 2
DEC_SEQ = 1024
PAST_LEN = 256

GRID_W = 64
D_SSM = 1024
SSM_CH = 16
SSM_GROUPS = D_SSM // SSM_CH
SSM_STATE = 64
N_HEADS = 8
N_KV_HEADS = 2
HEAD_DIM = 128
Q_PER_KV = N_HEADS // N_KV_HEADS
D_ATTN = N_HEADS * HEAD_DIM
D_KV = N_KV_HEADS * HEAD_DIM
WINDOW = 128
BLOCK = 128
D_FF = 5632
CONV_WIDTH = 3
ROPE_THETA = 10000.0
EPS = 1e-6
NEG_INF = -1e30
IN_COLS = D_SSM + D_ATTN + 2 * D_KV + 2 * D_MODEL

kernel_name = 'hybrid_s5_swa_prefix_dit_step'


def rmsnorm(x, g):
    xf = x.astype(jnp.float32)
    y = xf * lax.rsqrt(jnp.mean(xf * xf, axis=-1, keepdims=True) + EPS)
    return (y * g.astype(jnp.float32)).astype(x.dtype)


def adaln(cond, w, b):
    m = jax.nn.silu(cond) @ w + b
    if m.ndim == 2:
        m = m[:, None, :]
    return jnp.split(m, 6, axis=-1)


def axial_rope_tables(T):
    rows = T // GRID_W
    row = jnp.repeat(jnp.arange(rows, dtype=jnp.float32), GRID_W)
    col = jnp.tile(jnp.arange(GRID_W, dtype=jnp.float32), rows)
    n_freq = HEAD_DIM // 4
    inv = ROPE_THETA ** (-jnp.arange(n_freq, dtype=jnp.float32) / n_freq)
    ang_r = row[:, None] * inv[None, :]
    ang_c = col[:, None] * inv[None, :]
    return (jnp.cos(ang_r), jnp.sin(ang_r), jnp.cos(ang_c), jnp.sin(ang_c))


def rotate_half(x, cos, sin):
    x1, x2 = jnp.split(x, 2, axis=-1)
    return jnp.concatenate([x1 * cos - x2 * sin, x1 * sin + x2 * cos], axis=-1)


def apply_axial_rope(x, tables):
    cos_r, sin_r, cos_c, sin_c = tables
    shp = (cos_r.shape[0],) + (1,) * (x.ndim - 3) + (cos_r.shape[1],)
    xr, xc = jnp.split(x.astype(jnp.float32), 2, axis=-1)
    out = jnp.concatenate([rotate_half(xr, cos_r.reshape(shp), sin_r.reshape(shp)),
                           rotate_half(xc, cos_c.reshape(shp), sin_c.reshape(shp))], axis=-1)
    return out.astype(x.dtype)


def ssm_discretize(lam_re, lam_im, log_dt, b_re, b_im):
    lam = lax.complex(lam_re.astype(jnp.float32), lam_im.astype(jnp.float32))
    dt = jnp.exp(log_dt.astype(jnp.float32))[:, None]
    lam_bar = jnp.exp(lam * dt)
    bmat = lax.complex(b_re.astype(jnp.float32), b_im.astype(jnp.float32))
    b_bar = ((lam_bar - 1.0) / lam)[..., None] * bmat
    return lam_bar, b_bar


def scan_combine(e1, e2):
    a1, b1 = e1
    a2, b2 = e2
    return a1 * a2, a2 * b1 + b2


def ssm_scan(u_g, lam_bar, b_bar, h0, reverse):
    bu = jnp.einsum('gpc,btgc->btgp', b_bar, u_g.astype(jnp.complex64))
    edge = -1 if reverse else 0
    bu = bu.at[:, edge].add(lam_bar[None] * h0)
    a = jnp.broadcast_to(lam_bar, bu.shape)
    _, h = lax.associative_scan(scan_combine, (a, bu), axis=1, reverse=reverse)
    return h


def ssm_branch(u, p, h0_re, h0_im):
    bsz, T, _ = u.shape
    uf = u.astype(jnp.float32)
    ug = uf.reshape(bsz, T, SSM_GROUPS, SSM_CH)
    h0 = lax.complex(h0_re.astype(jnp.float32), h0_im.astype(jnp.float32))
    y = uf * p['ssm_d'].astype(jnp.float32)
    finals = []
    for d in range(2):
        lam_bar, b_bar = ssm_discretize(p['lam_re'][d], p['lam_im'][d], p['log_dt'][d],
                                        p['b_re'][d], p['b_im'][d])
        h = ssm_scan(ug, lam_bar, b_bar, h0[:, d], reverse=(d == 1))
        cmat = lax.complex(p['c_re'][d].astype(jnp.float32), p['c_im'][d].astype(jnp.float32))
        y = y + jnp.real(jnp.einsum('gcp,btgp->btgc', cmat, h)).reshape(bsz, T, D_SSM)
        finals.append(h[:, -1] if d == 0 else h[:, 0])
    hf = jnp.stack(finals, axis=1)
    z = jax.nn.gelu(y).astype(u.dtype)
    z = z * jax.nn.sigmoid(z @ p['w_glu'])
    return z, jnp.real(hf), jnp.imag(hf)


def sink_column(sink, s):
    sk = sink.astype(jnp.float32).reshape(N_KV_HEADS, Q_PER_KV, 1, 1)
    return jnp.broadcast_to(sk, s.shape[:-1] + (1,))


def attention_context(q, k, v, sink):
    bsz, S = q.shape[:2]
    nqb = S // BLOCK
    qb = jnp.moveaxis(q.reshape(bsz, nqb, BLOCK, N_KV_HEADS, Q_PER_KV, HEAD_DIM), 1, 0)

    def one_block(qblk):
        s = jnp.einsum('bqkgd,bskd->bkgqs', qblk, k).astype(jnp.float32)
        pr = jax.nn.softmax(jnp.concatenate([s, sink_column(sink, s)], axis=-1), axis=-1)[..., :-1]
        return jnp.einsum('bkgqs,bskd->bqkgd', pr.astype(v.dtype), v)

    o = lax.map(one_block, qb)
    return jnp.moveaxis(o, 0, 1).reshape(bsz, S, D_ATTN)


def attention_latent(q, k, v, ck, cv, sink):
    bsz, T = q.shape[:2]
    nb = T // BLOCK
    pad = ((0, 0), (BLOCK, BLOCK), (0, 0), (0, 0))
    kp = jnp.pad(k, pad)
    vp = jnp.pad(v, pad)
    ar_q = jnp.arange(BLOCK)
    ar_k = jnp.arange(3 * BLOCK)

    def one_block(bi):
        start = bi * BLOCK
        qblk = lax.dynamic_slice_in_dim(q, start, BLOCK, axis=1)
        kblk = lax.dynamic_slice_in_dim(kp, start, 3 * BLOCK, axis=1)
        vblk = lax.dynamic_slice_in_dim(vp, start, 3 * BLOCK, axis=1)
        qpos = start + ar_q
        kpos = start - BLOCK + ar_k
        valid = ((jnp.abs(qpos[:, None] - kpos[None, :]) <= WINDOW)
                 & (kpos >= 0)[None, :] & (kpos < T)[None, :])
        s_loc = jnp.einsum('bqkgd,bskd->bkgqs', qblk, kblk).astype(jnp.float32)
        s_loc = jnp.where(valid, s_loc, NEG_INF)
        s_ctx = jnp.einsum('bqkgd,bskd->bkgqs', qblk, ck).astype(jnp.float32)
        pr = jax.nn.softmax(jnp.concatenate([s_loc, s_ctx, sink_column(sink, s_loc)], axis=-1), axis=-1)
        p_loc = pr[..., :3 * BLOCK].astype(v.dtype)
        p_ctx = pr[..., 3 * BLOCK:-1].astype(v.dtype)
        return (jnp.einsum('bkgqs,bskd->bqkgd', p_loc, vblk)
                + jnp.einsum('bkgqs,bskd->bqkgd', p_ctx, cv))

    o = lax.map(one_block, jnp.arange(nb))
    return jnp.moveaxis(o, 0, 1).reshape(bsz, T, D_ATTN)


def mixer(xm, p, h0_re, h0_im, rope, ctx_kv):
    bsz, T, _ = xm.shape
    proj = xm @ p['w_in']
    o1 = D_SSM
    o2 = o1 + D_ATTN
    o3 = o2 + D_KV
    o4 = o3 + D_KV
    o5 = o4 + D_MODEL
    u, q, k, v, g_s, g_a = jnp.split(proj, [o1, o2, o3, o4, o5], axis=-1)
    y_ssm, h_re, h_im = ssm_branch(u, p, h0_re, h0_im)
    q = q.reshape(bsz, T, N_KV_HEADS, Q_PER_KV, HEAD_DIM)
    k = k.reshape(bsz, T, N_KV_HEADS, HEAD_DIM)
    v = v.reshape(bsz, T, N_KV_HEADS, HEAD_DIM)
    if rope is not None:
        q = apply_axial_rope(q, rope)
        k = apply_axial_rope(k, rope)
    q = q * (HEAD_DIM ** -0.5)
    if ctx_kv is None:
        o = attention_context(q, k, v, p['sink'])
    else:
        o = attention_latent(q, k, v, ctx_kv[0], ctx_kv[1], p['sink'])
    merged = jax.nn.sigmoid(g_s) * (y_ssm @ p['w_ssm_o']) + jax.nn.sigmoid(g_a) * (o @ p['w_attn_o'])
    return merged @ p['w_out'], k, v, h_re, h_im


def conv_ffn(xm, p):
    h = xm @ p['w_up']
    T = h.shape[1]
    hp = jnp.pad(h, ((0, 0), (1, 1), (0, 0)))
    w = p['conv_w']
    h = hp[:, :T] * w[0] + hp[:, 1:T + 1] * w[1] + hp[:, 2:] * w[2] + p['conv_b']
    a, b = jnp.split(h, 2, axis=-1)
    return (jax.nn.silu(a) * b) @ p['w_down']


def layer_forward(x, mod, p, h0_re, h0_im, rope, ctx_kv):
    sh1, sc1, g1, sh2, sc2, g2 = mod
    xm = rmsnorm(x, p['norm_mix_g']) * (1.0 + sc1) + sh1
    out, k, v, h_re, h_im = mixer(xm, p, h0_re, h0_im, rope, ctx_kv)
    x = x + g1 * out
    xm2 = rmsnorm(x, p['norm_ffn_g']) * (1.0 + sc2) + sh2
    x = x + g2 * conv_ffn(xm2, p)
    return x, k, v, h_re, h_im


def setup_inputs(seed: int = 0) -> dict:
    key = jax.random.key(seed)
    ks = jax.random.split(key, 32)
    nrm = jax.random.normal
    f32 = jnp.float32
    L = DEPTH
    lam_im_base = math.pi * jnp.arange(SSM_STATE, dtype=f32)
    return {
        'x_prompt': nrm(ks[0], (BATCH, SEQ, D_MODEL), f32),
        'x_sample': nrm(ks[1], (DEC_BATCH, DEC_SEQ, D_MODEL), f32),
        'cache_k': nrm(ks[2], (DEC_BATCH, L, PAST_LEN, N_KV_HEADS, HEAD_DIM), f32),
        'cache_v': nrm(ks[3], (DEC_BATCH, L, PAST_LEN, N_KV_HEADS, HEAD_DIM), f32),
        'state_ssm_re': 0.1 * nrm(ks[4], (DEC_BATCH, L, 2, SSM_GROUPS, SSM_STATE), f32),
        'state_ssm_im': 0.1 * nrm(ks[5], (DEC_BATCH, L, 2, SSM_GROUPS, SSM_STATE), f32),
        'c': nrm(ks[6], (DEC_BATCH, D_MODEL), f32),
        'c_ctx': nrm(ks[7], (D_MODEL,), f32),
        'norm_mix_g': 1.0 + 0.01 * nrm(ks[8], (L, D_MODEL), f32),
        'norm_ffn_g': 1.0 + 0.01 * nrm(ks[9], (L, D_MODEL), f32),
        'w_mod': 0.5 * D_MODEL ** -0.5 * nrm(ks[10], (L, D_MODEL, 6 * D_MODEL), f32),
        'b_mod': 0.01 * nrm(ks[11], (L, 6 * D_MODEL), f32),
        'w_in': D_MODEL ** -0.5 * nrm(ks[12], (L, D_MODEL, IN_COLS), f32),
        'ssm_lambda_re': -0.5 + 0.01 * nrm(ks[13], (L, 2, SSM_GROUPS, SSM_STATE), f32),
        'ssm_lambda_im': lam_im_base + 0.01 * nrm(ks[14], (L, 2, SSM_GROUPS, SSM_STATE), f32),
        'ssm_log_dt': jax.random.uniform(ks[15], (L, 2, SSM_GROUPS), f32,
                                         minval=math.log(1e-3), maxval=math.log(1e-1)),
        'ssm_b_re': (2 * SSM_CH) ** -0.5 * nrm(ks[16], (L, 2, SSM_GROUPS, SSM_STATE, SSM_CH), f32),
        'ssm_b_im': (2 * SSM_CH) ** -0.5 * nrm(ks[17], (L, 2, SSM_GROUPS, SSM_STATE, SSM_CH), f32),
        'ssm_c_re': SSM_STATE ** -0.5 * nrm(ks[18], (L, 2, SSM_GROUPS, SSM_CH, SSM_STATE), f32),
        'ssm_c_im': SSM_STATE ** -0.5 * nrm(ks[19], (L, 2, SSM_GROUPS, SSM_CH, SSM_STATE), f32),
        'ssm_d': nrm(ks[20], (L, D_SSM), f32),
        'w_glu': D_SSM ** -0.5 * nrm(ks[21], (L, D_SSM, D_SSM), f32),
        'attn_sink': 0.5 * nrm(ks[22], (L, N_HEADS), f32),
        'w_ssm_o': D_SSM ** -0.5 * nrm(ks[23], (L, D_SSM, D_MODEL), f32),
        'w_attn_o': D_ATTN ** -0.5 * nrm(ks[24], (L, D_ATTN, D_MODEL), f32),
        'w_out': D_MODEL ** -0.5 * nrm(ks[25], (L, D_MODEL, D_MODEL), f32),
        'w_up': D_MODEL ** -0.5 * nrm(ks[26], (L, D_MODEL, 2 * D_FF), f32),
        'conv_w': CONV_WIDTH ** -0.5 * nrm(ks[27], (L, CONV_WIDTH, 2 * D_FF), f32),
        'conv_b': 0.01 * nrm(ks[28], (L, 2 * D_FF), f32),
        'w_down': D_FF ** -0.5 * nrm(ks[29], (L, D_FF, D_MODEL), f32),
        'final_norm_g': 1.0 + 0.01 * nrm(ks[30], (D_MODEL,), f32),
    }


def reference(x_prompt, x_sample, cache_k, cache_v, state_ssm_re, state_ssm_im, c, c_ctx,
              norm_mix_g, norm_ffn_g, w_mod, b_mod, w_in, ssm_lambda_re, ssm_lambda_im,
              ssm_log_dt, ssm_b_re, ssm_b_im, ssm_c_re, ssm_c_im, ssm_d, w_glu, attn_sink,
              w_ssm_o, w_attn_o, w_out, w_up, conv_w, conv_b, w_down, final_norm_g):
    rope = axial_rope_tables(x_sample.shape[1])
    zero_h = jnp.zeros((x_prompt.shape[0], 2, SSM_GROUPS, SSM_STATE), jnp.float32)
    xp = x_prompt
    xs = x_sample
    ks_out, vs_out, hre_out, him_out = [], [], [], []
    for l in range(DEPTH):
        p = {
            'norm_mix_g': norm_mix_g[l], 'norm_ffn_g': norm_ffn_g[l], 'w_in': w_in[l],
            'lam_re': ssm_lambda_re[l], 'lam_im': ssm_lambda_im[l], 'log_dt': ssm_log_dt[l],
            'b_re': ssm_b_re[l], 'b_im': ssm_b_im[l], 'c_re': ssm_c_re[l], 'c_im': ssm_c_im[l],
            'ssm_d': ssm_d[l], 'w_glu': w_glu[l], 'sink': attn_sink[l],
            'w_ssm_o': w_ssm_o[l], 'w_attn_o': w_attn_o[l], 'w_out': w_out[l],
            'w_up': w_up[l], 'conv_w': conv_w[l], 'conv_b': conv_b[l], 'w_down': w_down[l],
        }
        mod_ctx = adaln(c_ctx, w_mod[l], b_mod[l])
        xp, k_l, v_l, h_re, h_im = layer_forward(xp, mod_ctx, p, zero_h, zero_h, None, None)
        ks_out.append(k_l)
        vs_out.append(v_l)
        hre_out.append(h_re)
        him_out.append(h_im)
        mod_lat = adaln(c, w_mod[l], b_mod[l])
        xs, _, _, _, _ = layer_forward(xs, mod_lat, p, state_ssm_re[:, l], state_ssm_im[:, l], rope,
                                       (cache_k[:, l], cache_v[:, l]))
    y_prompt = rmsnorm(xp, final_norm_g)
    y_sample = rmsnorm(xs, final_norm_g)
    new_cache_k = jnp.stack(ks_out, axis=1)
    new_cache_v = jnp.stack(vs_out, axis=1)
    new_state_ssm_re = jnp.stack(hre_out, axis=1)
    new_state_ssm_im = jnp.stack(him_out, axis=1)
    return (y_prompt, y_sample, new_cache_k, new_cache_v, new_state_ssm_re, new_state_ssm_im)
```

```python
import math
from contextlib import ExitStack
import numpy as np
import concourse.bass as bass
import concourse.mybir as mybir
from concourse.bass_utils import run_bass_kernel_spmd

F32 = mybir.dt.float32
BF16 = mybir.dt.bfloat16
ACT = mybir.ActivationFunctionType
ALU = mybir.AluOpType
AX = mybir.AxisListType
ENGS = ("pe", "act", "dve", "pool", "sp")

NTOK = 1280
TB = [(0, 512), (512, 512), (1024, 256)]
D = 2048
KC = 16
DFF = 5632
NG = 64
NJ = 160
SCALE = 128 ** -0.5
EPS = 1e-6
PI = math.pi


class Prog:
    def __init__(self, nc, stack, ndma=8):
        self.nc = nc
        self.ops = {e: [] for e in ENGS}
        self.count = {e: 0 for e in ENGS}
        self.waited = {e: {} for e in ENGS}
        self.last_w = {}
        self.readers = {}
        self.sems = {}
        self.pending = {e: [] for e in ENGS}
        for e in ENGS:
            if e != "sp":
                self.sems[e] = stack.enter_context(nc.semaphore("sem_" + e))
        self.dma_sems = {}
        self.dma_cnt = {}
        self.dma_rr = {}
        for e in ("sp", "act", "pool"):
            self.dma_sems[e] = [stack.enter_context(nc.semaphore("dsem_%s_%d" % (e, i))) for i in range(ndma)]
            self.dma_cnt[e] = [0] * ndma
            self.dma_rr[e] = 0
        self.out_deps = []

    def _add(self, eng, reads, writes, token, extra_deps=(), skip_same=False):
        deps = set(extra_deps)
        deps.update(self.pending[eng])
        self.pending[eng] = []
        for k in reads:
            if k in self.last_w:
                deps.update(self.last_w[k])
            if isinstance(k, str) and k.startswith("ps"):
                for d in self.readers.get(k, ()):
                    if d[0] != eng:
                        deps.add(d)
        for k in writes:
            if k in self.last_w:
                deps.update(self.last_w[k])
            for d in self.readers.get(k, ()):
                deps.add(d)
        best = {}
        for (sk, v) in deps:
            if skip_same and sk == eng:
                continue
            if v > best.get(sk, 0):
                best[sk] = v
        waits = []
        for sk, v in best.items():
            if self.waited[eng].get(sk, 0) < v:
                self.waited[eng][sk] = v
                waits.append((sk, v))
        for k in writes:
            self.last_w[k] = [token]
            self.readers[k] = []
        for k in reads:
            self.readers.setdefault(k, []).append(token)
        return waits

    def op(self, eng, fn, reads=(), writes=(), skip_same=False):
        self.count[eng] += 1
        token = (eng, self.count[eng])
        waits = self._add(eng, reads, writes, token, skip_same=skip_same)
        self.ops[eng].append((waits, fn, (eng, 1)))
        return token

    def dma(self, eng, fn, reads=(), writes=(), is_out=False):
        i = self.dma_rr[eng]
        self.dma_rr[eng] = (i + 1) % len(self.dma_sems[eng])
        prev = self.dma_cnt[eng][i]
        self.dma_cnt[eng][i] += 1
        sk = ("d", eng, i)
        token = (sk, 16 * (prev + 1))
        extra = [(sk, 16 * prev)] if prev > 0 else []
        waits = self._add(eng, reads, writes, token, extra_deps=extra)
        self.ops[eng].append((waits, fn, (sk, 16)))
        if is_out:
            self.out_deps.append(token)
        return token

    def join(self, newkey, keys):
        toks = []
        for k in keys:
            toks.extend(self.last_w.get(k, []))
        self.last_w[newkey] = toks
        self.readers[newkey] = []

    def barrier(self):
        toks = []
        for e in ENGS:
            if e != "sp" and self.count[e] > 0:
                toks.append((e, self.count[e]))
        for e in ("sp", "act", "pool"):
            for i, c in enumerate(self.dma_cnt[e]):
                if c > 0:
                    toks.append((("d", e, i), 16 * c))
        for e in ENGS:
            self.pending[e] = list(toks)

    def _sem(self, sk):
        if isinstance(sk, tuple):
            return self.dma_sems[sk[1]][sk[2]]
        return self.sems[sk]

    def emit(self):
        nc = self.nc
        best = {}
        for (sk, v) in self.out_deps:
            best[sk] = max(best.get(sk, 0), v)
        final_waits = list(best.items())
        with nc.Block() as block:
            def run(eng_name, eng):
                for waits, fn, (sk, inc) in self.ops[eng_name]:
                    for (wk, v) in waits:
                        eng.wait_ge(self._sem(wk), v)
                    ins = fn(eng)
                    ins.then_inc(self._sem(sk), inc)
                if eng_name == "sp":
                    for (wk, v) in final_waits:
                        eng.wait_ge(self._sem(wk), v)

            @block.sync
            def _(e):
                run("sp", e)

            @block.tensor
            def _(e):
                run("pe", e)

            @block.scalar
            def _(e):
                run("act", e)

            @block.vector
            def _(e):
                run("dve", e)

            @block.gpsimd
            def _(e):
                run("pool", e)


IN_SPECS = {
    "xT": (128, 16, NTOK), "cond": (128, 32), "bmod": (128, 96), "gvec": (128, 48),
    "wmod": (48, 128, 16, 256), "wqk": (5, 128, 16, 256), "wv": (1, 128, 16, 256), "wu": (4, 128, 16, 256),
    "wgs": (8, 128, 16, 256), "wga": (8, 128, 16, 256), "wglu": (2, 128, 8, 512), "wso": (4, 128, 8, 512),
    "wao": (4, 128, 8, 512), "wout": (8, 128, 16, 256), "wupa": (22, 128, 16, 256), "wupb": (22, 128, 16, 256),
    "wdown": (16, 128, 44, 128),
    "ident": (128, 128), "anti": (128, 128), "anti32": (128, 128), "psw": (128, 128), "mmask": (128, 2, 128),
    "ssmp": (128, 2, 4, 64),
    "ssmb": (128, 2, 2, 1024),
    "ssmc": (128, 2, 2, 1024),
    "sgn": (128, 2), "dvec": (128, 64), "bm": (128, NJ),
    "ropeC": (128, NTOK), "ropeS": (128, NTOK), "ckT": (128, 2, 256), "cv": (128, 2, 256),
    "maskL": (128, 10, 384), "cmask": (128, 10), "sink": (128, 8),
    "convw": (128, 3, 88), "convb": (128, 88), "bmL": (128, NTOK), "bmR": (128, NTOK),
}
OUT_SPECS = {"yT": (128, 16, NTOK), "kTo": (256, NTOK), "vo": (NTOK, 256), "hfin": (128, 2, 64, 5)}


def build(stop=None):
    nc = bass.Bass("TRN2", target_bir_lowering=False)
    din = {k: nc.dram_tensor(k, list(s), F32, kind="ExternalInput").ap() for k, s in IN_SPECS.items()}
    dout = {k: nc.dram_tensor(k, list(s), F32, kind="ExternalOutput").ap() for k, s in OUT_SPECS.items()}
    scr = lambda name, shape, dt=F32: nc.dram_tensor(name, list(shape), dt).ap()
    qk_T = scr("qk_T", [1280, NTOK])
    sg_T = scr("sg_T", [4096, NTOK])
    z_T = scr("z_T", [1024, NTOK])
    z2_T = scr("z2_T", [1024, NTOK])
    o_T = scr("o_T", [1024, NTOK])
    mrg_T = scr("mrg_T", [2048, NTOK])
    x1_T = scr("x1_T", [2048, NTOK])
    x2_T = scr("x2_T", [2048, NTOK])
    act_T = scr("act_T", [DFF, NTOK], BF16)
    sm_T = scr("sm_T", [2, 128, 6976])

    with ExitStack() as st:
        P = Prog(nc, st)
        ARENA = 53000
        arena = st.enter_context(nc.sbuf_tensor("arena", [128, ARENA], F32))
        psbig = st.enter_context(nc.psum_tensor("psbig", [128, 4096], F32))
        off = [0]

        def alloc(n, dt=F32):
            nb4 = n if dt == F32 else (n + 1) // 2
            a = arena[:, off[0]:off[0] + nb4]
            off[0] += nb4
            assert off[0] <= ARENA, off[0]
            if dt == F32:
                return a
            v = a.bitcast(dt)
            return v[:, :n] if 2 * nb4 != n else v

        psrr = [1]
        ps_lo = [1]

        def nps():
            i = psrr[0]
            psrr[0] = i + 1 if i + 1 < 8 else ps_lo[0]
            return psbig[:, i * 512:(i + 1) * 512], "ps%d" % i

        def bank(i):
            return psbig[:, i * 512:(i + 1) * 512], "ps%d" % i

        def r3(ap, **kw):
            return ap

        def load_const(name, n=None, eng="sp"):
            shp = IN_SPECS[name]
            n = int(np.prod(shp[1:]))
            t = alloc(n)
            src = din[name]
            if len(shp) == 3:
                src = src.rearrange("p a b -> p (a b)")
            elif len(shp) == 4:
                src = src.rearrange("p a b c -> p (a b c)")
            P.dma(eng, lambda e, t=t, src=src: e.dma_start(out=t, in_=src), writes=[name])
            return t

        ident_f = load_const("ident")
        anti_f = load_const("anti")
        anti32_f = load_const("anti32")
        psw_f = load_const("psw")
        mmask = load_const("mmask")
        cond_f = load_const("cond")
        bmod = load_const("bmod")
        gvec = load_const("gvec")
        sgn = load_const("sgn")
        ident_b = alloc(128, BF16)
        anti_b = alloc(128, BF16)
        anti32_b = alloc(128, BF16)
        ones_b = alloc(128, BF16)
        psw_b = alloc(128, BF16)
        P.op("dve", lambda e: e.tensor_copy(out=ident_b, in_=ident_f), reads=["ident"], writes=["ident_b"])
        P.op("dve", lambda e: e.tensor_copy(out=anti_b, in_=anti_f), reads=["anti"], writes=["anti_b"])
        P.op("dve", lambda e: e.tensor_copy(out=anti32_b, in_=anti32_f), reads=["anti32"], writes=["anti32_b"])
        P.op("dve", lambda e: e.memset(ones_b, 1.0), writes=["ones_b"])
        P.op("dve", lambda e: e.tensor_copy(out=psw_b, in_=psw_f), reads=["psw"], writes=["psw_b"])
        modv = alloc(192)
        a1 = alloc(32)
        a2 = alloc(32)
        NSLOT = 3
        SLOT = 6144
        wbuf_off = off[0]
        wbuf = [alloc(SLOT, BF16) for _ in range(NSLOT)]
        wbuf_end = off[0]
        wrr = [0]

        def wload(src_blk, n):
            i = wrr[0]
            wrr[0] = (i + 1) % NSLOT
            dst = wbuf[i][:, :n]
            P.dma("pool", lambda e, dst=dst, src_blk=src_blk: e.dma_start(out=dst, in_=src_blk), writes=["wbuf%d" % i])
            return wbuf[i], "wbuf%d" % i

        ssmp2 = load_const("ssmp").rearrange("p (d a g) -> p d a g", d=2, a=4)
        base_mark = off[0]
        sg_p = sgn[:, 0:1]
        sg_n = sgn[:, 1:2]
        PERS = [("L7re", 512), ("L7im", 512), ("LCre", 512), ("LCim", 512), ("L7imS", 512), ("LCreS", 512), ("Ec8", 512), ("Es8", 512),
                ("Qc", 320), ("Qs", 320), ("Ba", 1024), ("Bb", 1024), ("rho", 64), ("pc8", 64), ("ps8", 64)]
        NPERS = sum(n for _, n in PERS)

        def pers_views(buf):
            out, o = {}, 0
            for nm, n in PERS:
                out[nm] = buf[:, o:o + n]
                o += n
            return out

        def linear_fm(src, src_key, wname, Kc, NB, epi, after_block=None, src_k=None):
            if src_k is None:
                src_k = lambda k: src[:, k, :]
            wd = din[wname]
            nblk = IN_SPECS[wname][0]
            for b in range(nblk):
                if after_block is not None and b > 0:
                    after_block()
                wb, wkey = wload(wd[b].rearrange("p k n -> p (k n)"), Kc * NB)
                wv = wb[:, :Kc * NB].rearrange("p (k n) -> p k n", k=Kc)
                for mi in range(NB // 128):
                    m = b * (NB // 128) + mi
                    for (t0, n) in TB:
                        ps, pkey = nps()
                        for k in range(Kc):
                            P.op("pe", lambda e, ps=ps, wv=wv, k=k, mi=mi, t0=t0, n=n: e.matmul(
                                ps[:, :n], lhsT=wv[:, k, mi * 128:(mi + 1) * 128], rhs=src_k(k)[:, t0:t0 + n],
                                start=(k == 0), stop=(k == Kc - 1)),
                                reads=[wkey, src_key], writes=[pkey], skip_same=True)
                        epi(m, t0, n, ps, pkey)

        if stop == 100:
            P.dma("sp", lambda e: e.dma_start(out=dout["hfin"].rearrange("p a b c -> p (a b c)")[:, 0:128], in_=ident_f), reads=["ident", "ident_b", "ones_b"], is_out=True)
            P.emit()
            return nc
        m0 = off[0]
        scb = alloc(32, BF16)
        P.op("act", lambda e: e.activation(out=scb, in_=cond_f, func=ACT.Silu), reads=["cond"], writes=["scb"])
        scv = scb.rearrange("p (k c) -> p k c", c=2)
        psm, psmk = bank(0)
        psmB, psmBk = bank(6)
        modv3 = modv.rearrange("p (c t) -> p c t", t=2)

        def mod_block(b):
            wb, wkey = wload(din["wmod"][b].rearrange("p k n -> p (k n)"), 4096)
            wv = wb[:, :4096].rearrange("p (k n) -> p k n", k=16)
            for mi in range(2):
                ci = b * 2 + mi
                for k in range(16):
                    if ci < 32:
                        dst, dk = psm[:, ci * 2:ci * 2 + 2], psmk
                    else:
                        dst, dk = psmB[:, (ci - 32) * 2:(ci - 32) * 2 + 2], psmBk
                    P.op("pe", lambda e, wv=wv, k=k, mi=mi, dst=dst: e.matmul(
                        dst, lhsT=wv[:, k, mi * 128:(mi + 1) * 128], rhs=scv[:, k, :],
                        start=(k == 0), stop=(k == 15)), reads=[wkey, "scb"], writes=[dk], skip_same=True)

        def mod_evac(c0, c1, key):
            src_, sk_ = (psm[:, 2 * c0:2 * c1], psmk) if c0 < 32 else (psmB[:, 2 * (c0 - 32):2 * (c1 - 32)], psmBk)
            P.op("dve", lambda e: e.tensor_tensor(out=modv3[:, c0:c1, :], in0=src_.rearrange("p (c t) -> p c t", t=2),
                                                  in1=bmod[:, c0:c1].unsqueeze(2).broadcast_to([128, c1 - c0, 2]), op=ALU.add),
                 reads=[sk_, "bmod"], writes=[key])

        def emit_modA():
            for b in range(16):
                mod_block(b)
            mod_evac(0, 32, "modvA")
            ps_lo[0] = 0
            make_a(a1v, 16, 0, "a1", "modvA")
        mod_pending = list(range(16, 48))

        def mod_more(nmax):
            for _ in range(nmax):
                if mod_pending:
                    mod_block(mod_pending.pop(0))

        a1v = a1.rearrange("p (k t) -> p k t", t=2)
        a2v = a2.rearrange("p (k t) -> p k t", t=2)
        gv = gvec.rearrange("p (a k) -> p a k", a=3)

        def make_a(av, sc0, gi, key, mkey):
            P.op("dve", lambda e: e.tensor_scalar(out=av, in0=modv3[:, sc0:sc0 + 16, :], scalar1=1.0, scalar2=None, op0=ALU.add), reads=[mkey], writes=[key])
            P.op("dve", lambda e: e.tensor_tensor(out=av, in0=av, in1=gv[:, gi, :].unsqueeze(2).broadcast_to([128, 16, 2]), op=ALU.mult),
                 reads=[key, "gvec"], writes=[key])

        sh1 = modv3[:, 0:16, :]
        g1 = modv3[:, 32:48, :]
        sh2 = modv3[:, 48:64, :]
        g2 = modv3[:, 80:96, :]

        u_tmA = alloc(64 * 128, BF16)
        u_tmB = alloc(64 * 128, BF16)
        mU = off[0]
        v_tm = alloc(14 * 256, BF16)
        stage = [alloc(512) for _ in range(3)]
        m_att = off[0]
        xm = alloc(16 * NTOK, BF16)
        xm3 = xm.rearrange("p (k t) -> p k t", k=16)
        HEAD = {}
        for d_ in range(2):
            H = {nm: alloc(64) for nm in ("dt", "th", "mg", "c", "s")}
            H["rp"] = alloc(9 * 64); H["rn"] = alloc(8 * 64)
            HEAD[d_] = H

            def head_ops(d=d_, H=H):
                hk = "smh%d" % d
                lamre = ssmp2[:, d, 0, :]; lamim = ssmp2[:, d, 1, :]; logdt = ssmp2[:, d, 2, :]
                rp = H["rp"].rearrange("p (k g) -> p k g", k=9); rn = H["rn"].rearrange("p (k g) -> p k g", k=8)
                P.op("act", lambda e: e.activation(out=H["dt"], in_=logdt, func=ACT.Exp), reads=["ssmp"], writes=[hk])
                P.op("dve", lambda e: e.tensor_tensor(out=H["th"], in0=lamim, in1=H["dt"], op=ALU.mult), reads=["ssmp", hk], writes=[hk])
                P.op("dve", lambda e: e.tensor_tensor(out=H["mg"], in0=lamre, in1=H["dt"], op=ALU.mult), reads=["ssmp", hk], writes=[hk])
                P.op("act", lambda e: e.activation(out=H["s"], in_=H["th"], func=ACT.Sin, scale=1.0 / 64), reads=[hk], writes=[hk])
                P.op("act", lambda e: e.activation(out=H["c"], in_=H["th"], func=ACT.Sin, scale=1.0 / 64, bias=PI / 2), reads=[hk], writes=[hk])
                for k in range(9):
                    P.op("act", lambda e, k=k: e.activation(out=rp[:, k, :], in_=H["mg"], func=ACT.Exp, scale=float(k)), reads=[hk], writes=[hk])
                for k in range(8):
                    P.op("act", lambda e, k=k: e.activation(out=rn[:, k, :], in_=H["mg"], func=ACT.Exp, scale=-float(k)), reads=[hk], writes=[hk])
            head_ops()
        m1 = off[0]

        def rms_phase(tag, load_fn, scale_fn, bias_fn, out_fn, rkeys, wkey, store_fn=None, NBK=256, defer=None):
            sets = [(alloc(16 * NBK), alloc(16 * NBK, BF16), alloc(NBK), alloc(NBK)) for _ in range(2)]
            blocks = []
            for bi, t0 in enumerate(range(0, NTOK, NBK)):
                n = min(NBK, NTOK - t0)
                ci = 0 if t0 < 1024 else 1
                xs, sq, rt, rstd = sets[bi % 2]
                kx, ksq, krt, krs = [tag + nm + str(bi % 2) for nm in ("xs", "sq", "rt", "rstd")]
                xs3 = xs.rearrange("p (k t) -> p k t", k=16)[:, :, :n]
                sq3 = sq.rearrange("p (k t) -> p k t", k=16)[:, :, :n]
                rt = rt[:, :n]
                rstd = rstd[:, :n]

                def blk_stat(xs3=xs3, sq3=sq3, rt=rt, rstd=rstd, kx=kx, ksq=ksq, krt=krt, krs=krs, t0=t0, n=n, ci=ci):
                    load_fn(xs3, t0, n, kx)
                    P.op("act", lambda e: e.activation(out=sq3, in_=xs3, func=ACT.Square), reads=[kx], writes=[ksq])
                    ps, pk = nps()
                    for k in range(16):
                        P.op("pe", lambda e, k=k: e.matmul(ps[:, :n], lhsT=ones_b, rhs=sq3[:, k, :], start=(k == 0), stop=(k == 15)),
                             reads=["ones_b", ksq], writes=[pk], skip_same=True)
                    P.op("act", lambda e: e.activation(out=rt, in_=ps[:, :n], func=ACT.Sqrt, scale=1.0 / D, bias=EPS), reads=[pk], writes=[krt])
                    P.op("dve", lambda e: e.reciprocal(out=rstd, in_=rt), reads=[krt], writes=[krs])
                    P.op("dve", lambda e: e.tensor_tensor(out=xs3, in0=xs3, in1=rstd.unsqueeze(1).broadcast_to([128, 16, n]), op=ALU.mult),
                         reads=[kx, krs], writes=[kx])

                def blk_scale(xs3=xs3, kx=kx, t0=t0, n=n, ci=ci):
                    for k in range(16):
                        o = out_fn(xs3, k, t0, n)
                        sc = scale_fn(k, ci)
                        bs = bias_fn(k, ci) if bias_fn is not None else None
                        wk = [((wkey if wkey is not None else kx), k % 2)]
                        if k % 2 == 0:
                            if bs is None:
                                P.op("dve", lambda e, o=o, k=k, sc=sc: e.tensor_scalar(out=o, in0=xs3[:, k, :], scalar1=sc, scalar2=None, op0=ALU.mult),
                                     reads=[kx] + rkeys, writes=wk)
                            else:
                                P.op("dve", lambda e, o=o, k=k, sc=sc, bs=bs: e.tensor_scalar(out=o, in0=xs3[:, k, :], scalar1=sc, scalar2=bs,
                                                                                             op0=ALU.mult, op1=ALU.add), reads=[kx] + rkeys, writes=wk)
                        else:
                            P.op("act", lambda e, o=o, k=k, sc=sc, bs=bs: e.activation(out=o, in_=xs3[:, k, :], func=ACT.Identity, scale=sc,
                                                                                      bias=(bs if bs is not None else 0.0)),
                                 reads=[kx] + rkeys, writes=wk)
                    if store_fn is not None:
                        store_fn(xs3, t0, n, [kx, (kx, 0), (kx, 1)])
                blocks.append((blk_stat, blk_scale))
            if defer is None:
                for st_, sc_ in blocks:
                    st_()
                    sc_()
            else:
                blocks[0][0]()
                blocks[1][0]()
                defer()
                blocks[0][1]()
                for i_ in range(2, len(blocks)):
                    blocks[i_][0]()
                    blocks[i_ - 1][1]()
                blocks[-1][1]()
            if wkey is not None:
                P.join(wkey, [(wkey, 0), (wkey, 1)])

        def load_x(dst, t0, n, key):
            P.dma("sp", lambda e: e.dma_start(out=dst, in_=din["xT"][:, :, t0:t0 + n]), writes=[key])

        rms_phase("n1", load_x, lambda k, ci: a1v[:, k, ci:ci + 1], lambda k, ci: sh1[:, k, ci:ci + 1],
                  lambda xs3, k, t0, n: xm3[:, k, t0:t0 + n], ["a1", "modvA"], "xm", defer=emit_modA)
        off[0] = m1
        P.barrier()
        if stop == 0:
            P.emit()
            return nc

        srr = [0]

        def nstage():
            i = srr[0]
            srr[0] = (i + 1) % 3
            return stage[i], "stage%d" % i

        def epi_store(dst_T, row0, func=None, tagk=None):
            def epi(m, t0, n, ps, pkey):
                sg, sk = nstage()
                if func is None:
                    P.op("dve", lambda e: e.tensor_copy(out=sg[:, :n], in_=ps[:, :n]), reads=[pkey], writes=[sk])
                else:
                    P.op("act", lambda e: e.activation(out=sg[:, :n], in_=ps[:, :n], func=func), reads=[pkey], writes=[sk])
                r = row0 + m * 128
                P.dma("sp", lambda e: e.dma_start(out=dst_T[r:r + 128, t0:t0 + n], in_=sg[:, :n]), reads=[sk],
                      writes=[(tagk, m, t0)])
            return epi

        linear_fm(xm3, "xm", "wqk", 16, 256, epi_store(qk_T, 0, None, "qk"))
        m_pre = off[0]
        ssmb = load_const("ssmb").rearrange("p (d a n) -> p d a n", d=2, a=2)
        SCR = {}
        for nm in ("t1", "t2", "inv", "nre", "nim", "fre", "fim", "tmp"):
            SCR[nm] = alloc(64)
        SCR["Lpre"] = alloc(9 * 64); SCR["Lpim"] = alloc(9 * 64); SCR["Lnre"] = alloc(8 * 64); SCR["Lnim"] = alloc(8 * 64)
        SCR["pc"] = alloc(9 * 64); SCR["ps"] = alloc(9 * 64)
        tA = alloc(1024); tB = alloc(1024)
        pers_tmp = alloc(NPERS)
        def ssm_pre(d):
            S = pers_views(pers_tmp)
            S.update(SCR)
            S.update(HEAD[d])
            kk = "sm"
            lamre = ssmp2[:, d, 0, :]; lamim = ssmp2[:, d, 1, :]; logdt = ssmp2[:, d, 2, :]

            def dv(fn, S=S, kk=kk):
                P.op("dve", fn, reads=[kk, "ssmp", "ssmb", "sgn", "smh%d" % d], writes=[kk])

            for it in range(6):
                dv(lambda e, S=S: e.tensor_tensor(out=S["t1"], in0=S["c"], in1=S["c"], op=ALU.mult))
                dv(lambda e, S=S: e.tensor_tensor(out=S["t2"], in0=S["s"], in1=S["s"], op=ALU.mult))
                dv(lambda e, S=S: e.scalar_tensor_tensor(out=S["s"], in0=S["c"], scalar=2.0, in1=S["s"], op0=ALU.mult, op1=ALU.mult))
                dv(lambda e, S=S: e.tensor_tensor(out=S["c"], in0=S["t1"], in1=S["t2"], op=ALU.subtract))
            pc = S["pc"].rearrange("p (k g) -> p k g", k=9); ps_ = S["ps"].rearrange("p (k g) -> p k g", k=9)
            rp = S["rp"].rearrange("p (k g) -> p k g", k=9); rn = S["rn"].rearrange("p (k g) -> p k g", k=8)
            dv(lambda e, pc=pc: e.memset(pc[:, 0, :], 1.0))
            dv(lambda e, ps_=ps_: e.memset(ps_[:, 0, :], 0.0))
            for k in range(1, 9):
                dv(lambda e, S=S, pc=pc, k=k: e.tensor_tensor(out=S["t1"], in0=pc[:, k - 1, :], in1=S["c"], op=ALU.mult))
                dv(lambda e, S=S, ps_=ps_, k=k: e.tensor_tensor(out=S["t2"], in0=ps_[:, k - 1, :], in1=S["s"], op=ALU.mult))
                dv(lambda e, S=S, pc=pc, k=k: e.tensor_tensor(out=S["tmp"], in0=pc[:, k - 1, :], in1=S["s"], op=ALU.mult))
                dv(lambda e, S=S, pc=pc, k=k: e.tensor_tensor(out=pc[:, k, :], in0=S["t1"], in1=S["t2"], op=ALU.subtract))
                dv(lambda e, S=S, ps_=ps_, k=k: e.tensor_tensor(out=S["t1"], in0=ps_[:, k - 1, :], in1=S["c"], op=ALU.mult))
                dv(lambda e, S=S, ps_=ps_, k=k: e.tensor_tensor(out=ps_[:, k, :], in0=S["t1"], in1=S["tmp"], op=ALU.add))
            Lpre = S["Lpre"].rearrange("p (k g) -> p k g", k=9); Lpim = S["Lpim"].rearrange("p (k g) -> p k g", k=9)
            Lnre = S["Lnre"].rearrange("p (k g) -> p k g", k=8); Lnim = S["Lnim"].rearrange("p (k g) -> p k g", k=8)
            dv(lambda e, Lpre=Lpre, rp=rp, pc=pc: e.tensor_tensor(out=Lpre, in0=rp, in1=pc, op=ALU.mult))
            dv(lambda e, Lpim=Lpim, rp=rp, ps_=ps_: e.tensor_tensor(out=Lpim, in0=rp, in1=ps_, op=ALU.mult))
            dv(lambda e, Lnre=Lnre, rn=rn, pc=pc: e.tensor_tensor(out=Lnre, in0=rn, in1=pc[:, 0:8, :], op=ALU.mult))
            dv(lambda e, Lnim=Lnim, rn=rn, ps_=ps_: e.scalar_tensor_tensor(out=Lnim, in0=rn, scalar=-1.0, in1=ps_[:, 0:8, :], op0=ALU.mult, op1=ALU.mult))
            L7re = S["L7re"].rearrange("p (g s) -> p g s", s=8); L7im = S["L7im"].rearrange("p (g s) -> p g s", s=8)
            LCre = S["LCre"].rearrange("p (g s) -> p g s", s=8); LCim = S["LCim"].rearrange("p (g s) -> p g s", s=8)
            for s in range(8):
                kq = 7 - s if d == 0 else s
                dv(lambda e, L7re=L7re, Lpre=Lpre, s=s, kq=kq: e.tensor_copy(out=L7re[:, :, s], in_=Lpre[:, kq, :]))
                dv(lambda e, L7im=L7im, Lpim=Lpim, s=s, kq=kq: e.tensor_copy(out=L7im[:, :, s], in_=Lpim[:, kq, :]))
                dv(lambda e, LCre=LCre, Lnre=Lnre, s=s, kq=kq: e.tensor_copy(out=LCre[:, :, s], in_=Lnre[:, kq, :]))
                dv(lambda e, LCim=LCim, Lnim=Lnim, s=s, kq=kq: e.tensor_copy(out=LCim[:, :, s], in_=Lnim[:, kq, :]))
            dv(lambda e, S=S: e.tensor_scalar(out=S["L7imS"], in0=S["L7im"], scalar1=sg_p, scalar2=None, op0=ALU.mult))
            dv(lambda e, S=S: e.tensor_scalar(out=S["LCreS"], in0=S["LCre"], scalar1=sg_p, scalar2=None, op0=ALU.mult))
            dv(lambda e, S=S, rp=rp: e.tensor_copy(out=S["rho"], in_=rp[:, 8, :]))
            dv(lambda e, S=S, Lpre=Lpre: e.tensor_scalar(out=S["nre"], in0=Lpre[:, 1, :], scalar1=-1.0, scalar2=None, op0=ALU.add))
            dv(lambda e, S=S, Lpim=Lpim: e.tensor_copy(out=S["nim"], in_=Lpim[:, 1, :]))
            dv(lambda e, S=S, lamre=lamre: e.tensor_tensor(out=S["t1"], in0=lamre, in1=lamre, op=ALU.mult))
            dv(lambda e, S=S, lamim=lamim: e.tensor_tensor(out=S["t2"], in0=lamim, in1=lamim, op=ALU.mult))
            dv(lambda e, S=S: e.tensor_tensor(out=S["t1"], in0=S["t1"], in1=S["t2"], op=ALU.add))
            dv(lambda e, S=S: e.reciprocal(out=S["inv"], in_=S["t1"]))
            dv(lambda e, S=S, lamre=lamre: e.tensor_tensor(out=S["t1"], in0=S["nre"], in1=lamre, op=ALU.mult))
            dv(lambda e, S=S, lamim=lamim: e.tensor_tensor(out=S["t2"], in0=S["nim"], in1=lamim, op=ALU.mult))
            dv(lambda e, S=S: e.tensor_tensor(out=S["t1"], in0=S["t1"], in1=S["t2"], op=ALU.add))
            dv(lambda e, S=S: e.tensor_tensor(out=S["fre"], in0=S["t1"], in1=S["inv"], op=ALU.mult))
            dv(lambda e, S=S, lamre=lamre: e.tensor_tensor(out=S["t1"], in0=S["nim"], in1=lamre, op=ALU.mult))
            dv(lambda e, S=S, lamim=lamim: e.tensor_tensor(out=S["t2"], in0=S["nre"], in1=lamim, op=ALU.mult))
            dv(lambda e, S=S: e.tensor_tensor(out=S["t1"], in0=S["t1"], in1=S["t2"], op=ALU.subtract))
            dv(lambda e, S=S: e.tensor_tensor(out=S["fim"], in0=S["t1"], in1=S["inv"], op=ALU.mult))
            ba = ssmb[:, d, 0, :].rearrange("p (g c) -> p g c", c=16)
            bb = ssmb[:, d, 1, :].rearrange("p (g c) -> p g c", c=16)
            Ba = S["Ba"].rearrange("p (g c) -> p g c", c=16); Bb = S["Bb"].rearrange("p (g c) -> p g c", c=16)
            freb = S["fre"].unsqueeze(2).broadcast_to([128, 64, 16]); fimb = S["fim"].unsqueeze(2).broadcast_to([128, 64, 16])
            tA3 = tA.rearrange("p (g c) -> p g c", c=16); tB3 = tB.rearrange("p (g c) -> p g c", c=16)
            dv(lambda e, tA3=tA3, ba=ba, freb=freb: e.tensor_tensor(out=tA3, in0=ba, in1=freb, op=ALU.mult))
            dv(lambda e, tB3=tB3, bb=bb, fimb=fimb: e.scalar_tensor_tensor(out=tB3, in0=bb, scalar=sg_p, in1=fimb, op0=ALU.mult, op1=ALU.mult))
            dv(lambda e, Ba=Ba, tA3=tA3, tB3=tB3: e.tensor_tensor(out=Ba, in0=tA3, in1=tB3, op=ALU.subtract))
            dv(lambda e, tA3=tA3, bb=bb, freb=freb: e.tensor_tensor(out=tA3, in0=bb, in1=freb, op=ALU.mult))
            dv(lambda e, tB3=tB3, ba=ba, fimb=fimb: e.scalar_tensor_tensor(out=tB3, in0=ba, scalar=sg_p, in1=fimb, op0=ALU.mult, op1=ALU.mult))
            dv(lambda e, Bb=Bb, tA3=tA3, tB3=tB3: e.tensor_tensor(out=Bb, in0=tA3, in1=tB3, op=ALU.add))

            dv(lambda e, S=S, pc=pc: e.tensor_copy(out=S["pc8"], in_=pc[:, 8, :]))
            dv(lambda e, S=S, ps_=ps_: e.tensor_copy(out=S["ps8"], in_=ps_[:, 8, :]))
            Ec = S["Ec8"].rearrange("p (g j) -> p g j", j=8); Es = S["Es8"].rearrange("p (g j) -> p g j", j=8)
            dv(lambda e, S=S, Ec=Ec: e.tensor_copy(out=Ec[:, :, 0], in_=S["pc8"]))
            dv(lambda e, S=S, Es=Es: e.tensor_copy(out=Es[:, :, 0], in_=S["ps8"]))
            for j in range(1, 8):
                dv(lambda e, S=S, Ec=Ec, j=j: e.tensor_tensor(out=S["t1"], in0=Ec[:, :, j - 1], in1=S["pc8"], op=ALU.mult))
                dv(lambda e, S=S, Es=Es, j=j: e.tensor_tensor(out=S["t2"], in0=Es[:, :, j - 1], in1=S["ps8"], op=ALU.mult))
                dv(lambda e, S=S, Ec=Ec, j=j: e.tensor_tensor(out=S["tmp"], in0=Ec[:, :, j - 1], in1=S["ps8"], op=ALU.mult))
                dv(lambda e, S=S, Ec=Ec, j=j: e.tensor_tensor(out=Ec[:, :, j], in0=S["t1"], in1=S["t2"], op=ALU.subtract))
                dv(lambda e, S=S, Es=Es, j=j: e.tensor_tensor(out=S["t1"], in0=Es[:, :, j - 1], in1=S["pc8"], op=ALU.mult))
                dv(lambda e, S=S, Es=Es, j=j: e.tensor_tensor(out=Es[:, :, j], in0=S["t1"], in1=S["tmp"], op=ALU.add))
            Qc = S["Qc"].rearrange("p (l g) -> p l g", l=5); Qs = S["Qs"].rearrange("p (l g) -> p l g", l=5)
            dv(lambda e, Qc=Qc, Ec=Ec: e.tensor_copy(out=Qc[:, 0, :], in_=Ec[:, :, 7]))
            dv(lambda e, Qs=Qs, Es=Es: e.tensor_copy(out=Qs[:, 0, :], in_=Es[:, :, 7]))
            for l in range(1, 5):
                dv(lambda e, S=S, Qc=Qc, l=l: e.tensor_tensor(out=S["t1"], in0=Qc[:, l - 1, :], in1=Qc[:, l - 1, :], op=ALU.mult))
                dv(lambda e, S=S, Qs=Qs, l=l: e.tensor_tensor(out=S["t2"], in0=Qs[:, l - 1, :], in1=Qs[:, l - 1, :], op=ALU.mult))
                dv(lambda e, S=S, Qc=Qc, Qs=Qs, l=l: e.scalar_tensor_tensor(out=Qs[:, l, :], in0=Qc[:, l - 1, :], scalar=2.0, in1=Qs[:, l - 1, :], op0=ALU.mult, op1=ALU.mult))
                dv(lambda e, S=S, Qc=Qc, l=l: e.tensor_tensor(out=Qc[:, l, :], in0=S["t1"], in1=S["t2"], op=ALU.subtract))


        def pers_store(d):
            P.dma("sp", lambda e, d=d: e.dma_start(out=sm_T[d], in_=pers_tmp), reads=["sm"], writes=[("smT", d)])

        ssm_pre(0)
        linear_fm(xm3, "xm", "wgs", 16, 256, epi_store(sg_T, 0, ACT.Sigmoid, "sgs"))
        pers_store(0)
        ssm_pre(1)
        linear_fm(xm3, "xm", "wga", 16, 256, epi_store(sg_T, 2048, ACT.Sigmoid, "sga"))
        pers_store(1)
        off[0] = m_pre
        if stop == 110:
            P.emit()
            return nc
        P.dma("sp", lambda e: e.dma_start(out=dout["kTo"], in_=qk_T[1024:1280, :]),
              reads=[("qk", m, t0) for m in range(8, 10) for (t0, n) in TB], is_out=True)
        if stop == 111:
            P.emit()
            return nc
        v3 = v_tm.rearrange("p (t c) -> p t c", c=256)
        P.op("dve", lambda e: e.memset(v_tm, 0.0), writes=["v_tm"])
        wb, wkey = wload(din["wv"][0].rearrange("p k n -> p (k n)"), 4096)
        wv_ = wb[:, :4096].rearrange("p (k n) -> p k n", k=16)
        for tt in range(10):
            ps, pk = nps()
            for k in range(16):
                P.op("pe", lambda e, ps=ps, k=k, tt=tt: e.matmul(ps[:, :256], lhsT=xm3[:, k, tt * 128:(tt + 1) * 128], rhs=wv_[:, k, :],
                                                               start=(k == 0), stop=(k == 15)), reads=[wkey, "xm"], writes=[pk], skip_same=True)
            sg, sk = nstage()
            P.op("act", lambda e, ps=ps, sg=sg: e.activation(out=sg[:, :256], in_=ps[:, :256], func=ACT.Identity), reads=[pk], writes=[sk])
            pt = tt + 1 if tt < 8 else tt + 3
            import os
            VAR = os.environ.get("KVAR", "")
            if "a" not in VAR:
                P.op("dve", lambda e, sg=sg, pt=pt: e.tensor_copy(out=v3[:, pt, :], in_=sg[:, :256]), reads=[sk], writes=["v_tm"])
            if "b" not in VAR:
                P.dma("sp", lambda e, sg=sg, tt=tt: e.dma_start(out=dout["vo"][tt * 128:(tt + 1) * 128, :], in_=sg[:, :256]), reads=[sk], is_out=True)
        if stop == 112:
            P.emit()
            return nc
        uA4 = u_tmA.rearrange("p (g s c) -> p g s c", g=64, s=8)
        uB4 = u_tmB.rearrange("p (g s c) -> p g s c", g=64, s=8)
        xmA = xm3[:, :, 0:1024].rearrange("p k (j s) -> p k j s", s=8)
        xmB = xm3[:, :, 1024:1280].rearrange("p k (j s) -> p k j s", s=8)
        for b in range(4):
            wb, wkey = wload(din["wu"][b].rearrange("p k n -> p (k n)"), 4096)
            wvu = wb[:, :4096].rearrange("p (k n) -> p k n", k=16)
            for s in range(8):
                psA, pkA = nps()
                for k in range(16):
                    P.op("pe", lambda e, psA=psA, k=k, s=s, wvu=wvu: e.matmul(psA[:, :256], lhsT=xmA[:, k, :, s], rhs=wvu[:, k, :],
                                                                             start=(k == 0), stop=(k == 15)), reads=[wkey, "xm"], writes=[pkA], skip_same=True)
                P.op("dve", lambda e, psA=psA, s=s, b=b: e.tensor_copy(out=uA4[:, b * 16:(b + 1) * 16, s, :],
                                                                      in_=psA[:, :256].rearrange("p (g c) -> p g c", c=16)),
                     reads=[pkA], writes=["u_tmA"])
                psB, pkB = nps()
                for k in range(16):
                    P.op("pe", lambda e, psB=psB, k=k, s=s, wvu=wvu: e.matmul(psB[0:32, :256], lhsT=xmB[:, k, :, s], rhs=wvu[:, k, :],
                                                                             start=(k == 0), stop=(k == 15)), reads=[wkey, "xm"], writes=[pkB], skip_same=True)
                P.op("act", lambda e, psB=psB, s=s, b=b: e.activation(out=uB4[0:32, b * 16:(b + 1) * 16, s, :],
                                                                     in_=psB[0:32, :256].rearrange("p (g c) -> p g c", c=16), func=ACT.Identity),
                     reads=[pkB], writes=["u_tmB"])
        off[0] = m_att
        P.barrier()
        if stop == 1:
            P.emit()
            return nc

        qT = alloc(8 * NTOK, BF16)
        qT3 = qT.rearrange("p (h t) -> p h t", h=8)
        kTp = alloc(2 * 1792, BF16)
        kT3 = kTp.rearrange("p (h t) -> p h t", h=2)
        P.op("dve", lambda e: e.memset(kTp, 0.0), writes=["kTp"])
        ropeC = load_const("ropeC")
        ropeS = load_const("ropeS")
        maskL = load_const("maskL").rearrange("p (q k) -> p q k", q=10)
        cmask = load_const("cmask")
        sink = load_const("sink")
        nsink = alloc(8)
        P.op("dve", lambda e: e.tensor_scalar(out=nsink, in0=sink, scalar1=-1.0, scalar2=None, op0=ALU.mult), reads=["sink"], writes=["nsink"])
        ck_f = load_const("ckT")
        cv_f = load_const("cv")
        ck_b = alloc(512, BF16)
        cv_b = alloc(512, BF16)
        P.op("dve", lambda e: e.tensor_copy(out=ck_b, in_=ck_f), reads=["ckT"], writes=["ck_b"])
        P.op("dve", lambda e: e.tensor_copy(out=cv_b, in_=cv_f), reads=["cv"], writes=["cv_b"])
        ck3 = ck_b.rearrange("p (h t) -> p h t", h=2)
        cv3 = cv_b.rearrange("p (t c) -> p t c", t=2)
        PQ = [(alloc(NTOK), alloc(NTOK)) for _ in range(2)]
        allqk = [("qk", m, t0) for m in range(10) for (t0, n) in TB]

        def rope_head(idx, h):
            Pq, Qq = PQ[idx % 2]
            kp, kq = "Pq%d" % (idx % 2), "Qq%d" % (idx % 2)
            r0 = h * 128
            for part in range(2):
                for half in range(2):
                    rows = r0 + half * 64 + part * 32
                    pp = part * 64 + half * 32
                    qq = (1 - part) * 64 + half * 32
                    P.dma("sp", lambda e, rows=rows, pp=pp: e.dma_start(out=Pq[pp:pp + 32, :], in_=qk_T[rows:rows + 32, :]),
                          reads=allqk, writes=[kp])
                    P.dma("sp", lambda e, rows=rows, qq=qq: e.dma_start(out=Qq[qq:qq + 32, :], in_=qk_T[rows:rows + 32, :]),
                          reads=allqk, writes=[kq])
            P.op("dve", lambda e: e.tensor_tensor(out=Pq, in0=Pq, in1=ropeC, op=ALU.mult), reads=[kp, "ropeC"], writes=[kp])
            P.op("pool", lambda e: e.tensor_tensor(out=Qq, in0=Qq, in1=ropeS, op=ALU.mult), reads=[kq, "ropeS"], writes=[kq])
            if h < 8:
                P.op("dve", lambda e: e.tensor_tensor(out=qT3[:, h, :], in0=Pq, in1=Qq, op=ALU.add), reads=[kp, kq], writes=[("qT", h)])
            else:
                kh = h - 8
                P.op("dve", lambda e: e.tensor_tensor(out=kT3[:, kh, 128:1152], in0=Pq[:, 0:1024], in1=Qq[:, 0:1024], op=ALU.add),
                     reads=[kp, kq, "kTp"], writes=["kTp"])
                P.op("dve", lambda e: e.tensor_tensor(out=kT3[:, kh, 1408:1664], in0=Pq[:, 1024:1280], in1=Qq[:, 1024:1280], op=ALU.add),
                     reads=[kp, kq, "kTp"], writes=["kTp"])

        for idx, h in enumerate([8, 9] + list(range(8))):
            rope_head(idx, h)
        NSET = 3
        smcs = [alloc(640) for _ in range(NSET)]
        pexs = [alloc(640) for _ in range(NSET)]
        pns = [alloc(640, BF16) for _ in range(NSET)]
        pTs = [alloc(640, BF16) for _ in range(NSET)]
        stats = [alloc(8) for _ in range(NSET)]
        ost = [alloc(NTOK), alloc(NTOK)]
        tiles = [(h, qt) for h in range(8) for qt in range(10)]

        def sbanks(ti):
            b0 = 2 * (ti % 2)
            return bank(b0), bank(b0 + 1)

        def tbanks(ti):
            return bank(4), bank(5)

        def attn_scores(ti):
            h, qt = tiles[ti]
            kvh = h // 4
            pt = qt + 1 if qt < 8 else qt + 3
            (psS, kS), (psC, kC) = sbanks(ti)
            P.op("pe", lambda e: e.matmul(psS[:, 0:384], lhsT=qT3[:, h, qt * 128:(qt + 1) * 128], rhs=kT3[:, kvh, (pt - 1) * 128:(pt + 2) * 128],
                                         start=True, stop=True), reads=[("qT", h), "kTp"], writes=[kS], skip_same=True)
            P.op("pe", lambda e: e.matmul(psC[:, 0:256], lhsT=qT3[:, h, qt * 128:(qt + 1) * 128], rhs=ck3[:, kvh, :], start=True, stop=True),
                 reads=[("qT", h), "ck_b"], writes=[kC], skip_same=True)

        def attn_soft_a(ti):
            h, qt = tiles[ti]
            si = ti % NSET
            smc, stat = smcs[si], stats[si]
            K = lambda nm: "%s%d" % (nm, si)
            (psS, kS), (psC, kC) = sbanks(ti)
            P.op("dve", lambda e: e.scalar_tensor_tensor(out=smc[:, 0:384], in0=psS[:, 0:384], scalar=SCALE, in1=maskL[:, qt, :],
                                                        op0=ALU.mult, op1=ALU.add), reads=[kS, "maskL"], writes=[K("smcA")])
            P.op("act", lambda e: e.activation(out=smc[:, 384:640], in_=psC[:, 0:256], func=ACT.Identity, scale=SCALE, bias=cmask[:, qt:qt + 1]),
                 reads=[kC, "cmask"], writes=[K("smcB")])
            P.op("dve", lambda e: e.reduce_max(out=stat[:, 0:1], in_=smc, axis=AX.X, negate=True), reads=[K("smcA"), K("smcB")], writes=[K("stat0")])
            P.op("dve", lambda e: e.tensor_tensor(out=stat[:, 1:2], in0=stat[:, 0:1], in1=nsink[:, h:h + 1], op=ALU.min),
                 reads=[K("stat0"), "nsink"], writes=[K("stat1")])

        def attn_soft_b(ti):
            h, qt = tiles[ti]
            si = ti % NSET
            smc, pex, pn, stat = smcs[si], pexs[si], pns[si], stats[si]
            K = lambda nm: "%s%d" % (nm, si)
            P.op("act", lambda e: e.activation(out=pex, in_=smc, func=ACT.Exp, bias=stat[:, 1:2], scale=1.0, accum_out=stat[:, 2:3]),
                 reads=[K("smcA"), K("smcB"), K("stat1")], writes=[K("pex"), K("stat2")])
            P.op("act", lambda e: e.activation(out=stat[:, 3:4], in_=stat[:, 1:2], func=ACT.Exp, bias=sink[:, h:h + 1], scale=1.0),
                 reads=[K("stat1"), "sink"], writes=[K("stat3")])
            P.op("dve", lambda e: e.tensor_tensor(out=stat[:, 4:5], in0=stat[:, 2:3], in1=stat[:, 3:4], op=ALU.add),
                 reads=[K("stat2"), K("stat3")], writes=[K("stat4")])
            P.op("dve", lambda e: e.reciprocal(out=stat[:, 5:6], in_=stat[:, 4:5]), reads=[K("stat4")], writes=[K("stat5")])
            P.op("dve", lambda e: e.tensor_scalar(out=pn, in0=pex, scalar1=stat[:, 5:6], scalar2=None, op0=ALU.mult),
                 reads=[K("pex"), K("stat5")], writes=[K("pn")])

        def attn_pv(ti):
            h, qt = tiles[ti]
            kvh = h // 4
            pt = qt + 1 if qt < 8 else qt + 3
            si = ti % NSET
            pn, pT = pns[si], pTs[si]
            K = lambda nm: "%s%d" % (nm, si)
            (psT1, kT1), (psX, kX) = tbanks(ti)
            osg = ost[h % 2]
            okey = "ost%d" % (h % 2)
            for i in range(5):
                dst = psT1[:, i * 128:(i + 1) * 128] if i < 4 else psX[:, 0:128]
                P.op("pe", lambda e, dst=dst, i=i: e.matmul(dst, lhsT=pn[:, i * 128:(i + 1) * 128], rhs=ident_b, start=True, stop=True),
                     reads=[K("pn"), "ident_b"], writes=[kT1 if i < 4 else kX], skip_same=True)
            P.op("act", lambda e: e.activation(out=pT[:, 0:512], in_=psT1, func=ACT.Identity), reads=[kT1], writes=[K("pTa")])
            P.op("dve", lambda e: e.tensor_copy(out=pT[:, 512:640], in_=psX[:, 0:128]), reads=[kX], writes=[K("pTb")])
            for i in range(5):
                lhs = v3[:, pt - 1 + i, kvh * 128:(kvh + 1) * 128] if i < 3 else cv3[:, i - 3, kvh * 128:(kvh + 1) * 128]
                P.op("pe", lambda e, lhs=lhs, i=i: e.matmul(psX[:, 128:256], lhsT=lhs, rhs=pT[:, i * 128:(i + 1) * 128], start=(i == 0), stop=(i == 4)),
                     reads=["v_tm", "cv_b", K("pTa"), K("pTb")], writes=[kX], skip_same=True)
            P.op("act", lambda e: e.activation(out=osg[:, qt * 128:(qt + 1) * 128], in_=psX[:, 128:256], func=ACT.Identity), reads=[kX], writes=[okey])
            if qt == 9:
                P.dma("sp", lambda e: e.dma_start(out=o_T[h * 128:(h + 1) * 128, :], in_=osg), reads=[okey], writes=[("o_T", h)])

        NT_ = len(tiles)
        attn_scores(0)
        attn_scores(1)
        attn_soft_a(0)
        attn_scores(2)
        attn_soft_a(1)
        attn_soft_b(0)
        for ti in range(NT_):
            if ti + 3 < NT_:
                attn_scores(ti + 3)
            if ti + 2 < NT_:
                attn_soft_a(ti + 2)
            if ti + 1 < NT_:
                attn_soft_b(ti + 1)
            attn_pv(ti)
            if ti % 2 == 0:
                mod_more(1)
        mod_more(99)
        mod_evac(32, 96, "modvB")
        make_a(a2v, 64, 1, "a2", "modvB")
        off[0] = mU
        P.barrier()
        if stop == 2:
            P.emit()
            return nc

        ssmp = ssmp2
        ssmc = load_const("ssmc").rearrange("p (d a n) -> p d a n", d=2, a=2)
        dvec = load_const("dvec")
        bmv = load_const("bm")
        hfin = alloc(2 * 64 * 5)
        hfin4 = hfin.rearrange("p (d g q) -> p d g q", d=2, g=64)
        small = {}
        for d in range(2):
            pbuf = alloc(NPERS)
            P.dma("sp", lambda e, pbuf=pbuf, d=d: e.dma_start(out=pbuf, in_=sm_T[d]), reads=[("smT", d)], writes=["sm"])
            small[d] = pers_views(pbuf)
        if stop == 3:
            P.emit()
            return nc
        GC = 8
        _save = off[0]
        off[0] = wbuf_off
        g1a_full = alloc(GC * NJ); g1b_full = alloc(GC * NJ)
        Wd = alloc(GC * NJ); Vd = alloc(GC * NJ)
        cosTs = [alloc(GC * NJ), None]; sinTs = [alloc(GC * NJ), None]
        assert off[0] <= wbuf_end, (off[0], wbuf_end)
        off[0] = _save
        cosTs[1] = alloc(GC * NJ); sinTs[1] = alloc(GC * NJ)
        rhoMs = [alloc(GC * NJ), alloc(GC * NJ)]
        ptt1 = alloc(GC * 64); ptt2 = alloc(GC * 64)
        g1a = g1a_full[:, :GC * 128]; g1b = g1b_full[:, :GC * 128]
        PBst = alloc(GC * 128, BF16); PBsw = alloc(GC * 128, BF16); PCst = alloc(GC * 128, BF16); PCsw = alloc(GC * 128, BF16)
        Wst = alloc(GC * 128, BF16); Wsw = alloc(GC * 128, BF16); Mm = alloc(GC * 128, BF16)
        Ugc = alloc(GC * NJ, BF16)
        Tt = g1a_full
        g1b_bf = g1b_full.bitcast(BF16)
        Ah = g1b_bf[:, 0:GC * NJ]; Bh = g1b_bf[:, GC * NJ:2 * GC * NJ]
        qc = alloc(GC); qs = alloc(GC); q1 = alloc(GC); q2 = alloc(GC)
        yA = [alloc(8 * GC * 16, BF16) for _ in range(2)]
        yB = [alloc(8 * GC * 16, BF16) for _ in range(2)]
        zst = [alloc(NTOK)] * 2
        vsel = alloc(GC * 5)

        def v4(t):
            return t.rearrange("p (g s c) -> p g s c", g=GC, s=8)

        def v3g(t):
            return t.rearrange("p (g n) -> p g n", g=GC)

        def vj(t):
            return t.rearrange("p (g j) -> p g j", g=GC)

        jsel = [31, 63, 95, 127, 159]
        def ssm_tables(gc, d, part):
            G0 = gc * GC
            S = small[d]
            kk = "sm"
            par = (gc * 2 + d) % 2
            cosT = cosTs[par]; sinT = sinTs[par]; rhoM = rhoMs[par]
            tabk = "tab%d" % par
            cT = vj(cosT); sT = vj(sinT)
            eng_ = "pool" if part == "lo" else "dve"
            PO = lambda fn, r, w: P.op(eng_, fn, reads=r, writes=w)
            Ec = S["Ec8"].rearrange("p (g j) -> p g j", j=8)[:, G0:G0 + GC, :]
            Es = S["Es8"].rearrange("p (g j) -> p g j", j=8)[:, G0:G0 + GC, :]
            Qc = S["Qc"].rearrange("p (l g) -> p l g", l=5); Qs = S["Qs"].rearrange("p (l g) -> p l g", l=5)
            if part == "lo":
                PO(lambda e: e.tensor_copy(out=cT[:, :, 0:8], in_=Ec), [kk], [tabk])
                PO(lambda e: e.tensor_copy(out=sT[:, :, 0:8], in_=Es), [kk], [tabk])
            for (n, lvl) in ([(8, 0), (16, 1), (32, 2)] if part == "lo" else [(64, 3), (128, 4)]):
                mlen = min(n, NJ - n)
                qcb = Qc[:, lvl, G0:G0 + GC].unsqueeze(2).broadcast_to([128, GC, mlen]); qsb = Qs[:, lvl, G0:G0 + GC].unsqueeze(2).broadcast_to([128, GC, mlen])
                if part == "lo":
                    tt1 = ptt1.rearrange("p (g j) -> p g j", g=GC)[:, :, 0:mlen]; tt2 = ptt2.rearrange("p (g j) -> p g j", g=GC)[:, :, 0:mlen]
                    k1, k2 = "ptt1", "ptt2"
                else:
                    tt1 = vj(Wd)[:, :, 0:mlen]; tt2 = vj(Vd)[:, :, 0:mlen]
                    k1, k2 = "Wd", "Vd"
                PO(lambda e, tt1=tt1, mlen=mlen, qcb=qcb: e.tensor_tensor(out=tt1, in0=cT[:, :, 0:mlen], in1=qcb, op=ALU.mult), [tabk, kk], [k1])
                PO(lambda e, tt2=tt2, mlen=mlen, qsb=qsb: e.tensor_tensor(out=tt2, in0=sT[:, :, 0:mlen], in1=qsb, op=ALU.mult), [tabk, kk], [k2])
                PO(lambda e, tt1=tt1, tt2=tt2, n=n, mlen=mlen: e.tensor_tensor(out=cT[:, :, n:n + mlen], in0=tt1, in1=tt2, op=ALU.subtract), [k1, k2], [tabk])
                PO(lambda e, tt1=tt1, mlen=mlen, qsb=qsb: e.tensor_tensor(out=tt1, in0=cT[:, :, 0:mlen], in1=qsb, op=ALU.mult), [tabk, kk], [k1])
                PO(lambda e, tt2=tt2, mlen=mlen, qcb=qcb: e.tensor_tensor(out=tt2, in0=sT[:, :, 0:mlen], in1=qcb, op=ALU.mult), [tabk, kk], [k2])
                PO(lambda e, tt1=tt1, tt2=tt2, n=n, mlen=mlen: e.tensor_tensor(out=sT[:, :, n:n + mlen], in0=tt1, in1=tt2, op=ALU.add), [k1, k2], [tabk])
            if part == "lo":
              PO(lambda e: e.tensor_tensor(out=vj(rhoM), in0=S["rho"][:, G0:G0 + GC].unsqueeze(2).broadcast_to([128, GC, NJ]),
                                         in1=bmv.unsqueeze(1).broadcast_to([128, GC, NJ]), op=ALU.mult), [kk, "bm"], [tabk])

        def ssm_round(gc, d):
            G0 = gc * GC
            S = small[d]
            kk = "sm"
            rk = "r"
            idm = ident_b if d == 0 else anti_b
            idm32 = ident_b if d == 0 else anti32_b
            for q2_ in range(GC // 2):
                ps, pk = nps()
                for gi in range(2):
                    gl = q2_ * 2 + gi
                    g = G0 + gl
                    P.op("pe", lambda e, ps=ps, gi=gi, g=g: e.matmul(ps[:, gi * NJ:gi * NJ + 128], lhsT=u_tmA[:, g * 128:(g + 1) * 128], rhs=idm,
                                                                   start=True, stop=True), reads=["u_tmA", "ident_b", "anti_b"], writes=[pk], skip_same=True)
                    P.op("pe", lambda e, ps=ps, gi=gi, g=g: e.matmul(ps[:, gi * NJ + 128:(gi + 1) * NJ], lhsT=u_tmB[0:32, g * 128:(g + 1) * 128],
                                                                   rhs=idm32[0:32, 0:32], start=True, stop=True),
                         reads=["u_tmB", "ident_b", "anti32_b"], writes=[pk], skip_same=True)
                P.op("act", lambda e, ps=ps, q2_=q2_: e.activation(out=Ugc[:, q2_ * 2 * NJ:(q2_ + 1) * 2 * NJ], in_=ps[:, 0:2 * NJ], func=ACT.Identity),
                     reads=[pk], writes=["Ugc"])
            def Lv(nm):
                return S[nm].rearrange("p (g s) -> p g s", s=8)[:, G0:G0 + GC, :]
            Ba3 = S["Ba"].rearrange("p (g c) -> p g c", c=16)[:, G0:G0 + GC, :]
            Bb3 = S["Bb"].rearrange("p (g c) -> p g c", c=16)[:, G0:G0 + GC, :]
            Ca3 = ssmc[:, d, 0, :].rearrange("p (g c) -> p g c", c=16)[:, G0:G0 + GC, :]
            Cb3 = ssmc[:, d, 1, :].rearrange("p (g c) -> p g c", c=16)[:, G0:G0 + GC, :]

            def gen(out, X, L1, Y, L2, wkey):
                Xb = X.unsqueeze(2).broadcast_to([128, GC, 8, 16]); Yb = Y.unsqueeze(2).broadcast_to([128, GC, 8, 16])
                l1 = L1.unsqueeze(3).broadcast_to([128, GC, 8, 16]); l2 = L2.unsqueeze(3).broadcast_to([128, GC, 8, 16])
                P.op("dve", lambda e: e.tensor_tensor(out=v4(g1a), in0=Xb, in1=l1, op=ALU.mult), reads=[kk, "ssmc"], writes=["g1a"])
                P.op("dve", lambda e: e.tensor_tensor(out=v4(g1b), in0=Yb, in1=l2, op=ALU.mult), reads=[kk, "ssmc"], writes=["g1b"])
                P.op("dve", lambda e: e.tensor_tensor(out=out, in0=g1a, in1=g1b, op=ALU.subtract), reads=["g1a", "g1b"], writes=[wkey])

            def swap(src_, dst_, scl, rkey, wkey):
                for hf_ in range(GC * 128 // 512):
                    ps, pk = nps()
                    P.op("pe", lambda e, ps=ps, hf_=hf_: e.matmul(ps, lhsT=psw_b, rhs=src_[:, hf_ * 512:(hf_ + 1) * 512], start=True, stop=True),
                         reads=[rkey, "psw_b"], writes=[pk], skip_same=True)
                    P.op("act", lambda e, ps=ps, hf_=hf_: e.activation(out=dst_[:, hf_ * 512:(hf_ + 1) * 512], in_=ps, func=ACT.Identity, scale=scl),
                         reads=[pk], writes=[wkey])

            def m_build():
                for q4 in range(GC // 4):
                    ps, pk = nps()
                    for gi in range(4):
                        gl = q4 * 4 + gi
                        P.op("pe", lambda e, ps=ps, gi=gi, gl=gl: e.matmul(ps[:, gi * 128:(gi + 1) * 128], lhsT=v3g(PBst)[:, gl, :], rhs=v3g(PCst)[:, gl, :],
                                                                       start=True, stop=True), reads=["PB", "PC"], writes=[pk], skip_same=True)
                    mk = mmask.rearrange("p (d n) -> p d n", d=2)[:, d, :].unsqueeze(1).broadcast_to([128, 4, 128])
                    if d == 1:
                        P.op("dve", lambda e, ps=ps, q4=q4, mk=mk: e.tensor_tensor(out=v3g(Mm)[:, q4 * 4:(q4 + 1) * 4, :],
                                                                                 in0=ps.rearrange("p (g n) -> p g n", g=4), in1=mk, op=ALU.mult),
                             reads=[pk, "mmask"], writes=["Mm"])
                    else:
                        P.op("dve", lambda e, ps=ps, mk=mk: e.tensor_tensor(out=g1a[:, 0:512].rearrange("p (g n) -> p g n", g=4),
                                                                          in0=ps.rearrange("p (g n) -> p g n", g=4), in1=mk, op=ALU.mult),
                             reads=[pk, "mmask"], writes=["g1a"])
                        for gi in range(4):
                            gl = q4 * 4 + gi
                            P.op("dve", lambda e, gi=gi, gl=gl: e.scalar_tensor_tensor(out=v3g(Mm)[:, gl, :], in0=ident_f, scalar=dvec[:, G0 + gl:G0 + gl + 1],
                                                                                     in1=g1a[:, gi * 128:(gi + 1) * 128], op0=ALU.mult, op1=ALU.add),
                                 reads=["g1a", "ident", "dvec"], writes=["Mm"])

            gen(PBst, Ba3, Lv("L7re"), Bb3, Lv("L7imS"), "PB")
            swap(PBst, PBsw, 1.0, "PB", "PB2")
            for (src, dst, dk, rk_) in ((PBst, Wst, "Wst", "PB"), (PBsw, Wsw, "Wsw", "PB2")):
                for q4 in range(GC // 4):
                    ps, pk = nps()
                    for gi in range(4):
                        gl = q4 * 4 + gi
                        P.op("pe", lambda e, ps=ps, gi=gi, gl=gl, src=src: e.matmul(ps[:, gi * 128:(gi + 1) * 128], lhsT=v3g(src)[:, gl, :], rhs=ident_b,
                                                                                start=True, stop=True), reads=[rk_, "ident_b"], writes=[pk], skip_same=True)
                    P.op("act", lambda e, ps=ps, q4=q4, dst=dst: e.activation(out=dst[:, q4 * 512:(q4 + 1) * 512], in_=ps, func=ACT.Identity),
                         reads=[pk], writes=[dk])
            gen(PCst, Ca3, Lv("LCreS"), Cb3, Lv("LCim"), "PC")
            par = (gc * 2 + d) % 2
            cosT = cosTs[par]; sinT = sinTs[par]; rhoM = rhoMs[par]
            tabk = "tab%d" % par
            cT = vj(cosT); sT = vj(sinT)
            for q2_ in range(GC // 2):
                psS, kS = nps()
                psW, kW = nps()
                for gi in range(2):
                    gl = q2_ * 2 + gi
                    P.op("pe", lambda e, psS=psS, gi=gi, gl=gl: e.matmul(psS[:, gi * NJ:(gi + 1) * NJ], lhsT=v3g(Wst)[:, gl, :], rhs=vj(Ugc)[:, gl, :],
                                                                       start=True, stop=True), reads=["Wst", "Ugc"], writes=[kS], skip_same=True)
                    P.op("pe", lambda e, psW=psW, gi=gi, gl=gl: e.matmul(psW[:, gi * NJ:(gi + 1) * NJ], lhsT=v3g(Wsw)[:, gl, :], rhs=vj(Ugc)[:, gl, :],
                                                                       start=True, stop=True), reads=["Wsw", "Ugc"], writes=[kW], skip_same=True)
                sl = slice(q2_ * 2 * NJ, (q2_ + 1) * 2 * NJ)
                P.op("dve", lambda e, psS=psS, sl=sl: e.tensor_tensor(out=Wd[:, sl], in0=psS[:, 0:2 * NJ], in1=cosT[:, sl], op=ALU.mult),
                     reads=[kS, tabk], writes=["Wd"])
                P.op("dve", lambda e, psW=psW, sl=sl: e.tensor_tensor(out=Tt[:, sl], in0=psW[:, 0:2 * NJ], in1=sinT[:, sl], op=ALU.mult),
                     reads=[kW, tabk], writes=["g1a"])
            P.op("dve", lambda e: e.tensor_tensor(out=Wd, in0=Wd, in1=Tt, op=ALU.subtract), reads=["Wd", "g1a"], writes=["Wd"])
            h0 = ssmp[:, d, 3, :]
            for gl in range(GC):
                P.op("dve", lambda e, gl=gl: e.tensor_tensor_scan(out=vj(Vd)[:, gl, :], data0=vj(rhoM)[:, gl, :], data1=vj(Wd)[:, gl, :],
                                                                initial=h0[:, G0 + gl:G0 + gl + 1], op0=ALU.mult, op1=ALU.add),
                     reads=[tabk, "Wd", "ssmp"], writes=["Vd"])
            swap(PCst, PCsw, -1.0, "PC", "PC2")
            m_build()
            vsel3 = vsel.rearrange("p (g q) -> p g q", q=5)
            SEL = slice(31, NJ, 32)
            P.op("pool", lambda e: e.tensor_copy(out=vsel3, in_=vj(Vd)[:, :, SEL]), reads=["Vd"], writes=["vsel"])
            psF, kF = nps()
            P.op("pe", lambda e: e.matmul(psF[:, 0:GC * 5], lhsT=psw_f, rhs=vsel, start=True, stop=True), reads=["psw", "vsel"], writes=[kF], skip_same=True)
            hf = hfin4[:, d, G0:G0 + GC, :]
            tsel = vj(Tt)[:, :, 0:5]
            P.op("dve", lambda e: e.tensor_tensor(out=tsel, in0=psF[:, 0:GC * 5].rearrange("p (g q) -> p g q", q=5), in1=sT[:, :, SEL], op=ALU.mult),
                 reads=[kF, tabk], writes=["g1a"])
            P.op("dve", lambda e: e.tensor_tensor(out=hf, in0=vj(Vd)[:, :, SEL], in1=cT[:, :, SEL], op=ALU.mult), reads=["Vd", tabk], writes=["hfin"])
            P.op("dve", lambda e: e.tensor_tensor(out=hf, in0=hf, in1=tsel, op=ALU.add), reads=["g1a", "hfin"], writes=["hfin"])
            P.op("dve", lambda e: e.tensor_tensor(out=Tt, in0=Vd, in1=Wd, op=ALU.subtract), reads=["Vd", "Wd", "hfin"], writes=["g1a"])
            P.op("dve", lambda e: e.tensor_tensor(out=Ah, in0=Tt, in1=cosT, op=ALU.mult), reads=["g1a", tabk], writes=["g1b"])
            P.op("dve", lambda e: e.tensor_tensor(out=Bh, in0=Tt, in1=sinT, op=ALU.mult), reads=["g1a", tabk], writes=["g1b"])
            yAv = yA[d].rearrange("p (i g c) -> p i g c", i=8, g=GC)
            yBv = yB[d].rearrange("p (i g c) -> p i g c", i=8, g=GC)
            for q4 in range(GC // 4):
                for (c0, c1, npart, yv, yk) in ((0, 128, 128, yAv, "yA%d" % d), (128, NJ, 32, yBv, "yB%d" % d)):
                    ps, pk = nps()
                    for gi in range(4):
                        gl = q4 * 4 + gi
                        o = ps[0:npart, gi * 128:(gi + 1) * 128]
                        P.op("pe", lambda e, o=o, gl=gl, c0=c0, c1=c1: e.matmul(o, lhsT=vj(Ugc)[:, gl, c0:c1], rhs=v3g(Mm)[:, gl, :], start=True, stop=False),
                             reads=["Ugc", "Mm"], writes=[pk], skip_same=True)
                        P.op("pe", lambda e, o=o, gl=gl, c0=c0, c1=c1: e.matmul(o, lhsT=vj(Ah)[:, gl, c0:c1], rhs=v3g(PCst)[:, gl, :], start=False, stop=False),
                             reads=["g1b", "PC", "PC2"], writes=[pk], skip_same=True)
                        P.op("pe", lambda e, o=o, gl=gl, c0=c0, c1=c1: e.matmul(o, lhsT=vj(Bh)[:, gl, c0:c1], rhs=v3g(PCsw)[:, gl, :], start=False, stop=True),
                             reads=["g1b", "PC", "PC2"], writes=[pk], skip_same=True)
                    P.op("act", lambda e, ps=ps, npart=npart, yv=yv, q4=q4: e.activation(
                        out=yv[0:npart, :, q4 * 4:(q4 + 1) * 4, :].rearrange("p i g c -> p g i c"),
                        in_=ps[0:npart, :].rearrange("p (g i c) -> p g i c", g=4, i=8), func=ACT.Identity), reads=[pk], writes=[yk])

        def ssm_post(gc):
            for cc in range(GC // 8):
                zs = zst[cc % 2]
                zk = "zst"
                zA = zs[:, 0:1024].rearrange("p (j i) -> p j i", i=8)
                zB = zs[:, 1024:1280].rearrange("p (j i) -> p j i", i=8)
                for i in range(8):
                    ps, pk = nps()
                    for d in range(2):
                        yAv = yA[d].rearrange("p (i n) -> p i n", i=8)
                        yBv = yB[d].rearrange("p (i n) -> p i n", i=8)
                        P.op("pe", lambda e, ps=ps, d=d, yAv=yAv, i=i, cc=cc: e.matmul(ps[:, 0:128], lhsT=yAv[:, i, cc * 128:(cc + 1) * 128],
                                                                                     rhs=(ident_b if d == 0 else anti_b), start=(d == 0), stop=(d == 1)),
                             reads=["yA%d" % d, "ident_b", "anti_b"], writes=[pk], skip_same=True)
                    for d in range(2):
                        yBv = yB[d].rearrange("p (i n) -> p i n", i=8)
                        P.op("pe", lambda e, ps=ps, d=d, yBv=yBv, i=i, cc=cc: e.matmul(ps[:, 128:160], lhsT=yBv[0:32, i, cc * 128:(cc + 1) * 128],
                                                                                     rhs=(ident_b if d == 0 else anti32_b)[0:32, 0:32], start=(d == 0), stop=(d == 1)),
                             reads=["yB%d" % d, "ident_b", "anti32_b"], writes=[pk], skip_same=True)
                    P.op("act", lambda e, ps=ps, zA=zA, i=i: e.activation(out=zA[:, :, i], in_=ps[:, 0:128], func=ACT.Gelu), reads=[pk], writes=[zk])
                    P.op("act", lambda e, ps=ps, zB=zB, i=i: e.activation(out=zB[:, :, i], in_=ps[:, 128:160], func=ACT.Gelu), reads=[pk], writes=[zk])
                zi = gc * (GC // 8) + cc
                row = zi * 128
                P.dma("sp", lambda e, zs=zs, row=row: e.dma_start(out=z_T[row:row + 128, :], in_=zs), reads=[zk], writes=[("z_T", zi)])

        rounds = [(gc, d) for gc in range(64 // GC) for d in range(2)]
        ssm_tables(*rounds[0], "lo")
        ssm_tables(*rounds[0], "hi")
        for ri, (gc, d) in enumerate(rounds):
            if ri + 1 < len(rounds):
                ssm_tables(*rounds[ri + 1], "lo")
            ssm_round(gc, d)
            if ri + 1 < len(rounds):
                ssm_tables(*rounds[ri + 1], "hi")
            if d == 1:
                ssm_post(gc)
        P.dma("sp", lambda e: e.dma_start(out=dout["hfin"], in_=hfin4), reads=["hfin"], is_out=True)
        off[0] = base_mark
        P.barrier()
        if stop == 4:
            P.emit()
            return nc

        NST = 6
        stage = [alloc(512) for _ in range(NST)]
        srr[0] = 0
        aux = [alloc(512) for _ in range(NST)]
        arr = [0]

        def nstage():
            i = srr[0]
            srr[0] = (i + 1) % NST
            return stage[i], "stage%d" % i

        def naux():
            i = arr[0]
            arr[0] = (i + 1) % NST
            return aux[i], "aux%d" % i

        def load_src(dram_T, Kc, key, deps):
            t = alloc(Kc * NTOK, BF16)
            t3 = t.rearrange("p (k t) -> p k t", k=Kc)
            for k in range(Kc):
                P.dma("pool", lambda e, k=k: e.dma_start(out=t3[:, k, :], in_=dram_T[k * 128:(k + 1) * 128, :]), reads=deps, writes=[key])
            return t3

        zsrc = load_src(z_T, 8, "zsrc", [("z_T", i) for i in range(8)])
        osrc = load_src(o_T, 8, "osrc", [("o_T", h) for h in range(8)])
        z2b = alloc(8 * NTOK, BF16)
        z2src = z2b.rearrange("p (k t) -> p k t", k=8)
        msb = alloc(16 * NTOK, BF16)
        msrc = msb.rearrange("p (k t) -> p k t", k=16)

        def epi_glu(m, t0, n, ps, pkey):
            sg, sk = nstage()
            ax, ak = naux()
            P.dma("sp", lambda e: e.dma_start(out=ax[:, :n], in_=z_T[m * 128:(m + 1) * 128, t0:t0 + n]), reads=[("z_T", m)], writes=[ak])
            P.op("act", lambda e: e.activation(out=sg[:, :n], in_=ps[:, :n], func=ACT.Sigmoid), reads=[pkey], writes=[sk])
            P.op("dve", lambda e: e.tensor_tensor(out=z2src[:, m, t0:t0 + n], in0=sg[:, :n], in1=ax[:, :n], op=ALU.mult), reads=[sk, ak], writes=["z2src"])

        linear_fm(zsrc, "zsrc", "wglu", 8, 512, epi_glu)
        for b in range(4):
            wS, wSk = wload(din["wso"][b].rearrange("p k n -> p (k n)"), 4096)
            wA, wAk = wload(din["wao"][b].rearrange("p k n -> p (k n)"), 4096)
            wSv = wS[:, :4096].rearrange("p (k n) -> p k n", k=8)
            wAv = wA[:, :4096].rearrange("p (k n) -> p k n", k=8)

            def merge_tile(m, mi, t0, n, wSv=wSv, wAv=wAv, wSk=wSk, wAk=wAk):
                ax1, ak1 = naux()
                ax2, ak2 = naux()
                sg1, sk1 = nstage()
                sg2, sk2 = nstage()
                P.dma("sp", lambda e: e.dma_start(out=ax1[:, :n], in_=sg_T[m * 128:(m + 1) * 128, t0:t0 + n]), reads=[("sgs", m, t0)], writes=[ak1])
                P.dma("sp", lambda e: e.dma_start(out=ax2[:, :n], in_=sg_T[2048 + m * 128:2048 + (m + 1) * 128, t0:t0 + n]), reads=[("sga", m, t0)], writes=[ak2])
                psA, kA = nps()
                psB, kB = nps()
                for k in range(8):
                    P.op("pe", lambda e, k=k: e.matmul(psA[:, :n], lhsT=wSv[:, k, mi * 128:(mi + 1) * 128], rhs=z2src[:, k, t0:t0 + n],
                                                       start=(k == 0), stop=(k == 7)), reads=[wSk, "z2src"], writes=[kA], skip_same=True)
                for k in range(8):
                    P.op("pe", lambda e, k=k: e.matmul(psB[:, :n], lhsT=wAv[:, k, mi * 128:(mi + 1) * 128], rhs=osrc[:, k, t0:t0 + n],
                                                       start=(k == 0), stop=(k == 7)), reads=[wAk, "osrc"], writes=[kB], skip_same=True)
                P.op("dve", lambda e: e.tensor_tensor(out=sg1[:, :n], in0=psA[:, :n], in1=ax1[:, :n], op=ALU.mult), reads=[kA, ak1], writes=[sk1])
                P.op("dve", lambda e: e.tensor_tensor(out=sg2[:, :n], in0=psB[:, :n], in1=ax2[:, :n], op=ALU.mult), reads=[kB, ak2], writes=[sk2])
                P.op("pool", lambda e: e.tensor_tensor(out=msrc[:, m, t0:t0 + n], in0=sg1[:, :n], in1=sg2[:, :n], op=ALU.add), reads=[sk1, sk2], writes=["msrc"])

            for mi in range(4):
                for (t0, n) in TB:
                    merge_tile(b * 4 + mi, mi, t0, n)

        def epi_res(src_in, gate, dst_T, tagk, gkey):
            def epi(m, t0, n, ps, pkey):
                ci = 0 if t0 < 1024 else 1
                sg, sk = nstage()
                ax, ak = naux()
                P.dma("sp", lambda e: e.dma_start(out=ax[:, :n], in_=src_in(m, t0, n)), reads=[("x1", m, t0)], writes=[ak])
                P.op("dve", lambda e: e.scalar_tensor_tensor(out=sg[:, :n], in0=ps[:, :n], scalar=gate[:, m, ci:ci + 1], in1=ax[:, :n],
                                                            op0=ALU.mult, op1=ALU.add), reads=[pkey, ak, gkey], writes=[sk])
                P.dma("sp", lambda e: e.dma_start(out=dst_T[m * 128:(m + 1) * 128, t0:t0 + n], in_=sg[:, :n]), reads=[sk], writes=[(tagk, m, t0)])
            return epi

        linear_fm(msrc, "msrc", "wout", 16, 256, epi_res(lambda m, t0, n: din["xT"][:, m, t0:t0 + n], g1, x1_T, "x1", "modvB"))
        off[0] = base_mark
        P.barrier()
        if stop == 6:
            P.emit()
            return nc

        NRES = 28
        act_res = alloc(NRES * NTOK, BF16)
        act_res3 = act_res.rearrange("p (k t) -> p k t", k=NRES)
        m_res = off[0]
        xm2 = alloc(16 * NTOK, BF16)
        xm2v = xm2.rearrange("p (k t) -> p k t", k=16)
        allx1 = [("x1", m, t0) for m in range(16) for (t0, n) in TB]
        x1v = x1_T.rearrange("(k p) t -> p k t", p=128)

        def load_x1(dst, t0, n, key):
            P.dma("sp", lambda e: e.dma_start(out=dst, in_=x1v[:, :, t0:t0 + n]), reads=allx1, writes=[key])

        mk2 = off[0]
        rms_phase("n2", load_x1, lambda k, ci: a2v[:, k, ci:ci + 1], lambda k, ci: sh2[:, k, ci:ci + 1],
                  lambda xs3, k, t0, n: xm2v[:, k, t0:t0 + n], ["a2", "modvB"], "xm2")
        off[0] = mk2
        P.barrier()
        if stop == 7:
            P.emit()
            return nc

        convw = load_const("convw").rearrange("p (w c) -> p w c", w=3)
        convb = load_const("convb")
        bmL = load_const("bmL")
        bmR = load_const("bmR")
        accA = alloc(NTOK); accB = alloc(NTOK)
        tlA = alloc(NTOK); trA = alloc(NTOK); tlB = alloc(NTOK); trB = alloc(NTOK)
        actst = [alloc(NTOK, BF16), alloc(NTOK, BF16)]
        for b in range(22):
            wa, wak = wload(din["wupa"][b].rearrange("p k n -> p (k n)"), 4096)
            wbb, wbk = wload(din["wupb"][b].rearrange("p k n -> p (k n)"), 4096)
            wva = wa[:, :4096].rearrange("p (k n) -> p k n", k=16)
            wvb = wbb[:, :4096].rearrange("p (k n) -> p k n", k=16)
            for mi in range(2):
                m = b * 2 + mi
                for (wv, wk, base, acc, tl, tr, cm, tg) in ((wva, wak, 0, accA, tlA, trA, m, "A"), (wvb, wbk, 2048, accB, tlB, trB, m + 44, "B")):
                    keys = []
                    for bi, (t0, n) in enumerate(TB):
                        bkidx = base // 512 + bi
                        pkey = "ps%d" % bkidx
                        keys.append(pkey)
                        for k in range(16):
                            P.op("pe", lambda e, wv=wv, k=k, mi=mi, t0=t0, n=n, base=base: e.matmul(
                                psbig[:, base + t0:base + t0 + n], lhsT=wv[:, k, mi * 128:(mi + 1) * 128], rhs=xm2v[:, k, t0:t0 + n],
                                start=(k == 0), stop=(k == 15)), reads=[wk, "xm2"], writes=[pkey], skip_same=True)
                    hp = psbig[:, base:base + NTOK]
                    P.op("act", lambda e, hp=hp, acc=acc, cm=cm: e.activation(out=acc, in_=hp, func=ACT.Identity, scale=convw[:, 1, cm:cm + 1], bias=convb[:, cm:cm + 1]),
                         reads=keys + ["convw", "convb"], writes=["acc" + tg])
                    P.op("dve", lambda e, hp=hp, tl=tl, cm=cm: e.scalar_tensor_tensor(out=tl[:, 1:NTOK], in0=hp[:, 0:NTOK - 1], scalar=convw[:, 0, cm:cm + 1],
                                                                                   in1=bmL[:, 1:NTOK], op0=ALU.mult, op1=ALU.mult),
                         reads=keys + ["convw", "bmL"], writes=["tl" + tg])
                    P.op("dve", lambda e, hp=hp, tr=tr, cm=cm: e.scalar_tensor_tensor(out=tr[:, 0:NTOK - 1], in0=hp[:, 1:NTOK], scalar=convw[:, 2, cm:cm + 1],
                                                                                   in1=bmR[:, 0:NTOK - 1], op0=ALU.mult, op1=ALU.mult),
                         reads=keys + ["convw", "bmR"], writes=["tr" + tg])
                    P.op("dve", lambda e, acc=acc, tl=tl: e.tensor_tensor(out=acc[:, 1:NTOK], in0=acc[:, 1:NTOK], in1=tl[:, 1:NTOK], op=ALU.add),
                         reads=["acc" + tg, "tl" + tg], writes=["acc" + tg])
                    P.op("dve", lambda e, acc=acc, tr=tr: e.tensor_tensor(out=acc[:, 0:NTOK - 1], in0=acc[:, 0:NTOK - 1], in1=tr[:, 0:NTOK - 1], op=ALU.add),
                         reads=["acc" + tg, "tr" + tg], writes=["acc" + tg])
                ast = actst[m % 2]
                akk = "actst%d" % (m % 2)
                P.op("act", lambda e: e.activation(out=accA, in_=accA, func=ACT.Silu), reads=["accA"], writes=["accA"])
                if m < NRES:
                    P.op("dve", lambda e, m=m: e.tensor_tensor(out=act_res3[:, m, :], in0=accA, in1=accB, op=ALU.mult), reads=["accA", "accB"], writes=[("actres", m)])
                else:
                    P.op("dve", lambda e, ast=ast: e.tensor_tensor(out=ast, in0=accA, in1=accB, op=ALU.mult), reads=["accA", "accB"], writes=[akk])
                    P.dma("sp", lambda e, ast=ast, m=m: e.dma_start(out=act_T[m * 128:(m + 1) * 128, :], in_=ast), reads=[akk], writes=[("act", m)])
        off[0] = m_res
        P.barrier()
        if stop == 8:
            P.emit()
            return nc

        stage[:] = [alloc(512) for _ in range(NST)]
        srr[0] = 0
        aux[:] = [alloc(512) for _ in range(NST)]
        arr[0] = 0
        NTAIL = 44 - NRES
        asrc = alloc(NTAIL * NTOK, BF16)
        asrc3 = asrc.rearrange("p (k t) -> p k t", k=NTAIL)
        for k in range(NTAIL):
            P.dma("sp", lambda e, k=k: e.dma_start(out=asrc3[:, k, :], in_=act_T[(NRES + k) * 128:(NRES + k + 1) * 128, :]),
                  reads=[("act", NRES + k)], writes=[("asrc", k)])
        P.join("asrc", [("actres", m) for m in range(NRES)] + [("asrc", k) for k in range(NTAIL)])
        linear_fm(None, "asrc", "wdown", 44, 128, epi_res(lambda m, t0, n: x1_T[m * 128:(m + 1) * 128, t0:t0 + n], g2, x2_T, "x2", "modvB"),
                  src_k=lambda k: act_res3[:, k, :] if k < NRES else asrc3[:, k - NRES, :])
        off[0] = base_mark
        P.barrier()
        if stop == 9:
            P.emit()
            return nc

        x2v = x2_T.rearrange("(k p) t -> p k t", p=128)
        allx2 = [("x2", m, t0) for m in range(16) for (t0, n) in TB]
        gfin = gv[:, 2, :]

        def load_x2(dst, t0, n, key):
            P.dma("sp", lambda e: e.dma_start(out=dst, in_=x2v[:, :, t0:t0 + n]), reads=allx2, writes=[key])

        def store_y(xs3, t0, n, key):
            P.dma("sp", lambda e: e.dma_start(out=dout["yT"][:, :, t0:t0 + n], in_=xs3), reads=key, is_out=True)

        rms_phase("nf", load_x2, lambda k, ci: gfin[:, k:k + 1], None, lambda xs3, k, t0, n: xs3[:, k, :], ["gvec"], None, store_fn=store_y, NBK=512)
        P.emit()
    return nc


def _blk(W, NB):
    K, N = W.shape
    return np.ascontiguousarray(W.reshape(K // 128, 128, N // NB, NB).transpose(2, 1, 0, 3))


def _fm(v):
    return np.ascontiguousarray(v.reshape(-1, 128).T)


_NC_CACHE = {}
_DEBUG_HOOK = None


def kernel(x_prompt, x_sample, cache_k, cache_v, state_ssm_re, state_ssm_im, c, c_ctx,
           norm_mix_g, norm_ffn_g, w_mod, b_mod, w_in, ssm_lambda_re, ssm_lambda_im,
           ssm_log_dt, ssm_b_re, ssm_b_im, ssm_c_re, ssm_c_im, ssm_d, w_glu, attn_sink,
           w_ssm_o, w_attn_o, w_out, w_up, conv_w, conv_b, w_down, final_norm_g):
    f = lambda a: np.asarray(a, dtype=np.float32)
    x_prompt, x_sample = f(x_prompt), f(x_sample)
    w_in0 = f(w_in)[0]
    shared = {}
    shared["bmod"] = _fm(f(b_mod)[0])
    shared["gvec"] = np.concatenate([_fm(f(norm_mix_g)[0]), _fm(f(norm_ffn_g)[0]), _fm(f(final_norm_g))], axis=1)
    shared["wmod"] = _blk(f(w_mod)[0], 256)
    shared["wu"] = _blk(w_in0[:, 0:1024], 256)
    shared["wqk"] = _blk(w_in0[:, 1024:2304], 256)
    shared["wv"] = _blk(w_in0[:, 2304:2560], 256)
    shared["wgs"] = _blk(w_in0[:, 2560:4608], 256)
    shared["wga"] = _blk(w_in0[:, 4608:6656], 256)
    shared["wglu"] = _blk(f(w_glu)[0], 512)
    shared["wso"] = _blk(f(w_ssm_o)[0], 512)
    shared["wao"] = _blk(f(w_attn_o)[0], 512)
    shared["wout"] = _blk(f(w_out)[0], 256)
    wup = f(w_up)[0]
    shared["wupa"] = _blk(wup[:, :DFF], 256)
    shared["wupb"] = _blk(wup[:, DFF:], 256)
    shared["wdown"] = _blk(f(w_down)[0], 128)
    eye = np.eye(128, dtype=np.float32)
    shared["ident"] = eye
    shared["anti"] = np.ascontiguousarray(eye[::-1])
    a32 = np.zeros((128, 128), np.float32)
    a32[:32, :32] = np.eye(32, dtype=np.float32)[::-1]
    shared["anti32"] = a32
    psw = np.zeros((128, 128), np.float32)
    for m in range(64):
        psw[m + 64, m] = -1.0
        psw[m, m + 64] = 1.0
    shared["psw"] = psw
    sidx = np.arange(128) // 16
    shared["mmask"] = np.stack([(sidx[None, :] >= sidx[:, None]), (sidx[:, None] >= sidx[None, :])], axis=1).astype(np.float32)
    dup = lambda a: np.concatenate([a.T, a.T], axis=0)
    ssmp = np.zeros((8, 128, 2, 4, 64), np.float32)
    ssmb = np.zeros((128, 2, 2, 1024), np.float32)
    ssmc = np.zeros((128, 2, 2, 1024), np.float32)
    lre, lim, ldt = f(ssm_lambda_re)[0], f(ssm_lambda_im)[0], f(ssm_log_dt)[0]
    bre, bim, cre, cim = f(ssm_b_re)[0], f(ssm_b_im)[0], f(ssm_c_re)[0], f(ssm_c_im)[0]
    sre, sim = f(state_ssm_re), f(state_ssm_im)
    for d in range(2):
        ssmp[:, :, d, 0, :] = dup(lre[d])
        ssmp[:, :, d, 1, :] = dup(lim[d])
        ssmp[:, :, d, 2, :] = np.broadcast_to(ldt[d][None, :], (128, 64))
        br = bre[d].transpose(1, 0, 2).reshape(64, 1024)
        bi = bim[d].transpose(1, 0, 2).reshape(64, 1024)
        ssmb[:, d, 0] = np.concatenate([br, bi], 0)
        ssmb[:, d, 1] = np.concatenate([bi, br], 0)
        cr = cre[d].transpose(2, 0, 1).reshape(64, 1024)
        ci_ = cim[d].transpose(2, 0, 1).reshape(64, 1024)
        ssmc[:, d, 0] = np.concatenate([cr, ci_], 0)
        ssmc[:, d, 1] = np.concatenate([ci_, cr], 0)
        for b in range(2):
            ssmp[6 + b, :, d, 3, :] = np.concatenate([sre[b, 0, d].T, sim[b, 0, d].T], 0)
    shared["ssmb"] = ssmb
    shared["ssmc"] = ssmc
    sg = np.ones((128, 2), np.float32)
    sg[64:, 0] = -1.0
    sg[:64, 1] = -1.0
    shared["sgn"] = sg
    shared["dvec"] = np.ascontiguousarray(np.tile(f(ssm_d)[0].reshape(64, 16).T, (8, 1)))
    shared["sink"] = np.ascontiguousarray(np.broadcast_to(f(attn_sink)[0][None, :], (128, 8)))
    cw = f(conv_w)[0]
    shared["convw"] = np.ascontiguousarray(np.stack([_fm(cw[i]) for i in range(3)], axis=1))
    shared["convb"] = _fm(f(conv_b)[0])

    T = 1024
    row = np.repeat(np.arange(T // 64, dtype=np.float32), 64)
    col = np.tile(np.arange(64, dtype=np.float32), T // 64)
    inv = (10000.0 ** (-np.arange(32, dtype=np.float32) / 32)).astype(np.float32)
    ang = np.concatenate([row[None, :] * inv[:, None], col[None, :] * inv[:, None]], axis=0).astype(np.float32)
    cosA, sinA = np.cos(ang), np.sin(ang)
    ropeC_s = np.concatenate([cosA, cosA], 0)
    ropeS_s = np.concatenate([-sinA, sinA], 0)
    perm = np.array([(p % 64 // 32) * 64 + (p // 64) * 32 + p % 32 for p in range(128)])

    qi = np.arange(128)[:, None]
    kj = np.arange(384)[None, :] - 128
    NEG = np.float32(-1e30)
    in_maps = []
    c_f, cctx = f(c), f(c_ctx)
    ck_all, cv_all = f(cache_k), f(cache_v)
    for core in range(8):
        m = dict(shared)
        if core < 6:
            xa = x_prompt[5 * core:5 * core + 4].reshape(1024, D)
            xb = x_prompt[5 * core + 4]
            condA = cctx
        else:
            xa = x_sample[core - 6]
            xb = x_prompt[30 + core - 6]
            condA = c_f[core - 6]
        x = np.concatenate([xa, xb], 0)
        m["xT"] = np.ascontiguousarray(x.reshape(NTOK, 16, 128).transpose(2, 1, 0))
        m["cond"] = np.ascontiguousarray(np.stack([_fm(condA), _fm(cctx)], axis=2).reshape(128, 32))
        m["ssmp"] = ssmp[core]
        bm = np.ones((128, NJ), np.float32)
        bm[:, 128] = 0
        maskL = np.full((128, 10, 384), NEG, np.float32)
        cmask = np.full((128, 10), NEG, np.float32)
        bmL = np.ones((128, NTOK), np.float32)
        bmR = np.ones((128, NTOK), np.float32)
        rC = np.ones((128, NTOK), np.float32)
        rS = np.zeros((128, NTOK), np.float32)
        if core < 6:
            bm[:, [32, 64, 96]] = 0
            starts = [0, 256, 512, 768, 1024]
            for qt in range(10):
                if qt % 2 == 0:
                    maskL[:, qt, 128:384] = 0
                else:
                    maskL[:, qt, 0:256] = 0
            ck = np.zeros((128, 2, 256), np.float32)
            cv = np.zeros((128, 2, 256), np.float32)
        else:
            starts = [0, 1024]
            band = np.where(np.abs(qi - kj) <= 128, np.float32(0), NEG).astype(np.float32)
            for qt in range(8):
                mk = band.copy()
                if qt == 0:
                    mk[:, 0:128] = NEG
                if qt == 7:
                    mk[:, 256:384] = NEG
                maskL[:, qt] = mk
                cmask[:, qt] = 0
            maskL[:, 8, 128:384] = 0
            maskL[:, 9, 0:256] = 0
            rC[:, :1024] = ropeC_s
            rS[:, :1024] = ropeS_s
            b = core - 6
            ck = np.ascontiguousarray(ck_all[b, 0].transpose(2, 1, 0)[perm])
            cv = np.ascontiguousarray(cv_all[b, 0].reshape(2, 128, 256).transpose(1, 0, 2))
        for s0 in starts:
            bmL[:, s0] = 0
        for e0 in starts[1:] + [NTOK]:
            bmR[:, e0 - 1] = 0
        m.update(bm=bm, maskL=maskL, cmask=cmask, bmL=bmL, bmR=bmR, ropeC=rC, ropeS=rS, ckT=ck, cv=cv)
        in_maps.append(m)

    if _DEBUG_HOOK is not None:
        return _DEBUG_HOOK(in_maps)
    if "nc" not in _NC_CACHE:
        _NC_CACHE["nc"] = build()
    nc = _NC_CACHE["nc"]
    res = run_bass_kernel_spmd(nc, in_maps, core_ids=list(range(8)))
    R = res.results

    y_prompt = np.zeros((32, 256, D), np.float32)
    y_sample = np.zeros((2, 1024, D), np.float32)
    nk = np.zeros((32, 1, 256, 2, 128), np.float32)
    nv = np.zeros((32, 1, 256, 2, 128), np.float32)
    hre = np.zeros((32, 1, 2, 64, 64), np.float32)
    him = np.zeros((32, 1, 2, 64, 64), np.float32)
    for core in range(8):
        y = R[core]["yT"].transpose(2, 1, 0).reshape(NTOK, D)
        k = R[core]["kTo"].T.reshape(NTOK, 2, 128)
        v = R[core]["vo"].reshape(NTOK, 2, 128)
        hf = R[core]["hfin"]
        if core < 6:
            seqs = [(5 * core + i, i * 256) for i in range(4)] + [(5 * core + 4, 1024)]
        else:
            y_sample[core - 6] = y[:1024]
            seqs = [(30 + core - 6, 1024)]
        for (sidx_, t0) in seqs:
            y_prompt[sidx_] = y[t0:t0 + 256]
            nk[sidx_, 0] = k[t0:t0 + 256]
            nv[sidx_, 0] = v[t0:t0 + 256]
            si = t0 // 256
            for d in range(2):
                if si == 4:
                    q = 4
                else:
                    q = si if d == 0 else 3 - si
                hre[sidx_, 0, d] = hf[0:64, d, :, q].T
                him[sidx_, 0, d] = hf[64:128, d, :, q].T
    return (y_prompt, y_sample, nk, nv, hre, him)
```

```python
import math
from contextlib import ExitStack
import numpy as np
import concourse.bass as bass
import concourse.mybir as mybir
from concourse.bass_utils import run_bass_kernel_spmd

F32 = mybir.dt.float32
BF16 = mybir.dt.bfloat16
ACT = mybir.ActivationFunctionType
ALU = mybir.AluOpType
AX = mybir.AxisListType
ENGS = ("pe", "act", "dve", "pool", "sp")

NTOK = 1280
TB = [(0, 512), (512, 512), (1024, 256)]
D = 2048
KC = 16
DFF = 5632
NG = 64
NJ = 160
SCALE = 128 ** -0.5
EPS = 1e-6
PI = math.pi


class Prog:
    def __init__(self, nc, stack, ndma=8):
        self.nc = nc
        self.ops = {e: [] for e in ENGS}
        self.count = {e: 0 for e in ENGS}
        self.waited = {e: {} for e in ENGS}
        self.last_w = {}
        self.readers = {}
        self.sems = {}
        self.pending = {e: [] for e in ENGS}
        for e in ENGS:
            if e != "sp":
                self.sems[e] = stack.enter_context(nc.semaphore("sem_" + e))
        self.dma_sems = {}
        self.dma_cnt = {}
        self.dma_rr = {}
        for e in ("sp", "act", "pool"):
            self.dma_sems[e] = [stack.enter_context(nc.semaphore("dsem_%s_%d" % (e, i))) for i in range(ndma)]
            self.dma_cnt[e] = [0] * ndma
            self.dma_rr[e] = 0
        self.out_deps = []

    def _add(self, eng, reads, writes, token, extra_deps=(), skip_same=False):
        deps = set(extra_deps)
        deps.update(self.pending[eng])
        self.pending[eng] = []
        for k in reads:
            if k in self.last_w:
                deps.update(self.last_w[k])
            if isinstance(k, str) and k.startswith("ps"):
                for d in self.readers.get(k, ()):
                    if d[0] != eng:
                        deps.add(d)
        for k in writes:
            if k in self.last_w:
                deps.update(self.last_w[k])
            for d in self.readers.get(k, ()):
                deps.add(d)
        best = {}
        for (sk, v) in deps:
            if skip_same and sk == eng:
                continue
            if v > best.get(sk, 0):
                best[sk] = v
        waits = []
        for sk, v in best.items():
            if self.waited[eng].get(sk, 0) < v:
                self.waited[eng][sk] = v
                waits.append((sk, v))
        for k in writes:
            self.last_w[k] = [token]
            self.readers[k] = []
        for k in reads:
            self.readers.setdefault(k, []).append(token)
        return waits

    def op(self, eng, fn, reads=(), writes=(), skip_same=False):
        self.count[eng] += 1
        token = (eng, self.count[eng])
        waits = self._add(eng, reads, writes, token, skip_same=skip_same)
        self.ops[eng].append((waits, fn, (eng, 1)))
        return token

    def dma(self, eng, fn, reads=(), writes=(), is_out=False):
        i = self.dma_rr[eng]
        self.dma_rr[eng] = (i + 1) % len(self.dma_sems[eng])
        prev = self.dma_cnt[eng][i]
        self.dma_cnt[eng][i] += 1
        sk = ("d", eng, i)
        token = (sk, 16 * (prev + 1))
        extra = [(sk, 16 * prev)] if prev > 0 else []
        waits = self._add(eng, reads, writes, token, extra_deps=extra)
        self.ops[eng].append((waits, fn, (sk, 16)))
        if is_out:
            self.out_deps.append(token)
        return token

    def join(self, newkey, keys):
        toks = []
        for k in keys:
            toks.extend(self.last_w.get(k, []))
        self.last_w[newkey] = toks
        self.readers[newkey] = []

    def barrier(self):
        toks = []
        for e in ENGS:
            if e != "sp" and self.count[e] > 0:
                toks.append((e, self.count[e]))
        for e in ("sp", "act", "pool"):
            for i, c in enumerate(self.dma_cnt[e]):
                if c > 0:
                    toks.append((("d", e, i), 16 * c))
        for e in ENGS:
            self.pending[e] = list(toks)

    def _sem(self, sk):
        if isinstance(sk, tuple):
            return self.dma_sems[sk[1]][sk[2]]
        return self.sems[sk]

    def emit(self):
        nc = self.nc
        best = {}
        for (sk, v) in self.out_deps:
            best[sk] = max(best.get(sk, 0), v)
        final_waits = list(best.items())
        with nc.Block() as block:
            def run(eng_name, eng):
                for waits, fn, (sk, inc) in self.ops[eng_name]:
                    for (wk, v) in waits:
                        eng.wait_ge(self._sem(wk), v)
                    ins = fn(eng)
                    ins.then_inc(self._sem(sk), inc)
                if eng_name == "sp":
                    for (wk, v) in final_waits:
                        eng.wait_ge(self._sem(wk), v)

            @block.sync
            def _(e):
                run("sp", e)

            @block.tensor
            def _(e):
                run("pe", e)

            @block.scalar
            def _(e):
                run("act", e)

            @block.vector
            def _(e):
                run("dve", e)

            @block.gpsimd
            def _(e):
                run("pool", e)


IN_SPECS = {
    "xT": (128, 16, NTOK), "cond": (128, 32), "bmod": (128, 96), "gvec": (128, 48),
    "wmod": (48, 128, 16, 256), "wqk": (5, 128, 16, 256), "wv": (1, 128, 16, 256), "wu": (4, 128, 16, 256),
    "wgs": (8, 128, 16, 256), "wga": (8, 128, 16, 256), "wglu": (2, 128, 8, 512), "wso": (4, 128, 8, 512),
    "wao": (4, 128, 8, 512), "wout": (8, 128, 16, 256), "wupa": (22, 128, 16, 256), "wupb": (22, 128, 16, 256),
    "wdown": (16, 128, 44, 128),
    "ident": (128, 128), "anti": (128, 128), "anti32": (128, 128), "psw": (128, 128), "mmask": (128, 2, 128),
    "ssmp": (128, 2, 4, 64),
    "ssmb": (128, 2, 2, 1024),
    "ssmc": (128, 2, 2, 1024),
    "sgn": (128, 2), "dvec": (128, 64), "bm": (128, NJ),
    "ropeC": (128, NTOK), "ropeS": (128, NTOK), "ckT": (128, 2, 256), "cv": (128, 2, 256),
    "maskL": (128, 10, 384), "cmask": (128, 10), "sink": (128, 8),
    "convw": (128, 3, 88), "convb": (128, 88), "bmL": (128, NTOK), "bmR": (128, NTOK),
}
OUT_SPECS = {"yT": (128, 16, NTOK), "kTo": (256, NTOK), "vo": (NTOK, 256), "hfin": (128, 2, 64, 5)}


def build(stop=None):
    nc = bass.Bass("TRN2", target_bir_lowering=False)
    din = {k: nc.dram_tensor(k, list(s), F32, kind="ExternalInput").ap() for k, s in IN_SPECS.items()}
    dout = {k: nc.dram_tensor(k, list(s), F32, kind="ExternalOutput").ap() for k, s in OUT_SPECS.items()}
    scr = lambda name, shape, dt=F32: nc.dram_tensor(name, list(shape), dt).ap()
    qk_T = scr("qk_T", [1280, NTOK])
    sg_T = scr("sg_T", [4096, NTOK])
    z_T = scr("z_T", [1024, NTOK])
    z2_T = scr("z2_T", [1024, NTOK])
    o_T = scr("o_T", [1024, NTOK], BF16)
    z_Tb = scr("z_Tb", [1024, NTOK], BF16)
    mrg_T = scr("mrg_T", [2048, NTOK])
    x1_T = scr("x1_T", [2048, NTOK])
    x2_T = scr("x2_T", [2048, NTOK])
    act_T = scr("act_T", [DFF, NTOK], BF16)
    sm_T = scr("sm_T", [2, 128, 6976])

    with ExitStack() as st:
        P = Prog(nc, st)
        ARENA = 53000
        arena = st.enter_context(nc.sbuf_tensor("arena", [128, ARENA], F32))
        psbig = st.enter_context(nc.psum_tensor("psbig", [128, 4096], F32))
        off = [0]

        def alloc(n, dt=F32):
            nb4 = n if dt == F32 else (n + 1) // 2
            a = arena[:, off[0]:off[0] + nb4]
            off[0] += nb4
            assert off[0] <= ARENA, off[0]
            if dt == F32:
                return a
            v = a.bitcast(dt)
            return v[:, :n] if 2 * nb4 != n else v

        psrr = [1]
        ps_lo = [1]

        def nps():
            i = psrr[0]
            psrr[0] = i + 1 if i + 1 < 8 else ps_lo[0]
            return psbig[:, i * 512:(i + 1) * 512], "ps%d" % i

        def bank(i):
            return psbig[:, i * 512:(i + 1) * 512], "ps%d" % i

        def r3(ap, **kw):
            return ap

        def load_const(name, n=None, eng="sp"):
            shp = IN_SPECS[name]
            n = int(np.prod(shp[1:]))
            t = alloc(n)
            src = din[name]
            if len(shp) == 3:
                src = src.rearrange("p a b -> p (a b)")
            elif len(shp) == 4:
                src = src.rearrange("p a b c -> p (a b c)")
            P.dma(eng, lambda e, t=t, src=src: e.dma_start(out=t, in_=src), writes=[name])
            return t

        ident_f = load_const("ident")
        anti_f = load_const("anti")
        anti32_f = load_const("anti32")
        psw_f = load_const("psw")
        mmask = load_const("mmask")
        cond_f = load_const("cond")
        bmod = load_const("bmod")
        gvec = load_const("gvec")
        sgn = load_const("sgn")
        ident_b = alloc(128, BF16)
        anti_b = alloc(128, BF16)
        anti32_b = alloc(128, BF16)
        ones_b = alloc(128, BF16)
        psw_b = alloc(128, BF16)
        P.op("dve", lambda e: e.tensor_copy(out=ident_b, in_=ident_f), reads=["ident"], writes=["ident_b"])
        P.op("dve", lambda e: e.tensor_copy(out=anti_b, in_=anti_f), reads=["anti"], writes=["anti_b"])
        P.op("dve", lambda e: e.tensor_copy(out=anti32_b, in_=anti32_f), reads=["anti32"], writes=["anti32_b"])
        P.op("dve", lambda e: e.memset(ones_b, 1.0), writes=["ones_b"])
        P.op("dve", lambda e: e.tensor_copy(out=psw_b, in_=psw_f), reads=["psw"], writes=["psw_b"])
        modv = alloc(192)
        a1 = alloc(32)
        a2 = alloc(32)
        NSLOT = 3
        SLOT = 6144
        wbuf_off = off[0]
        wbuf = [alloc(SLOT, BF16) for _ in range(NSLOT)]
        wbuf_end = off[0]
        wrr = [0]

        def wload(src_blk, n):
            i = wrr[0]
            wrr[0] = (i + 1) % NSLOT
            dst = wbuf[i][:, :n]
            P.dma("pool", lambda e, dst=dst, src_blk=src_blk: e.dma_start(out=dst, in_=src_blk), writes=["wbuf%d" % i])
            return wbuf[i], "wbuf%d" % i

        ssmp2 = load_const("ssmp").rearrange("p (d a g) -> p d a g", d=2, a=4)
        base_mark = off[0]
        sg_p = sgn[:, 0:1]
        sg_n = sgn[:, 1:2]
        PERS = [("L7re", 512), ("L7im", 512), ("LCre", 512), ("LCim", 512), ("L7imS", 512), ("LCreS", 512), ("Ec8", 512), ("Es8", 512),
                ("Qc", 320), ("Qs", 320), ("Ba", 1024), ("Bb", 1024), ("rho", 64), ("pc8", 64), ("ps8", 64)]
        NPERS = sum(n for _, n in PERS)

        def pers_views(buf):
            out, o = {}, 0
            for nm, n in PERS:
                out[nm] = buf[:, o:o + n]
                o += n
            return out

        def linear_fm(src, src_key, wname, Kc, NB, epi, after_block=None, src_k=None):
            if src_k is None:
                src_k = lambda k: src[:, k, :]
            wd = din[wname]
            nblk = IN_SPECS[wname][0]
            for b in range(nblk):
                if after_block is not None and b > 0:
                    after_block()
                wb, wkey = wload(wd[b].rearrange("p k n -> p (k n)"), Kc * NB)
                wv = wb[:, :Kc * NB].rearrange("p (k n) -> p k n", k=Kc)
                for mi in range(NB // 128):
                    m = b * (NB // 128) + mi
                    for (t0, n) in TB:
                        ps, pkey = nps()
                        for k in range(Kc):
                            P.op("pe", lambda e, ps=ps, wv=wv, k=k, mi=mi, t0=t0, n=n: e.matmul(
                                ps[:, :n], lhsT=wv[:, k, mi * 128:(mi + 1) * 128], rhs=src_k(k)[:, t0:t0 + n],
                                start=(k == 0), stop=(k == Kc - 1)),
                                reads=[wkey, src_key], writes=[pkey], skip_same=True)
                        epi(m, t0, n, ps, pkey)

        if stop == 100:
            P.dma("sp", lambda e: e.dma_start(out=dout["hfin"].rearrange("p a b c -> p (a b c)")[:, 0:128], in_=ident_f), reads=["ident", "ident_b", "ones_b"], is_out=True)
            P.emit()
            return nc
        m0 = off[0]
        scb = alloc(32, BF16)
        P.op("act", lambda e: e.activation(out=scb, in_=cond_f, func=ACT.Silu), reads=["cond"], writes=["scb"])
        scv = scb.rearrange("p (k c) -> p k c", c=2)
        psm, psmk = bank(0)
        psmB, psmBk = bank(6)
        modv3 = modv.rearrange("p (c t) -> p c t", t=2)

        def mod_block(b):
            wb, wkey = wload(din["wmod"][b].rearrange("p k n -> p (k n)"), 4096)
            wv = wb[:, :4096].rearrange("p (k n) -> p k n", k=16)
            for mi in range(2):
                ci = b * 2 + mi
                for k in range(16):
                    if ci < 32:
                        dst, dk = psm[:, ci * 2:ci * 2 + 2], psmk
                    else:
                        dst, dk = psmB[:, (ci - 32) * 2:(ci - 32) * 2 + 2], psmBk
                    P.op("pe", lambda e, wv=wv, k=k, mi=mi, dst=dst: e.matmul(
                        dst, lhsT=wv[:, k, mi * 128:(mi + 1) * 128], rhs=scv[:, k, :],
                        start=(k == 0), stop=(k == 15)), reads=[wkey, "scb"], writes=[dk], skip_same=True)

        def mod_evac(c0, c1, key):
            src_, sk_ = (psm[:, 2 * c0:2 * c1], psmk) if c0 < 32 else (psmB[:, 2 * (c0 - 32):2 * (c1 - 32)], psmBk)
            P.op("dve", lambda e: e.tensor_tensor(out=modv3[:, c0:c1, :], in0=src_.rearrange("p (c t) -> p c t", t=2),
                                                  in1=bmod[:, c0:c1].unsqueeze(2).broadcast_to([128, c1 - c0, 2]), op=ALU.add),
                 reads=[sk_, "bmod"], writes=[key])

        for b in range(16):
            mod_block(b)
        mod_evac(0, 32, "modvA")
        ps_lo[0] = 0
        mod_pending = list(range(16, 48))

        def mod_more(nmax):
            for _ in range(nmax):
                if mod_pending:
                    mod_block(mod_pending.pop(0))

        a1v = a1.rearrange("p (k t) -> p k t", t=2)
        a2v = a2.rearrange("p (k t) -> p k t", t=2)
        gv = gvec.rearrange("p (a k) -> p a k", a=3)

        def make_a(av, sc0, gi, key, mkey):
            P.op("dve", lambda e: e.tensor_scalar(out=av, in0=modv3[:, sc0:sc0 + 16, :], scalar1=1.0, scalar2=None, op0=ALU.add), reads=[mkey], writes=[key])
            P.op("dve", lambda e: e.tensor_tensor(out=av, in0=av, in1=gv[:, gi, :].unsqueeze(2).broadcast_to([128, 16, 2]), op=ALU.mult),
                 reads=[key, "gvec"], writes=[key])

        make_a(a1v, 16, 0, "a1", "modvA")
        sh1 = modv3[:, 0:16, :]
        g1 = modv3[:, 32:48, :]
        sh2 = modv3[:, 48:64, :]
        g2 = modv3[:, 80:96, :]

        u_tmA = alloc(64 * 128, BF16)
        u_tmB = alloc(64 * 128, BF16)
        mU = off[0]
        v_tm = alloc(14 * 256, BF16)
        stage = [alloc(512) for _ in range(3)]
        m_att = off[0]
        xm = alloc(16 * NTOK, BF16)
        xm3 = xm.rearrange("p (k t) -> p k t", k=16)
        HEAD = {}
        for d_ in range(2):
            H = {nm: alloc(64) for nm in ("dt", "th", "mg", "c", "s")}
            H["rp"] = alloc(9 * 64); H["rn"] = alloc(8 * 64)
            HEAD[d_] = H

            def head_ops(d=d_, H=H):
                hk = "smh%d" % d
                lamre = ssmp2[:, d, 0, :]; lamim = ssmp2[:, d, 1, :]; logdt = ssmp2[:, d, 2, :]
                rp = H["rp"].rearrange("p (k g) -> p k g", k=9); rn = H["rn"].rearrange("p (k g) -> p k g", k=8)
                P.op("act", lambda e: e.activation(out=H["dt"], in_=logdt, func=ACT.Exp), reads=["ssmp"], writes=[hk])
                P.op("dve", lambda e: e.tensor_tensor(out=H["th"], in0=lamim, in1=H["dt"], op=ALU.mult), reads=["ssmp", hk], writes=[hk])
                P.op("dve", lambda e: e.tensor_tensor(out=H["mg"], in0=lamre, in1=H["dt"], op=ALU.mult), reads=["ssmp", hk], writes=[hk])
                P.op("act", lambda e: e.activation(out=H["s"], in_=H["th"], func=ACT.Sin, scale=1.0 / 64), reads=[hk], writes=[hk])
                P.op("act", lambda e: e.activation(out=H["c"], in_=H["th"], func=ACT.Sin, scale=1.0 / 64, bias=PI / 2), reads=[hk], writes=[hk])
                for k in range(9):
                    P.op("act", lambda e, k=k: e.activation(out=rp[:, k, :], in_=H["mg"], func=ACT.Exp, scale=float(k)), reads=[hk], writes=[hk])
                for k in range(8):
                    P.op("act", lambda e, k=k: e.activation(out=rn[:, k, :], in_=H["mg"], func=ACT.Exp, scale=-float(k)), reads=[hk], writes=[hk])
            head_ops()
        m1 = off[0]

        def rms_phase(tag, load_fn, scale_fn, bias_fn, out_fn, rkeys, wkey, store_fn=None, NBK=256):
            sets = [(alloc(16 * NBK), alloc(16 * NBK, BF16), alloc(NBK), alloc(NBK)) for _ in range(2)]
            for bi, t0 in enumerate(range(0, NTOK, NBK)):
                n = min(NBK, NTOK - t0)
                ci = 0 if t0 < 1024 else 1
                xs, sq, rt, rstd = sets[bi % 2]
                kx, ksq, krt, krs = [tag + nm + str(bi % 2) for nm in ("xs", "sq", "rt", "rstd")]
                xs3 = xs.rearrange("p (k t) -> p k t", k=16)[:, :, :n]
                sq3 = sq.rearrange("p (k t) -> p k t", k=16)[:, :, :n]
                rt = rt[:, :n]
                rstd = rstd[:, :n]

                def blk(xs3=xs3, sq3=sq3, rt=rt, rstd=rstd, kx=kx, ksq=ksq, krt=krt, krs=krs, t0=t0, n=n, ci=ci):
                    load_fn(xs3, t0, n, kx)
                    P.op("act", lambda e: e.activation(out=sq3, in_=xs3, func=ACT.Square), reads=[kx], writes=[ksq])
                    ps, pk = nps()
                    for k in range(16):
                        P.op("pe", lambda e, k=k: e.matmul(ps[:, :n], lhsT=ones_b, rhs=sq3[:, k, :], start=(k == 0), stop=(k == 15)),
                             reads=["ones_b", ksq], writes=[pk], skip_same=True)
                    P.op("act", lambda e: e.activation(out=rt, in_=ps[:, :n], func=ACT.Sqrt, scale=1.0 / D, bias=EPS), reads=[pk], writes=[krt])
                    P.op("dve", lambda e: e.reciprocal(out=rstd, in_=rt), reads=[krt], writes=[krs])
                    P.op("dve", lambda e: e.tensor_tensor(out=xs3, in0=xs3, in1=rstd.unsqueeze(1).broadcast_to([128, 16, n]), op=ALU.mult),
                         reads=[kx, krs], writes=[kx])
                    for k in range(16):
                        o = out_fn(xs3, k, t0, n)
                        sc = scale_fn(k, ci)
                        bs = bias_fn(k, ci) if bias_fn is not None else None
                        wk = [((wkey if wkey is not None else kx), k % 2)]
                        if k % 2 == 0:
                            if bs is None:
                                P.op("dve", lambda e, o=o, k=k, sc=sc: e.tensor_scalar(out=o, in0=xs3[:, k, :], scalar1=sc, scalar2=None, op0=ALU.mult),
                                     reads=[kx] + rkeys, writes=wk)
                            else:
                                P.op("dve", lambda e, o=o, k=k, sc=sc, bs=bs: e.tensor_scalar(out=o, in0=xs3[:, k, :], scalar1=sc, scalar2=bs,
                                                                                             op0=ALU.mult, op1=ALU.add), reads=[kx] + rkeys, writes=wk)
                        else:
                            P.op("act", lambda e, o=o, k=k, sc=sc, bs=bs: e.activation(out=o, in_=xs3[:, k, :], func=ACT.Identity, scale=sc,
                                                                                      bias=(bs if bs is not None else 0.0)),
                                 reads=[kx] + rkeys, writes=wk)
                    if store_fn is not None:
                        store_fn(xs3, t0, n, [kx, (kx, 0), (kx, 1)])
                blk()
            if wkey is not None:
                P.join(wkey, [(wkey, 0), (wkey, 1)])

        def load_x(dst, t0, n, key):
            P.dma("sp", lambda e: e.dma_start(out=dst, in_=din["xT"][:, :, t0:t0 + n]), writes=[key])

        rms_phase("n1", load_x, lambda k, ci: a1v[:, k, ci:ci + 1], lambda k, ci: sh1[:, k, ci:ci + 1],
                  lambda xs3, k, t0, n: xm3[:, k, t0:t0 + n], ["a1", "modvA"], "xm")
        off[0] = m1
        P.barrier()
        if stop == 0:
            P.emit()
            return nc

        srr = [0]

        def nstage():
            i = srr[0]
            srr[0] = (i + 1) % 3
            return stage[i], "stage%d" % i

        def epi_store(dst_T, row0, func=None, tagk=None):
            def epi(m, t0, n, ps, pkey):
                sg, sk = nstage()
                if func is None:
                    P.op("dve", lambda e: e.tensor_copy(out=sg[:, :n], in_=ps[:, :n]), reads=[pkey], writes=[sk])
                else:
                    P.op("act", lambda e: e.activation(out=sg[:, :n], in_=ps[:, :n], func=func), reads=[pkey], writes=[sk])
                r = row0 + m * 128
                P.dma("sp", lambda e: e.dma_start(out=dst_T[r:r + 128, t0:t0 + n], in_=sg[:, :n]), reads=[sk],
                      writes=[(tagk, m, t0)])
            return epi

        linear_fm(xm3, "xm", "wqk", 16, 256, epi_store(qk_T, 0, None, "qk"))
        m_pre = off[0]
        ssmb = load_const("ssmb").rearrange("p (d a n) -> p d a n", d=2, a=2)
        SCR = {}
        for nm in ("t1", "t2", "inv", "nre", "nim", "fre", "fim", "tmp"):
            SCR[nm] = alloc(64)
        SCR["Lpre"] = alloc(9 * 64); SCR["Lpim"] = alloc(9 * 64); SCR["Lnre"] = alloc(8 * 64); SCR["Lnim"] = alloc(8 * 64)
        SCR["pc"] = alloc(9 * 64); SCR["ps"] = alloc(9 * 64)
        tA = alloc(1024); tB = alloc(1024)
        pers_tmp = alloc(NPERS)
        def ssm_pre(d):
            S = pers_views(pers_tmp)
            S.update(SCR)
            S.update(HEAD[d])
            kk = "sm"
            lamre = ssmp2[:, d, 0, :]; lamim = ssmp2[:, d, 1, :]; logdt = ssmp2[:, d, 2, :]

            def dv(fn, S=S, kk=kk):
                P.op("dve", fn, reads=[kk, "ssmp", "ssmb", "sgn", "smh%d" % d], writes=[kk])

            for it in range(6):
                dv(lambda e, S=S: e.tensor_tensor(out=S["t1"], in0=S["c"], in1=S["c"], op=ALU.mult))
                dv(lambda e, S=S: e.tensor_tensor(out=S["t2"], in0=S["s"], in1=S["s"], op=ALU.mult))
                dv(lambda e, S=S: e.scalar_tensor_tensor(out=S["s"], in0=S["c"], scalar=2.0, in1=S["s"], op0=ALU.mult, op1=ALU.mult))
                dv(lambda e, S=S: e.tensor_tensor(out=S["c"], in0=S["t1"], in1=S["t2"], op=ALU.subtract))
            pc = S["pc"].rearrange("p (k g) -> p k g", k=9); ps_ = S["ps"].rearrange("p (k g) -> p k g", k=9)
            rp = S["rp"].rearrange("p (k g) -> p k g", k=9); rn = S["rn"].rearrange("p (k g) -> p k g", k=8)
            dv(lambda e, pc=pc: e.memset(pc[:, 0, :], 1.0))
            dv(lambda e, ps_=ps_: e.memset(ps_[:, 0, :], 0.0))
            for k in range(1, 9):
                dv(lambda e, S=S, pc=pc, k=k: e.tensor_tensor(out=S["t1"], in0=pc[:, k - 1, :], in1=S["c"], op=ALU.mult))
                dv(lambda e, S=S, ps_=ps_, k=k: e.tensor_tensor(out=S["t2"], in0=ps_[:, k - 1, :], in1=S["s"], op=ALU.mult))
                dv(lambda e, S=S, pc=pc, k=k: e.tensor_tensor(out=S["tmp"], in0=pc[:, k - 1, :], in1=S["s"], op=ALU.mult))
                dv(lambda e, S=S, pc=pc, k=k: e.tensor_tensor(out=pc[:, k, :], in0=S["t1"], in1=S["t2"], op=ALU.subtract))
                dv(lambda e, S=S, ps_=ps_, k=k: e.tensor_tensor(out=S["t1"], in0=ps_[:, k - 1, :], in1=S["c"], op=ALU.mult))
                dv(lambda e, S=S, ps_=ps_, k=k: e.tensor_tensor(out=ps_[:, k, :], in0=S["t1"], in1=S["tmp"], op=ALU.add))
            Lpre = S["Lpre"].rearrange("p (k g) -> p k g", k=9); Lpim = S["Lpim"].rearrange("p (k g) -> p k g", k=9)
            Lnre = S["Lnre"].rearrange("p (k g) -> p k g", k=8); Lnim = S["Lnim"].rearrange("p (k g) -> p k g", k=8)
            dv(lambda e, Lpre=Lpre, rp=rp, pc=pc: e.tensor_tensor(out=Lpre, in0=rp, in1=pc, op=ALU.mult))
            dv(lambda e, Lpim=Lpim, rp=rp, ps_=ps_: e.tensor_tensor(out=Lpim, in0=rp, in1=ps_, op=ALU.mult))
            dv(lambda e, Lnre=Lnre, rn=rn, pc=pc: e.tensor_tensor(out=Lnre, in0=rn, in1=pc[:, 0:8, :], op=ALU.mult))
            dv(lambda e, Lnim=Lnim, rn=rn, ps_=ps_: e.scalar_tensor_tensor(out=Lnim, in0=rn, scalar=-1.0, in1=ps_[:, 0:8, :], op0=ALU.mult, op1=ALU.mult))
            L7re = S["L7re"].rearrange("p (g s) -> p g s", s=8); L7im = S["L7im"].rearrange("p (g s) -> p g s", s=8)
            LCre = S["LCre"].rearrange("p (g s) -> p g s", s=8); LCim = S["LCim"].rearrange("p (g s) -> p g s", s=8)
            for s in range(8):
                kq = 7 - s if d == 0 else s
                dv(lambda e, L7re=L7re, Lpre=Lpre, s=s, kq=kq: e.tensor_copy(out=L7re[:, :, s], in_=Lpre[:, kq, :]))
                dv(lambda e, L7im=L7im, Lpim=Lpim, s=s, kq=kq: e.tensor_copy(out=L7im[:, :, s], in_=Lpim[:, kq, :]))
                dv(lambda e, LCre=LCre, Lnre=Lnre, s=s, kq=kq: e.tensor_copy(out=LCre[:, :, s], in_=Lnre[:, kq, :]))
                dv(lambda e, LCim=LCim, Lnim=Lnim, s=s, kq=kq: e.tensor_copy(out=LCim[:, :, s], in_=Lnim[:, kq, :]))
            dv(lambda e, S=S: e.tensor_scalar(out=S["L7imS"], in0=S["L7im"], scalar1=sg_p, scalar2=None, op0=ALU.mult))
            dv(lambda e, S=S: e.tensor_scalar(out=S["LCreS"], in0=S["LCre"], scalar1=sg_p, scalar2=None, op0=ALU.mult))
            dv(lambda e, S=S, rp=rp: e.tensor_copy(out=S["rho"], in_=rp[:, 8, :]))
            dv(lambda e, S=S, Lpre=Lpre: e.tensor_scalar(out=S["nre"], in0=Lpre[:, 1, :], scalar1=-1.0, scalar2=None, op0=ALU.add))
            dv(lambda e, S=S, Lpim=Lpim: e.tensor_copy(out=S["nim"], in_=Lpim[:, 1, :]))
            dv(lambda e, S=S, lamre=lamre: e.tensor_tensor(out=S["t1"], in0=lamre, in1=lamre, op=ALU.mult))
            dv(lambda e, S=S, lamim=lamim: e.tensor_tensor(out=S["t2"], in0=lamim, in1=lamim, op=ALU.mult))
            dv(lambda e, S=S: e.tensor_tensor(out=S["t1"], in0=S["t1"], in1=S["t2"], op=ALU.add))
            dv(lambda e, S=S: e.reciprocal(out=S["inv"], in_=S["t1"]))
            dv(lambda e, S=S, lamre=lamre: e.tensor_tensor(out=S["t1"], in0=S["nre"], in1=lamre, op=ALU.mult))
            dv(lambda e, S=S, lamim=lamim: e.tensor_tensor(out=S["t2"], in0=S["nim"], in1=lamim, op=ALU.mult))
            dv(lambda e, S=S: e.tensor_tensor(out=S["t1"], in0=S["t1"], in1=S["t2"], op=ALU.add))
            dv(lambda e, S=S: e.tensor_tensor(out=S["fre"], in0=S["t1"], in1=S["inv"], op=ALU.mult))
            dv(lambda e, S=S, lamre=lamre: e.tensor_tensor(out=S["t1"], in0=S["nim"], in1=lamre, op=ALU.mult))
            dv(lambda e, S=S, lamim=lamim: e.tensor_tensor(out=S["t2"], in0=S["nre"], in1=lamim, op=ALU.mult))
            dv(lambda e, S=S: e.tensor_tensor(out=S["t1"], in0=S["t1"], in1=S["t2"], op=ALU.subtract))
            dv(lambda e, S=S: e.tensor_tensor(out=S["fim"], in0=S["t1"], in1=S["inv"], op=ALU.mult))
            ba = ssmb[:, d, 0, :].rearrange("p (g c) -> p g c", c=16)
            bb = ssmb[:, d, 1, :].rearrange("p (g c) -> p g c", c=16)
            Ba = S["Ba"].rearrange("p (g c) -> p g c", c=16); Bb = S["Bb"].rearrange("p (g c) -> p g c", c=16)
            freb = S["fre"].unsqueeze(2).broadcast_to([128, 64, 16]); fimb = S["fim"].unsqueeze(2).broadcast_to([128, 64, 16])
            tA3 = tA.rearrange("p (g c) -> p g c", c=16); tB3 = tB.rearrange("p (g c) -> p g c", c=16)
            dv(lambda e, tA3=tA3, ba=ba, freb=freb: e.tensor_tensor(out=tA3, in0=ba, in1=freb, op=ALU.mult))
            dv(lambda e, tB3=tB3, bb=bb, fimb=fimb: e.scalar_tensor_tensor(out=tB3, in0=bb, scalar=sg_p, in1=fimb, op0=ALU.mult, op1=ALU.mult))
            dv(lambda e, Ba=Ba, tA3=tA3, tB3=tB3: e.tensor_tensor(out=Ba, in0=tA3, in1=tB3, op=ALU.subtract))
            dv(lambda e, tA3=tA3, bb=bb, freb=freb: e.tensor_tensor(out=tA3, in0=bb, in1=freb, op=ALU.mult))
            dv(lambda e, tB3=tB3, ba=ba, fimb=fimb: e.scalar_tensor_tensor(out=tB3, in0=ba, scalar=sg_p, in1=fimb, op0=ALU.mult, op1=ALU.mult))
            dv(lambda e, Bb=Bb, tA3=tA3, tB3=tB3: e.tensor_tensor(out=Bb, in0=tA3, in1=tB3, op=ALU.add))

            dv(lambda e, S=S, pc=pc: e.tensor_copy(out=S["pc8"], in_=pc[:, 8, :]))
            dv(lambda e, S=S, ps_=ps_: e.tensor_copy(out=S["ps8"], in_=ps_[:, 8, :]))
            Ec = S["Ec8"].rearrange("p (g j) -> p g j", j=8); Es = S["Es8"].rearrange("p (g j) -> p g j", j=8)
            dv(lambda e, S=S, Ec=Ec: e.tensor_copy(out=Ec[:, :, 0], in_=S["pc8"]))
            dv(lambda e, S=S, Es=Es: e.tensor_copy(out=Es[:, :, 0], in_=S["ps8"]))
            for j in range(1, 8):
                dv(lambda e, S=S, Ec=Ec, j=j: e.tensor_tensor(out=S["t1"], in0=Ec[:, :, j - 1], in1=S["pc8"], op=ALU.mult))
                dv(lambda e, S=S, Es=Es, j=j: e.tensor_tensor(out=S["t2"], in0=Es[:, :, j - 1], in1=S["ps8"], op=ALU.mult))
                dv(lambda e, S=S, Ec=Ec, j=j: e.tensor_tensor(out=S["tmp"], in0=Ec[:, :, j - 1], in1=S["ps8"], op=ALU.mult))
                dv(lambda e, S=S, Ec=Ec, j=j: e.tensor_tensor(out=Ec[:, :, j], in0=S["t1"], in1=S["t2"], op=ALU.subtract))
                dv(lambda e, S=S, Es=Es, j=j: e.tensor_tensor(out=S["t1"], in0=Es[:, :, j - 1], in1=S["pc8"], op=ALU.mult))
                dv(lambda e, S=S, Es=Es, j=j: e.tensor_tensor(out=Es[:, :, j], in0=S["t1"], in1=S["tmp"], op=ALU.add))
            Qc = S["Qc"].rearrange("p (l g) -> p l g", l=5); Qs = S["Qs"].rearrange("p (l g) -> p l g", l=5)
            dv(lambda e, Qc=Qc, Ec=Ec: e.tensor_copy(out=Qc[:, 0, :], in_=Ec[:, :, 7]))
            dv(lambda e, Qs=Qs, Es=Es: e.tensor_copy(out=Qs[:, 0, :], in_=Es[:, :, 7]))
            for l in range(1, 5):
                dv(lambda e, S=S, Qc=Qc, l=l: e.tensor_tensor(out=S["t1"], in0=Qc[:, l - 1, :], in1=Qc[:, l - 1, :], op=ALU.mult))
                dv(lambda e, S=S, Qs=Qs, l=l: e.tensor_tensor(out=S["t2"], in0=Qs[:, l - 1, :], in1=Qs[:, l - 1, :], op=ALU.mult))
                dv(lambda e, S=S, Qc=Qc, Qs=Qs, l=l: e.scalar_tensor_tensor(out=Qs[:, l, :], in0=Qc[:, l - 1, :], scalar=2.0, in1=Qs[:, l - 1, :], op0=ALU.mult, op1=ALU.mult))
                dv(lambda e, S=S, Qc=Qc, l=l: e.tensor_tensor(out=Qc[:, l, :], in0=S["t1"], in1=S["t2"], op=ALU.subtract))


        def pers_store(d):
            P.dma("sp", lambda e, d=d: e.dma_start(out=sm_T[d], in_=pers_tmp), reads=["sm"], writes=[("smT", d)])

        ssm_pre(0)
        linear_fm(xm3, "xm", "wgs", 16, 256, epi_store(sg_T, 0, ACT.Sigmoid, "sgs"))
        pers_store(0)
        ssm_pre(1)
        linear_fm(xm3, "xm", "wga", 16, 256, epi_store(sg_T, 2048, ACT.Sigmoid, "sga"))
        pers_store(1)
        off[0] = m_pre
        if stop == 110:
            P.emit()
            return nc
        P.dma("sp", lambda e: e.dma_start(out=dout["kTo"], in_=qk_T[1024:1280, :]),
              reads=[("qk", m, t0) for m in range(8, 10) for (t0, n) in TB], is_out=True)
        if stop == 111:
            P.emit()
            return nc
        v3 = v_tm.rearrange("p (t c) -> p t c", c=256)
        P.op("dve", lambda e: e.memset(v_tm, 0.0), writes=["v_tm"])
        wb, wkey = wload(din["wv"][0].rearrange("p k n -> p (k n)"), 4096)
        wv_ = wb[:, :4096].rearrange("p (k n) -> p k n", k=16)
        for tt in range(10):
            ps, pk = nps()
            for k in range(16):
                P.op("pe", lambda e, ps=ps, k=k, tt=tt: e.matmul(ps[:, :256], lhsT=xm3[:, k, tt * 128:(tt + 1) * 128], rhs=wv_[:, k, :],
                                                               start=(k == 0), stop=(k == 15)), reads=[wkey, "xm"], writes=[pk], skip_same=True)
            sg, sk = nstage()
            P.op("act", lambda e, ps=ps, sg=sg: e.activation(out=sg[:, :256], in_=ps[:, :256], func=ACT.Identity), reads=[pk], writes=[sk])
            pt = tt + 1 if tt < 8 else tt + 3
            import os
            VAR = os.environ.get("KVAR", "")
            if "a" not in VAR:
                P.op("dve", lambda e, sg=sg, pt=pt: e.tensor_copy(out=v3[:, pt, :], in_=sg[:, :256]), reads=[sk], writes=["v_tm"])
            if "b" not in VAR:
                P.dma("sp", lambda e, sg=sg, tt=tt: e.dma_start(out=dout["vo"][tt * 128:(tt + 1) * 128, :], in_=sg[:, :256]), reads=[sk], is_out=True)
        if stop == 112:
            P.emit()
            return nc
        uA4 = u_tmA.rearrange("p (g s c) -> p g s c", g=64, s=8)
        uB4 = u_tmB.rearrange("p (g s c) -> p g s c", g=64, s=8)
        xmA = xm3[:, :, 0:1024].rearrange("p k (j s) -> p k j s", s=8)
        xmB = xm3[:, :, 1024:1280].rearrange("p k (j s) -> p k j s", s=8)
        for b in range(4):
            wb, wkey = wload(din["wu"][b].rearrange("p k n -> p (k n)"), 4096)
            wvu = wb[:, :4096].rearrange("p (k n) -> p k n", k=16)
            for s in range(8):
                psA, pkA = nps()
                for k in range(16):
                    P.op("pe", lambda e, psA=psA, k=k, s=s, wvu=wvu: e.matmul(psA[:, :256], lhsT=xmA[:, k, :, s], rhs=wvu[:, k, :],
                                                                             start=(k == 0), stop=(k == 15)), reads=[wkey, "xm"], writes=[pkA], skip_same=True)
                P.op("dve", lambda e, psA=psA, s=s, b=b: e.tensor_copy(out=uA4[:, b * 16:(b + 1) * 16, s, :],
                                                                      in_=psA[:, :256].rearrange("p (g c) -> p g c", c=16)),
                     reads=[pkA], writes=["u_tmA"])
                psB, pkB = nps()
                for k in range(16):
                    P.op("pe", lambda e, psB=psB, k=k, s=s, wvu=wvu: e.matmul(psB[0:32, :256], lhsT=xmB[:, k, :, s], rhs=wvu[:, k, :],
                                                                             start=(k == 0), stop=(k == 15)), reads=[wkey, "xm"], writes=[pkB], skip_same=True)
                P.op("act", lambda e, psB=psB, s=s, b=b: e.activation(out=uB4[0:32, b * 16:(b + 1) * 16, s, :],
                                                                     in_=psB[0:32, :256].rearrange("p (g c) -> p g c", c=16), func=ACT.Identity),
                     reads=[pkB], writes=["u_tmB"])
        off[0] = m_att
        P.barrier()
        if stop == 1:
            P.emit()
            return nc

        qT = alloc(8 * NTOK, BF16)
        qT3 = qT.rearrange("p (h t) -> p h t", h=8)
        kTp = alloc(2 * 1792, BF16)
        kT3 = kTp.rearrange("p (h t) -> p h t", h=2)
        P.op("dve", lambda e: e.memset(kTp, 0.0), writes=["kTp"])
        ropeC = load_const("ropeC")
        ropeS = load_const("ropeS")
        maskL = load_const("maskL").rearrange("p (q k) -> p q k", q=10)
        cmask = load_const("cmask")
        sink = load_const("sink")
        nsink = alloc(8)
        P.op("dve", lambda e: e.tensor_scalar(out=nsink, in0=sink, scalar1=-1.0, scalar2=None, op0=ALU.mult), reads=["sink"], writes=["nsink"])
        ck_f = load_const("ckT")
        cv_f = load_const("cv")
        ck_b = alloc(512, BF16)
        cv_b = alloc(512, BF16)
        P.op("dve", lambda e: e.tensor_copy(out=ck_b, in_=ck_f), reads=["ckT"], writes=["ck_b"])
        P.op("dve", lambda e: e.tensor_copy(out=cv_b, in_=cv_f), reads=["cv"], writes=["cv_b"])
        ck3 = ck_b.rearrange("p (h t) -> p h t", h=2)
        cv3 = cv_b.rearrange("p (t c) -> p t c", t=2)
        PQ = [(alloc(NTOK), alloc(NTOK)) for _ in range(2)]
        allqk = [("qk", m, t0) for m in range(10) for (t0, n) in TB]

        def rope_head(idx, h):
            Pq, Qq = PQ[idx % 2]
            kp, kq = "Pq%d" % (idx % 2), "Qq%d" % (idx % 2)
            r0 = h * 128
            for part in range(2):
                for half in range(2):
                    rows = r0 + half * 64 + part * 32
                    pp = part * 64 + half * 32
                    qq = (1 - part) * 64 + half * 32
                    P.dma("sp", lambda e, rows=rows, pp=pp: e.dma_start(out=Pq[pp:pp + 32, :], in_=qk_T[rows:rows + 32, :]),
                          reads=allqk, writes=[kp])
                    P.dma("sp", lambda e, rows=rows, qq=qq: e.dma_start(out=Qq[qq:qq + 32, :], in_=qk_T[rows:rows + 32, :]),
                          reads=allqk, writes=[kq])
            P.op("dve", lambda e: e.tensor_tensor(out=Pq, in0=Pq, in1=ropeC, op=ALU.mult), reads=[kp, "ropeC"], writes=[kp])
            P.op("pool", lambda e: e.tensor_tensor(out=Qq, in0=Qq, in1=ropeS, op=ALU.mult), reads=[kq, "ropeS"], writes=[kq])
            if h < 8:
                P.op("dve", lambda e: e.tensor_tensor(out=qT3[:, h, :], in0=Pq, in1=Qq, op=ALU.add), reads=[kp, kq], writes=[("qT", h)])
            else:
                kh = h - 8
                P.op("dve", lambda e: e.tensor_tensor(out=kT3[:, kh, 128:1152], in0=Pq[:, 0:1024], in1=Qq[:, 0:1024], op=ALU.add),
                     reads=[kp, kq, "kTp"], writes=["kTp"])
                P.op("dve", lambda e: e.tensor_tensor(out=kT3[:, kh, 1408:1664], in0=Pq[:, 1024:1280], in1=Qq[:, 1024:1280], op=ALU.add),
                     reads=[kp, kq, "kTp"], writes=["kTp"])

        for idx, h in enumerate([8, 9] + list(range(8))):
            rope_head(idx, h)
        NSET = 3
        smcs = [alloc(640) for _ in range(NSET)]
        pexs = [alloc(640) for _ in range(NSET)]
        pns = [alloc(640, BF16) for _ in range(NSET)]
        pTs = [alloc(640, BF16) for _ in range(NSET)]
        stats = [alloc(8) for _ in range(NSET)]
        ost = [alloc(NTOK, BF16), alloc(NTOK, BF16)]
        tiles = [(h, qt) for h in range(8) for qt in range(10)]

        def sbanks(ti):
            b0 = 2 * (ti % 2)
            return bank(b0), bank(b0 + 1)

        def tbanks(ti):
            return bank(4), bank(5)

        def attn_scores(ti):
            h, qt = tiles[ti]
            kvh = h // 4
            pt = qt + 1 if qt < 8 else qt + 3
            (psS, kS), (psC, kC) = sbanks(ti)
            P.op("pe", lambda e: e.matmul(psS[:, 0:384], lhsT=qT3[:, h, qt * 128:(qt + 1) * 128], rhs=kT3[:, kvh, (pt - 1) * 128:(pt + 2) * 128],
                                         start=True, stop=True), reads=[("qT", h), "kTp"], writes=[kS], skip_same=True)
            P.op("pe", lambda e: e.matmul(psC[:, 0:256], lhsT=qT3[:, h, qt * 128:(qt + 1) * 128], rhs=ck3[:, kvh, :], start=True, stop=True),
                 reads=[("qT", h), "ck_b"], writes=[kC], skip_same=True)

        def attn_soft_a(ti):
            h, qt = tiles[ti]
            si = ti % NSET
            smc, stat = smcs[si], stats[si]
            K = lambda nm: "%s%d" % (nm, si)
            (psS, kS), (psC, kC) = sbanks(ti)
            P.op("dve", lambda e: e.scalar_tensor_tensor(out=smc[:, 0:384], in0=psS[:, 0:384], scalar=SCALE, in1=maskL[:, qt, :],
                                                        op0=ALU.mult, op1=ALU.add), reads=[kS, "maskL"], writes=[K("smcA")])
            P.op("act", lambda e: e.activation(out=smc[:, 384:640], in_=psC[:, 0:256], func=ACT.Identity, scale=SCALE, bias=cmask[:, qt:qt + 1]),
                 reads=[kC, "cmask"], writes=[K("smcB")])
            P.op("dve", lambda e: e.reduce_max(out=stat[:, 0:1], in_=smc, axis=AX.X, negate=True), reads=[K("smcA"), K("smcB")], writes=[K("stat0")])
            P.op("dve", lambda e: e.tensor_tensor(out=stat[:, 1:2], in0=stat[:, 0:1], in1=nsink[:, h:h + 1], op=ALU.min),
                 reads=[K("stat0"), "nsink"], writes=[K("stat1")])

        def attn_soft_b(ti):
            h, qt = tiles[ti]
            si = ti % NSET
            smc, pex, pn, stat = smcs[si], pexs[si], pns[si], stats[si]
            K = lambda nm: "%s%d" % (nm, si)
            P.op("act", lambda e: e.activation(out=pex, in_=smc, func=ACT.Exp, bias=stat[:, 1:2], scale=1.0, accum_out=stat[:, 2:3]),
                 reads=[K("smcA"), K("smcB"), K("stat1")], writes=[K("pex"), K("stat2")])
            P.op("act", lambda e: e.activation(out=stat[:, 3:4], in_=stat[:, 1:2], func=ACT.Exp, bias=sink[:, h:h + 1], scale=1.0),
                 reads=[K("stat1"), "sink"], writes=[K("stat3")])
            P.op("dve", lambda e: e.tensor_tensor(out=stat[:, 4:5], in0=stat[:, 2:3], in1=stat[:, 3:4], op=ALU.add),
                 reads=[K("stat2"), K("stat3")], writes=[K("stat4")])
            P.op("dve", lambda e: e.reciprocal(out=stat[:, 5:6], in_=stat[:, 4:5]), reads=[K("stat4")], writes=[K("stat5")])
            P.op("dve", lambda e: e.tensor_scalar(out=pn, in0=pex, scalar1=stat[:, 5:6], scalar2=None, op0=ALU.mult),
                 reads=[K("pex"), K("stat5")], writes=[K("pn")])

        def attn_pv(ti):
            h, qt = tiles[ti]
            kvh = h // 4
            pt = qt + 1 if qt < 8 else qt + 3
            si = ti % NSET
            pn, pT = pns[si], pTs[si]
            K = lambda nm: "%s%d" % (nm, si)
            (psT1, kT1), (psX, kX) = tbanks(ti)
            osg = ost[h % 2]
            okey = "ost%d" % (h % 2)
            for i in range(5):
                dst = psT1[:, i * 128:(i + 1) * 128] if i < 4 else psX[:, 0:128]
                P.op("pe", lambda e, dst=dst, i=i: e.matmul(dst, lhsT=pn[:, i * 128:(i + 1) * 128], rhs=ident_b, start=True, stop=True),
                     reads=[K("pn"), "ident_b"], writes=[kT1 if i < 4 else kX], skip_same=True)
            P.op("act", lambda e: e.activation(out=pT[:, 0:512], in_=psT1, func=ACT.Identity), reads=[kT1], writes=[K("pTa")])
            P.op("dve", lambda e: e.tensor_copy(out=pT[:, 512:640], in_=psX[:, 0:128]), reads=[kX], writes=[K("pTb")])
            for i in range(5):
                lhs = v3[:, pt - 1 + i, kvh * 128:(kvh + 1) * 128] if i < 3 else cv3[:, i - 3, kvh * 128:(kvh + 1) * 128]
                P.op("pe", lambda e, lhs=lhs, i=i: e.matmul(psX[:, 128:256], lhsT=lhs, rhs=pT[:, i * 128:(i + 1) * 128], start=(i == 0), stop=(i == 4)),
                     reads=["v_tm", "cv_b", K("pTa"), K("pTb")], writes=[kX], skip_same=True)
            P.op("act", lambda e: e.activation(out=osg[:, qt * 128:(qt + 1) * 128], in_=psX[:, 128:256], func=ACT.Identity), reads=[kX], writes=[okey])
            if qt == 9:
                P.dma("sp", lambda e: e.dma_start(out=o_T[h * 128:(h + 1) * 128, :], in_=osg), reads=[okey], writes=[("o_T", h)])

        NT_ = len(tiles)
        attn_scores(0)
        attn_scores(1)
        attn_soft_a(0)
        attn_scores(2)
        attn_soft_a(1)
        attn_soft_b(0)
        for ti in range(NT_):
            if ti + 3 < NT_:
                attn_scores(ti + 3)
            if ti + 2 < NT_:
                attn_soft_a(ti + 2)
            if ti + 1 < NT_:
                attn_soft_b(ti + 1)
            attn_pv(ti)
            if ti % 2 == 0:
                mod_more(1)
        mod_more(99)
        mod_evac(32, 96, "modvB")
        make_a(a2v, 64, 1, "a2", "modvB")
        off[0] = mU
        P.barrier()
        if stop == 2:
            P.emit()
            return nc

        ssmp = ssmp2
        ssmc = load_const("ssmc").rearrange("p (d a n) -> p d a n", d=2, a=2)
        dvec = load_const("dvec")
        bmv = load_const("bm")
        hfin = alloc(2 * 64 * 5)
        hfin4 = hfin.rearrange("p (d g q) -> p d g q", d=2, g=64)
        small = {}
        for d in range(2):
            pbuf = alloc(NPERS)
            P.dma("sp", lambda e, pbuf=pbuf, d=d: e.dma_start(out=pbuf, in_=sm_T[d]), reads=[("smT", d)], writes=["sm"])
            small[d] = pers_views(pbuf)
        if stop == 3:
            P.emit()
            return nc
        GC = 8
        _save = off[0]
        off[0] = wbuf_off
        g1a_full = alloc(GC * NJ); g1b_full = alloc(GC * NJ)
        Wd = alloc(GC * NJ); Vd = alloc(GC * NJ)
        cosTs = [alloc(GC * NJ), None]; sinTs = [alloc(GC * NJ), None]
        assert off[0] <= wbuf_end, (off[0], wbuf_end)
        off[0] = _save
        cosTs[1] = alloc(GC * NJ); sinTs[1] = alloc(GC * NJ)
        rhoMs = [alloc(GC * NJ), alloc(GC * NJ)]
        ptt1 = alloc(GC * 64); ptt2 = alloc(GC * 64)
        g1a = g1a_full[:, :GC * 128]; g1b = g1b_full[:, :GC * 128]
        PBst = alloc(GC * 128, BF16); PBsw = alloc(GC * 128, BF16); PCst = alloc(GC * 128, BF16); PCsw = alloc(GC * 128, BF16)
        Wst = alloc(GC * 128, BF16); Wsw = alloc(GC * 128, BF16); Mm = alloc(GC * 128, BF16)
        Ugc = alloc(GC * NJ, BF16)
        Tt = g1a_full
        g1b_bf = g1b_full.bitcast(BF16)
        Ah = g1b_bf[:, 0:GC * NJ]; Bh = g1b_bf[:, GC * NJ:2 * GC * NJ]
        qc = alloc(GC); qs = alloc(GC); q1 = alloc(GC); q2 = alloc(GC)
        yA = [alloc(8 * GC * 16, BF16) for _ in range(2)]
        yB = [alloc(8 * GC * 16, BF16) for _ in range(2)]
        zst = [alloc(NTOK)] * 2
        zsb = alloc(NTOK, BF16)
        vsel = alloc(GC * 5)

        def v4(t):
            return t.rearrange("p (g s c) -> p g s c", g=GC, s=8)

        def v3g(t):
            return t.rearrange("p (g n) -> p g n", g=GC)

        def vj(t):
            return t.rearrange("p (g j) -> p g j", g=GC)

        jsel = [31, 63, 95, 127, 159]
        def ssm_tables(gc, d, part):
            G0 = gc * GC
            S = small[d]
            kk = "sm"
            par = (gc * 2 + d) % 2
            cosT = cosTs[par]; sinT = sinTs[par]; rhoM = rhoMs[par]
            tabk = "tab%d" % par
            cT = vj(cosT); sT = vj(sinT)
            eng_ = "pool" if part == "lo" else "dve"
            PO = lambda fn, r, w: P.op(eng_, fn, reads=r, writes=w)
            Ec = S["Ec8"].rearrange("p (g j) -> p g j", j=8)[:, G0:G0 + GC, :]
            Es = S["Es8"].rearrange("p (g j) -> p g j", j=8)[:, G0:G0 + GC, :]
            Qc = S["Qc"].rearrange("p (l g) -> p l g", l=5); Qs = S["Qs"].rearrange("p (l g) -> p l g", l=5)
            if part == "lo":
                PO(lambda e: e.tensor_copy(out=cT[:, :, 0:8], in_=Ec), [kk], [tabk])
                PO(lambda e: e.tensor_copy(out=sT[:, :, 0:8], in_=Es), [kk], [tabk])
            for (n, lvl) in ([(8, 0), (16, 1), (32, 2)] if part == "lo" else [(64, 3), (128, 4)]):
                mlen = min(n, NJ - n)
                qcb = Qc[:, lvl, G0:G0 + GC].unsqueeze(2).broadcast_to([128, GC, mlen]); qsb = Qs[:, lvl, G0:G0 + GC].unsqueeze(2).broadcast_to([128, GC, mlen])
                if part == "lo":
                    tt1 = ptt1.rearrange("p (g j) -> p g j", g=GC)[:, :, 0:mlen]; tt2 = ptt2.rearrange("p (g j) -> p g j", g=GC)[:, :, 0:mlen]
                    k1, k2 = "ptt1", "ptt2"
                else:
                    tt1 = vj(Wd)[:, :, 0:mlen]; tt2 = vj(Vd)[:, :, 0:mlen]
                    k1, k2 = "Wd", "Vd"
                PO(lambda e, tt1=tt1, mlen=mlen, qcb=qcb: e.tensor_tensor(out=tt1, in0=cT[:, :, 0:mlen], in1=qcb, op=ALU.mult), [tabk, kk], [k1])
                PO(lambda e, tt2=tt2, mlen=mlen, qsb=qsb: e.tensor_tensor(out=tt2, in0=sT[:, :, 0:mlen], in1=qsb, op=ALU.mult), [tabk, kk], [k2])
                PO(lambda e, tt1=tt1, tt2=tt2, n=n, mlen=mlen: e.tensor_tensor(out=cT[:, :, n:n + mlen], in0=tt1, in1=tt2, op=ALU.subtract), [k1, k2], [tabk])
                PO(lambda e, tt1=tt1, mlen=mlen, qsb=qsb: e.tensor_tensor(out=tt1, in0=cT[:, :, 0:mlen], in1=qsb, op=ALU.mult), [tabk, kk], [k1])
                PO(lambda e, tt2=tt2, mlen=mlen, qcb=qcb: e.tensor_tensor(out=tt2, in0=sT[:, :, 0:mlen], in1=qcb, op=ALU.mult), [tabk, kk], [k2])
                PO(lambda e, tt1=tt1, tt2=tt2, n=n, mlen=mlen: e.tensor_tensor(out=sT[:, :, n:n + mlen], in0=tt1, in1=tt2, op=ALU.add), [k1, k2], [tabk])
            if part == "lo":
              PO(lambda e: e.tensor_tensor(out=vj(rhoM), in0=S["rho"][:, G0:G0 + GC].unsqueeze(2).broadcast_to([128, GC, NJ]),
                                         in1=bmv.unsqueeze(1).broadcast_to([128, GC, NJ]), op=ALU.mult), [kk, "bm"], [tabk])

        def ssm_round(gc, d):
            G0 = gc * GC
            S = small[d]
            kk = "sm"
            rk = "r"
            idm = ident_b if d == 0 else anti_b
            idm32 = ident_b if d == 0 else anti32_b
            for q2_ in range(GC // 2):
                ps, pk = nps()
                for gi in range(2):
                    gl = q2_ * 2 + gi
                    g = G0 + gl
                    P.op("pe", lambda e, ps=ps, gi=gi, g=g: e.matmul(ps[:, gi * NJ:gi * NJ + 128], lhsT=u_tmA[:, g * 128:(g + 1) * 128], rhs=idm,
                                                                   start=True, stop=True), reads=["u_tmA", "ident_b", "anti_b"], writes=[pk], skip_same=True)
                    P.op("pe", lambda e, ps=ps, gi=gi, g=g: e.matmul(ps[:, gi * NJ + 128:(gi + 1) * NJ], lhsT=u_tmB[0:32, g * 128:(g + 1) * 128],
                                                                   rhs=idm32[0:32, 0:32], start=True, stop=True),
                         reads=["u_tmB", "ident_b", "anti32_b"], writes=[pk], skip_same=True)
                P.op("act", lambda e, ps=ps, q2_=q2_: e.activation(out=Ugc[:, q2_ * 2 * NJ:(q2_ + 1) * 2 * NJ], in_=ps[:, 0:2 * NJ], func=ACT.Identity),
                     reads=[pk], writes=["Ugc"])
            def Lv(nm):
                return S[nm].rearrange("p (g s) -> p g s", s=8)[:, G0:G0 + GC, :]
            Ba3 = S["Ba"].rearrange("p (g c) -> p g c", c=16)[:, G0:G0 + GC, :]
            Bb3 = S["Bb"].rearrange("p (g c) -> p g c", c=16)[:, G0:G0 + GC, :]
            Ca3 = ssmc[:, d, 0, :].rearrange("p (g c) -> p g c", c=16)[:, G0:G0 + GC, :]
            Cb3 = ssmc[:, d, 1, :].rearrange("p (g c) -> p g c", c=16)[:, G0:G0 + GC, :]

            def gen(out, X, L1, Y, L2, wkey):
                Xb = X.unsqueeze(2).broadcast_to([128, GC, 8, 16]); Yb = Y.unsqueeze(2).broadcast_to([128, GC, 8, 16])
                l1 = L1.unsqueeze(3).broadcast_to([128, GC, 8, 16]); l2 = L2.unsqueeze(3).broadcast_to([128, GC, 8, 16])
                P.op("dve", lambda e: e.tensor_tensor(out=v4(g1a), in0=Xb, in1=l1, op=ALU.mult), reads=[kk, "ssmc"], writes=["g1a"])
                P.op("dve", lambda e: e.tensor_tensor(out=v4(g1b), in0=Yb, in1=l2, op=ALU.mult), reads=[kk, "ssmc"], writes=["g1b"])
                P.op("dve", lambda e: e.tensor_tensor(out=out, in0=g1a, in1=g1b, op=ALU.subtract), reads=["g1a", "g1b"], writes=[wkey])

            def swap(src_, dst_, scl, rkey, wkey):
                for hf_ in range(GC * 128 // 512):
                    ps, pk = nps()
                    P.op("pe", lambda e, ps=ps, hf_=hf_: e.matmul(ps, lhsT=psw_b, rhs=src_[:, hf_ * 512:(hf_ + 1) * 512], start=True, stop=True),
                         reads=[rkey, "psw_b"], writes=[pk], skip_same=True)
                    P.op("act", lambda e, ps=ps, hf_=hf_: e.activation(out=dst_[:, hf_ * 512:(hf_ + 1) * 512], in_=ps, func=ACT.Identity, scale=scl),
                         reads=[pk], writes=[wkey])

            def m_build():
                for q4 in range(GC // 4):
                    ps, pk = nps()
                    for gi in range(4):
                        gl = q4 * 4 + gi
                        P.op("pe", lambda e, ps=ps, gi=gi, gl=gl: e.matmul(ps[:, gi * 128:(gi + 1) * 128], lhsT=v3g(PBst)[:, gl, :], rhs=v3g(PCst)[:, gl, :],
                                                                       start=True, stop=True), reads=["PB", "PC"], writes=[pk], skip_same=True)
                    mk = mmask.rearrange("p (d n) -> p d n", d=2)[:, d, :].unsqueeze(1).broadcast_to([128, 4, 128])
                    if d == 1:
                        P.op("dve", lambda e, ps=ps, q4=q4, mk=mk: e.tensor_tensor(out=v3g(Mm)[:, q4 * 4:(q4 + 1) * 4, :],
                                                                                 in0=ps.rearrange("p (g n) -> p g n", g=4), in1=mk, op=ALU.mult),
                             reads=[pk, "mmask"], writes=["Mm"])
                    else:
                        P.op("dve", lambda e, ps=ps, mk=mk: e.tensor_tensor(out=g1a[:, 0:512].rearrange("p (g n) -> p g n", g=4),
                                                                          in0=ps.rearrange("p (g n) -> p g n", g=4), in1=mk, op=ALU.mult),
                             reads=[pk, "mmask"], writes=["g1a"])
                        for gi in range(4):
                            gl = q4 * 4 + gi
                            P.op("dve", lambda e, gi=gi, gl=gl: e.scalar_tensor_tensor(out=v3g(Mm)[:, gl, :], in0=ident_f, scalar=dvec[:, G0 + gl:G0 + gl + 1],
                                                                                     in1=g1a[:, gi * 128:(gi + 1) * 128], op0=ALU.mult, op1=ALU.add),
                                 reads=["g1a", "ident", "dvec"], writes=["Mm"])

            gen(PBst, Ba3, Lv("L7re"), Bb3, Lv("L7imS"), "PB")
            swap(PBst, PBsw, 1.0, "PB", "PB2")
            for (src, dst, dk, rk_) in ((PBst, Wst, "Wst", "PB"), (PBsw, Wsw, "Wsw", "PB2")):
                for q4 in range(GC // 4):
                    ps, pk = nps()
                    for gi in range(4):
                        gl = q4 * 4 + gi
                        P.op("pe", lambda e, ps=ps, gi=gi, gl=gl, src=src: e.matmul(ps[:, gi * 128:(gi + 1) * 128], lhsT=v3g(src)[:, gl, :], rhs=ident_b,
                                                                                start=True, stop=True), reads=[rk_, "ident_b"], writes=[pk], skip_same=True)
                    P.op("act", lambda e, ps=ps, q4=q4, dst=dst: e.activation(out=dst[:, q4 * 512:(q4 + 1) * 512], in_=ps, func=ACT.Identity),
                         reads=[pk], writes=[dk])
            gen(PCst, Ca3, Lv("LCreS"), Cb3, Lv("LCim"), "PC")
            par = (gc * 2 + d) % 2
            cosT = cosTs[par]; sinT = sinTs[par]; rhoM = rhoMs[par]
            tabk = "tab%d" % par
            cT = vj(cosT); sT = vj(sinT)
            for q2_ in range(GC // 2):
                psS, kS = nps()
                psW, kW = nps()
                for gi in range(2):
                    gl = q2_ * 2 + gi
                    P.op("pe", lambda e, psS=psS, gi=gi, gl=gl: e.matmul(psS[:, gi * NJ:(gi + 1) * NJ], lhsT=v3g(Wst)[:, gl, :], rhs=vj(Ugc)[:, gl, :],
                                                                       start=True, stop=True), reads=["Wst", "Ugc"], writes=[kS], skip_same=True)
                    P.op("pe", lambda e, psW=psW, gi=gi, gl=gl: e.matmul(psW[:, gi * NJ:(gi + 1) * NJ], lhsT=v3g(Wsw)[:, gl, :], rhs=vj(Ugc)[:, gl, :],
                                                                       start=True, stop=True), reads=["Wsw", "Ugc"], writes=[kW], skip_same=True)
                sl = slice(q2_ * 2 * NJ, (q2_ + 1) * 2 * NJ)
                P.op("dve", lambda e, psS=psS, sl=sl: e.tensor_tensor(out=Wd[:, sl], in0=psS[:, 0:2 * NJ], in1=cosT[:, sl], op=ALU.mult),
                     reads=[kS, tabk], writes=["Wd"])
                P.op("dve", lambda e, psW=psW, sl=sl: e.tensor_tensor(out=Tt[:, sl], in0=psW[:, 0:2 * NJ], in1=sinT[:, sl], op=ALU.mult),
                     reads=[kW, tabk], writes=["g1a"])
            P.op("dve", lambda e: e.tensor_tensor(out=Wd, in0=Wd, in1=Tt, op=ALU.subtract), reads=["Wd", "g1a"], writes=["Wd"])
            h0 = ssmp[:, d, 3, :]
            for gl in range(GC):
                P.op("dve", lambda e, gl=gl: e.tensor_tensor_scan(out=vj(Vd)[:, gl, :], data0=vj(rhoM)[:, gl, :], data1=vj(Wd)[:, gl, :],
                                                                initial=h0[:, G0 + gl:G0 + gl + 1], op0=ALU.mult, op1=ALU.add),
                     reads=[tabk, "Wd", "ssmp"], writes=["Vd"])
            swap(PCst, PCsw, -1.0, "PC", "PC2")
            m_build()
            vsel3 = vsel.rearrange("p (g q) -> p g q", q=5)
            SEL = slice(31, NJ, 32)
            P.op("pool", lambda e: e.tensor_copy(out=vsel3, in_=vj(Vd)[:, :, SEL]), reads=["Vd"], writes=["vsel"])
            psF, kF = nps()
            P.op("pe", lambda e: e.matmul(psF[:, 0:GC * 5], lhsT=psw_f, rhs=vsel, start=True, stop=True), reads=["psw", "vsel"], writes=[kF], skip_same=True)
            hf = hfin4[:, d, G0:G0 + GC, :]
            tsel = vj(Tt)[:, :, 0:5]
            P.op("dve", lambda e: e.tensor_tensor(out=tsel, in0=psF[:, 0:GC * 5].rearrange("p (g q) -> p g q", q=5), in1=sT[:, :, SEL], op=ALU.mult),
                 reads=[kF, tabk], writes=["g1a"])
            P.op("dve", lambda e: e.tensor_tensor(out=hf, in0=vj(Vd)[:, :, SEL], in1=cT[:, :, SEL], op=ALU.mult), reads=["Vd", tabk], writes=["hfin"])
            P.op("dve", lambda e: e.tensor_tensor(out=hf, in0=hf, in1=tsel, op=ALU.add), reads=["g1a", "hfin"], writes=["hfin"])
            P.op("dve", lambda e: e.tensor_tensor(out=Tt, in0=Vd, in1=Wd, op=ALU.subtract), reads=["Vd", "Wd", "hfin"], writes=["g1a"])
            P.op("dve", lambda e: e.tensor_tensor(out=Ah, in0=Tt, in1=cosT, op=ALU.mult), reads=["g1a", tabk], writes=["g1b"])
            P.op("dve", lambda e: e.tensor_tensor(out=Bh, in0=Tt, in1=sinT, op=ALU.mult), reads=["g1a", tabk], writes=["g1b"])
            yAv = yA[d].rearrange("p (i g c) -> p i g c", i=8, g=GC)
            yBv = yB[d].rearrange("p (i g c) -> p i g c", i=8, g=GC)
            for q4 in range(GC // 4):
                for (c0, c1, npart, yv, yk) in ((0, 128, 128, yAv, "yA%d" % d), (128, NJ, 32, yBv, "yB%d" % d)):
                    ps, pk = nps()
                    for gi in range(4):
                        gl = q4 * 4 + gi
                        o = ps[0:npart, gi * 128:(gi + 1) * 128]
                        P.op("pe", lambda e, o=o, gl=gl, c0=c0, c1=c1: e.matmul(o, lhsT=vj(Ugc)[:, gl, c0:c1], rhs=v3g(Mm)[:, gl, :], start=True, stop=False),
                             reads=["Ugc", "Mm"], writes=[pk], skip_same=True)
                        P.op("pe", lambda e, o=o, gl=gl, c0=c0, c1=c1: e.matmul(o, lhsT=vj(Ah)[:, gl, c0:c1], rhs=v3g(PCst)[:, gl, :], start=False, stop=False),
                             reads=["g1b", "PC", "PC2"], writes=[pk], skip_same=True)
                        P.op("pe", lambda e, o=o, gl=gl, c0=c0, c1=c1: e.matmul(o, lhsT=vj(Bh)[:, gl, c0:c1], rhs=v3g(PCsw)[:, gl, :], start=False, stop=True),
                             reads=["g1b", "PC", "PC2"], writes=[pk], skip_same=True)
                    P.op("act", lambda e, ps=ps, npart=npart, yv=yv, q4=q4: e.activation(
                        out=yv[0:npart, :, q4 * 4:(q4 + 1) * 4, :].rearrange("p i g c -> p g i c"),
                        in_=ps[0:npart, :].rearrange("p (g i c) -> p g i c", g=4, i=8), func=ACT.Identity), reads=[pk], writes=[yk])

        def ssm_post(gc):
            for cc in range(GC // 8):
                zs = zst[cc % 2]
                zk = "zst"
                zA = zs[:, 0:1024].rearrange("p (j i) -> p j i", i=8)
                zB = zs[:, 1024:1280].rearrange("p (j i) -> p j i", i=8)
                for i in range(8):
                    ps, pk = nps()
                    for d in range(2):
                        yAv = yA[d].rearrange("p (i n) -> p i n", i=8)
                        yBv = yB[d].rearrange("p (i n) -> p i n", i=8)
                        P.op("pe", lambda e, ps=ps, d=d, yAv=yAv, i=i, cc=cc: e.matmul(ps[:, 0:128], lhsT=yAv[:, i, cc * 128:(cc + 1) * 128],
                                                                                     rhs=(ident_b if d == 0 else anti_b), start=(d == 0), stop=(d == 1)),
                             reads=["yA%d" % d, "ident_b", "anti_b"], writes=[pk], skip_same=True)
                    for d in range(2):
                        yBv = yB[d].rearrange("p (i n) -> p i n", i=8)
                        P.op("pe", lambda e, ps=ps, d=d, yBv=yBv, i=i, cc=cc: e.matmul(ps[:, 128:160], lhsT=yBv[0:32, i, cc * 128:(cc + 1) * 128],
                                                                                     rhs=(ident_b if d == 0 else anti32_b)[0:32, 0:32], start=(d == 0), stop=(d == 1)),
                             reads=["yB%d" % d, "ident_b", "anti32_b"], writes=[pk], skip_same=True)
                    P.op("act", lambda e, ps=ps, zA=zA, i=i: e.activation(out=zA[:, :, i], in_=ps[:, 0:128], func=ACT.Gelu), reads=[pk], writes=[zk])
                    P.op("act", lambda e, ps=ps, zB=zB, i=i: e.activation(out=zB[:, :, i], in_=ps[:, 128:160], func=ACT.Gelu), reads=[pk], writes=[zk])
                zi = gc * (GC // 8) + cc
                row = zi * 128
                P.dma("sp", lambda e, zs=zs, row=row: e.dma_start(out=z_T[row:row + 128, :], in_=zs), reads=[zk], writes=[("z_T", zi)])
                P.op("act", lambda e, zs=zs: e.activation(out=zsb, in_=zs, func=ACT.Identity), reads=[zk], writes=["zsb"])
                P.dma("sp", lambda e, row=row: e.dma_start(out=z_Tb[row:row + 128, :], in_=zsb), reads=["zsb"], writes=[("z_Tb", zi)])

        rounds = [(gc, d) for gc in range(64 // GC) for d in range(2)]
        ssm_tables(*rounds[0], "lo")
        ssm_tables(*rounds[0], "hi")
        for ri, (gc, d) in enumerate(rounds):
            if ri + 1 < len(rounds):
                ssm_tables(*rounds[ri + 1], "lo")
            ssm_round(gc, d)
            if ri + 1 < len(rounds):
                ssm_tables(*rounds[ri + 1], "hi")
            if d == 1:
                ssm_post(gc)
        P.dma("sp", lambda e: e.dma_start(out=dout["hfin"], in_=hfin4), reads=["hfin"], is_out=True)
        off[0] = base_mark
        P.barrier()
        if stop == 4:
            P.emit()
            return nc

        NST = 6
        stage = [alloc(512) for _ in range(NST)]
        srr[0] = 0
        aux = [alloc(512) for _ in range(NST)]
        arr = [0]

        def nstage():
            i = srr[0]
            srr[0] = (i + 1) % NST
            return stage[i], "stage%d" % i

        def naux():
            i = arr[0]
            arr[0] = (i + 1) % NST
            return aux[i], "aux%d" % i

        def load_src(dram_T, Kc, key, deps):
            t = alloc(Kc * NTOK, BF16)
            t3 = t.rearrange("p (k t) -> p k t", k=Kc)
            for k in range(Kc):
                P.dma("pool", lambda e, k=k: e.dma_start(out=t3[:, k, :], in_=dram_T[k * 128:(k + 1) * 128, :]), reads=deps, writes=[key])
            return t3

        def load_src_bf(dram_T, Kc, key, deps):
            t = alloc(Kc * NTOK, BF16)
            t3 = t.rearrange("p (k t) -> p k t", k=Kc)
            for k in range(Kc):
                P.dma("sp", lambda e, k=k: e.dma_start(out=t3[:, k, :], in_=dram_T[k * 128:(k + 1) * 128, :]), reads=deps, writes=[key])
            return t3

        zsrc = load_src_bf(z_Tb, 8, "zsrc", [("z_Tb", i) for i in range(8)])
        osrc = load_src_bf(o_T, 8, "osrc", [("o_T", h) for h in range(8)])
        z2b = alloc(8 * NTOK, BF16)
        z2src = z2b.rearrange("p (k t) -> p k t", k=8)
        msb = alloc(16 * NTOK, BF16)
        msrc = msb.rearrange("p (k t) -> p k t", k=16)

        def epi_glu(m, t0, n, ps, pkey):
            sg, sk = nstage()
            ax, ak = naux()
            P.dma("sp", lambda e: e.dma_start(out=ax[:, :n], in_=z_T[m * 128:(m + 1) * 128, t0:t0 + n]), reads=[("z_T", m)], writes=[ak])
            P.op("act", lambda e: e.activation(out=sg[:, :n], in_=ps[:, :n], func=ACT.Sigmoid), reads=[pkey], writes=[sk])
            P.op("dve", lambda e: e.tensor_tensor(out=z2src[:, m, t0:t0 + n], in0=sg[:, :n], in1=ax[:, :n], op=ALU.mult), reads=[sk, ak], writes=["z2src"])

        linear_fm(zsrc, "zsrc", "wglu", 8, 512, epi_glu)
        for b in range(4):
            wS, wSk = wload(din["wso"][b].rearrange("p k n -> p (k n)"), 4096)
            wA, wAk = wload(din["wao"][b].rearrange("p k n -> p (k n)"), 4096)
            wSv = wS[:, :4096].rearrange("p (k n) -> p k n", k=8)
            wAv = wA[:, :4096].rearrange("p (k n) -> p k n", k=8)

            def merge_tile(m, mi, t0, n, wSv=wSv, wAv=wAv, wSk=wSk, wAk=wAk):
                ax1, ak1 = naux()
                ax2, ak2 = naux()
                sg1, sk1 = nstage()
                sg2, sk2 = nstage()
                P.dma("sp", lambda e: e.dma_start(out=ax1[:, :n], in_=sg_T[m * 128:(m + 1) * 128, t0:t0 + n]), reads=[("sgs", m, t0)], writes=[ak1])
                P.dma("sp", lambda e: e.dma_start(out=ax2[:, :n], in_=sg_T[2048 + m * 128:2048 + (m + 1) * 128, t0:t0 + n]), reads=[("sga", m, t0)], writes=[ak2])
                psA, kA = nps()
                psB, kB = nps()
                for k in range(8):
                    P.op("pe", lambda e, k=k: e.matmul(psA[:, :n], lhsT=wSv[:, k, mi * 128:(mi + 1) * 128], rhs=z2src[:, k, t0:t0 + n],
                                                       start=(k == 0), stop=(k == 7)), reads=[wSk, "z2src"], writes=[kA], skip_same=True)
                for k in range(8):
                    P.op("pe", lambda e, k=k: e.matmul(psB[:, :n], lhsT=wAv[:, k, mi * 128:(mi + 1) * 128], rhs=osrc[:, k, t0:t0 + n],
                                                       start=(k == 0), stop=(k == 7)), reads=[wAk, "osrc"], writes=[kB], skip_same=True)
                P.op("dve", lambda e: e.tensor_tensor(out=sg1[:, :n], in0=psA[:, :n], in1=ax1[:, :n], op=ALU.mult), reads=[kA, ak1], writes=[sk1])
                P.op("dve", lambda e: e.tensor_tensor(out=sg2[:, :n], in0=psB[:, :n], in1=ax2[:, :n], op=ALU.mult), reads=[kB, ak2], writes=[sk2])
                P.op("pool", lambda e: e.tensor_tensor(out=msrc[:, m, t0:t0 + n], in0=sg1[:, :n], in1=sg2[:, :n], op=ALU.add), reads=[sk1, sk2], writes=["msrc"])

            for mi in range(4):
                for (t0, n) in TB:
                    merge_tile(b * 4 + mi, mi, t0, n)

        def epi_res(src_in, gate, dst_T, tagk, gkey):
            def epi(m, t0, n, ps, pkey):
                ci = 0 if t0 < 1024 else 1
                sg, sk = nstage()
                ax, ak = naux()
                P.dma("sp", lambda e: e.dma_start(out=ax[:, :n], in_=src_in(m, t0, n)), reads=[("x1", m, t0)], writes=[ak])
                P.op("dve", lambda e: e.scalar_tensor_tensor(out=sg[:, :n], in0=ps[:, :n], scalar=gate[:, m, ci:ci + 1], in1=ax[:, :n],
                                                            op0=ALU.mult, op1=ALU.add), reads=[pkey, ak, gkey], writes=[sk])
                P.dma("sp", lambda e: e.dma_start(out=dst_T[m * 128:(m + 1) * 128, t0:t0 + n], in_=sg[:, :n]), reads=[sk], writes=[(tagk, m, t0)])
            return epi

        linear_fm(msrc, "msrc", "wout", 16, 256, epi_res(lambda m, t0, n: din["xT"][:, m, t0:t0 + n], g1, x1_T, "x1", "modvB"))
        off[0] = base_mark
        P.barrier()
        if stop == 6:
            P.emit()
            return nc

        NRES = 28
        act_res = alloc(NRES * NTOK, BF16)
        act_res3 = act_res.rearrange("p (k t) -> p k t", k=NRES)
        m_res = off[0]
        xm2 = alloc(16 * NTOK, BF16)
        xm2v = xm2.rearrange("p (k t) -> p k t", k=16)
        allx1 = [("x1", m, t0) for m in range(16) for (t0, n) in TB]
        x1v = x1_T.rearrange("(k p) t -> p k t", p=128)

        def load_x1(dst, t0, n, key):
            P.dma("sp", lambda e: e.dma_start(out=dst, in_=x1v[:, :, t0:t0 + n]), reads=allx1, writes=[key])

        mk2 = off[0]
        rms_phase("n2", load_x1, lambda k, ci: a2v[:, k, ci:ci + 1], lambda k, ci: sh2[:, k, ci:ci + 1],
                  lambda xs3, k, t0, n: xm2v[:, k, t0:t0 + n], ["a2", "modvB"], "xm2")
        off[0] = mk2
        P.barrier()
        if stop == 7:
            P.emit()
            return nc

        convw = load_const("convw").rearrange("p (w c) -> p w c", w=3)
        convb = load_const("convb")
        bmL = load_const("bmL")
        bmR = load_const("bmR")
        accA = alloc(NTOK); accB = alloc(NTOK)
        tlA = alloc(NTOK); trA = alloc(NTOK); tlB = alloc(NTOK); trB = alloc(NTOK)
        actst = [alloc(NTOK, BF16), alloc(NTOK, BF16)]
        for b in range(22):
            wa, wak = wload(din["wupa"][b].rearrange("p k n -> p (k n)"), 4096)
            wbb, wbk = wload(din["wupb"][b].rearrange("p k n -> p (k n)"), 4096)
            wva = wa[:, :4096].rearrange("p (k n) -> p k n", k=16)
            wvb = wbb[:, :4096].rearrange("p (k n) -> p k n", k=16)
            for mi in range(2):
                m = b * 2 + mi
                for (wv, wk, base, acc, tl, tr, cm, tg) in ((wva, wak, 0, accA, tlA, trA, m, "A"), (wvb, wbk, 2048, accB, tlB, trB, m + 44, "B")):
                    keys = []
                    for bi, (t0, n) in enumerate(TB):
                        bkidx = base // 512 + bi
                        pkey = "ps%d" % bkidx
                        keys.append(pkey)
                        for k in range(16):
                            P.op("pe", lambda e, wv=wv, k=k, mi=mi, t0=t0, n=n, base=base: e.matmul(
                                psbig[:, base + t0:base + t0 + n], lhsT=wv[:, k, mi * 128:(mi + 1) * 128], rhs=xm2v[:, k, t0:t0 + n],
                                start=(k == 0), stop=(k == 15)), reads=[wk, "xm2"], writes=[pkey], skip_same=True)
                    hp = psbig[:, base:base + NTOK]
                    P.op("act", lambda e, hp=hp, acc=acc, cm=cm: e.activation(out=acc, in_=hp, func=ACT.Identity, scale=convw[:, 1, cm:cm + 1], bias=convb[:, cm:cm + 1]),
                         reads=keys + ["convw", "convb"], writes=["acc" + tg])
                    P.op("dve", lambda e, hp=hp, tl=tl, cm=cm: e.scalar_tensor_tensor(out=tl[:, 1:NTOK], in0=hp[:, 0:NTOK - 1], scalar=convw[:, 0, cm:cm + 1],
                                                                                   in1=bmL[:, 1:NTOK], op0=ALU.mult, op1=ALU.mult),
                         reads=keys + ["convw", "bmL"], writes=["tl" + tg])
                    P.op("dve", lambda e, hp=hp, tr=tr, cm=cm: e.scalar_tensor_tensor(out=tr[:, 0:NTOK - 1], in0=hp[:, 1:NTOK], scalar=convw[:, 2, cm:cm + 1],
                                                                                   in1=bmR[:, 0:NTOK - 1], op0=ALU.mult, op1=ALU.mult),
                         reads=keys + ["convw", "bmR"], writes=["tr" + tg])
                    P.op("dve", lambda e, acc=acc, tl=tl: e.tensor_tensor(out=acc[:, 1:NTOK], in0=acc[:, 1:NTOK], in1=tl[:, 1:NTOK], op=ALU.add),
                         reads=["acc" + tg, "tl" + tg], writes=["acc" + tg])
                    P.op("dve", lambda e, acc=acc, tr=tr: e.tensor_tensor(out=acc[:, 0:NTOK - 1], in0=acc[:, 0:NTOK - 1], in1=tr[:, 0:NTOK - 1], op=ALU.add),
                         reads=["acc" + tg, "tr" + tg], writes=["acc" + tg])
                ast = actst[m % 2]
                akk = "actst%d" % (m % 2)
                P.op("act", lambda e: e.activation(out=accA, in_=accA, func=ACT.Silu), reads=["accA"], writes=["accA"])
                if m < NRES:
                    P.op("dve", lambda e, m=m: e.tensor_tensor(out=act_res3[:, m, :], in0=accA, in1=accB, op=ALU.mult), reads=["accA", "accB"], writes=[("actres", m)])
                else:
                    P.op("dve", lambda e, ast=ast: e.tensor_tensor(out=ast, in0=accA, in1=accB, op=ALU.mult), reads=["accA", "accB"], writes=[akk])
                    P.dma("sp", lambda e, ast=ast, m=m: e.dma_start(out=act_T[m * 128:(m + 1) * 128, :], in_=ast), reads=[akk], writes=[("act", m)])
        off[0] = m_res
        P.barrier()
        if stop == 8:
            P.emit()
            return nc

        stage[:] = [alloc(512) for _ in range(NST)]
        srr[0] = 0
        aux[:] = [alloc(512) for _ in range(NST)]
        arr[0] = 0
        NTAIL = 44 - NRES
        asrc = alloc(NTAIL * NTOK, BF16)
        asrc3 = asrc.rearrange("p (k t) -> p k t", k=NTAIL)
        for k in range(NTAIL):
            P.dma("sp", lambda e, k=k: e.dma_start(out=asrc3[:, k, :], in_=act_T[(NRES + k) * 128:(NRES + k + 1) * 128, :]),
                  reads=[("act", NRES + k)], writes=[("asrc", k)])
        P.join("asrc", [("actres", m) for m in range(NRES)] + [("asrc", k) for k in range(NTAIL)])
        linear_fm(None, "asrc", "wdown", 44, 128, epi_res(lambda m, t0, n: x1_T[m * 128:(m + 1) * 128, t0:t0 + n], g2, x2_T, "x2", "modvB"),
                  src_k=lambda k: act_res3[:, k, :] if k < NRES else asrc3[:, k - NRES, :])
        off[0] = base_mark
        P.barrier()
        if stop == 9:
            P.emit()
            return nc

        x2v = x2_T.rearrange("(k p) t -> p k t", p=128)
        allx2 = [("x2", m, t0) for m in range(16) for (t0, n) in TB]
        gfin = gv[:, 2, :]

        def load_x2(dst, t0, n, key):
            P.dma("sp", lambda e: e.dma_start(out=dst, in_=x2v[:, :, t0:t0 + n]), reads=allx2, writes=[key])

        def store_y(xs3, t0, n, key):
            P.dma("sp", lambda e: e.dma_start(out=dout["yT"][:, :, t0:t0 + n], in_=xs3), reads=key, is_out=True)

        rms_phase("nf", load_x2, lambda k, ci: gfin[:, k:k + 1], None, lambda xs3, k, t0, n: xs3[:, k, :], ["gvec"], None, store_fn=store_y, NBK=512)
        P.emit()
    return nc


def _blk(W, NB):
    K, N = W.shape
    return np.ascontiguousarray(W.reshape(K // 128, 128, N // NB, NB).transpose(2, 1, 0, 3))


def _fm(v):
    return np.ascontiguousarray(v.reshape(-1, 128).T)


_NC_CACHE = {}
_DEBUG_HOOK = None


def kernel(x_prompt, x_sample, cache_k, cache_v, state_ssm_re, state_ssm_im, c, c_ctx,
           norm_mix_g, norm_ffn_g, w_mod, b_mod, w_in, ssm_lambda_re, ssm_lambda_im,
           ssm_log_dt, ssm_b_re, ssm_b_im, ssm_c_re, ssm_c_im, ssm_d, w_glu, attn_sink,
           w_ssm_o, w_attn_o, w_out, w_up, conv_w, conv_b, w_down, final_norm_g):
    f = lambda a: np.asarray(a, dtype=np.float32)
    x_prompt, x_sample = f(x_prompt), f(x_sample)
    w_in0 = f(w_in)[0]
    shared = {}
    shared["bmod"] = _fm(f(b_mod)[0])
    shared["gvec"] = np.concatenate([_fm(f(norm_mix_g)[0]), _fm(f(norm_ffn_g)[0]), _fm(f(final_norm_g))], axis=1)
    shared["wmod"] = _blk(f(w_mod)[0], 256)
    shared["wu"] = _blk(w_in0[:, 0:1024], 256)
    shared["wqk"] = _blk(w_in0[:, 1024:2304], 256)
    shared["wv"] = _blk(w_in0[:, 2304:2560], 256)
    shared["wgs"] = _blk(w_in0[:, 2560:4608], 256)
    shared["wga"] = _blk(w_in0[:, 4608:6656], 256)
    shared["wglu"] = _blk(f(w_glu)[0], 512)
    shared["wso"] = _blk(f(w_ssm_o)[0], 512)
    shared["wao"] = _blk(f(w_attn_o)[0], 512)
    shared["wout"] = _blk(f(w_out)[0], 256)
    wup = f(w_up)[0]
    shared["wupa"] = _blk(wup[:, :DFF], 256)
    shared["wupb"] = _blk(wup[:, DFF:], 256)
    shared["wdown"] = _blk(f(w_down)[0], 128)
    eye = np.eye(128, dtype=np.float32)
    shared["ident"] = eye
    shared["anti"] = np.ascontiguousarray(eye[::-1])
    a32 = np.zeros((128, 128), np.float32)
    a32[:32, :32] = np.eye(32, dtype=np.float32)[::-1]
    shared["anti32"] = a32
    psw = np.zeros((128, 128), np.float32)
    for m in range(64):
        psw[m + 64, m] = -1.0
        psw[m, m + 64] = 1.0
    shared["psw"] = psw
    sidx = np.arange(128) // 16
    shared["mmask"] = np.stack([(sidx[None, :] >= sidx[:, None]), (sidx[:, None] >= sidx[None, :])], axis=1).astype(np.float32)
    dup = lambda a: np.concatenate([a.T, a.T], axis=0)
    ssmp = np.zeros((8, 128, 2, 4, 64), np.float32)
    ssmb = np.zeros((128, 2, 2, 1024), np.float32)
    ssmc = np.zeros((128, 2, 2, 1024), np.float32)
    lre, lim, ldt = f(ssm_lambda_re)[0], f(ssm_lambda_im)[0], f(ssm_log_dt)[0]
    bre, bim, cre, cim = f(ssm_b_re)[0], f(ssm_b_im)[0], f(ssm_c_re)[0], f(ssm_c_im)[0]
    sre, sim = f(state_ssm_re), f(state_ssm_im)
    for d in range(2):
        ssmp[:, :, d, 0, :] = dup(lre[d])
        ssmp[:, :, d, 1, :] = dup(lim[d])
        ssmp[:, :, d, 2, :] = np.broadcast_to(ldt[d][None, :], (128, 64))
        br = bre[d].transpose(1, 0, 2).reshape(64, 1024)
        bi = bim[d].transpose(1, 0, 2).reshape(64, 1024)
        ssmb[:, d, 0] = np.concatenate([br, bi], 0)
        ssmb[:, d, 1] = np.concatenate([bi, br], 0)
        cr = cre[d].transpose(2, 0, 1).reshape(64, 1024)
        ci_ = cim[d].transpose(2, 0, 1).reshape(64, 1024)
        ssmc[:, d, 0] = np.concatenate([cr, ci_], 0)
        ssmc[:, d, 1] = np.concatenate([ci_, cr], 0)
        for b in range(2):
            ssmp[6 + b, :, d, 3, :] = np.concatenate([sre[b, 0, d].T, sim[b, 0, d].T], 0)
    shared["ssmb"] = ssmb
    shared["ssmc"] = ssmc
    sg = np.ones((128, 2), np.float32)
    sg[64:, 0] = -1.0
    sg[:64, 1] = -1.0
    shared["sgn"] = sg
    shared["dvec"] = np.ascontiguousarray(np.tile(f(ssm_d)[0].reshape(64, 16).T, (8, 1)))
    shared["sink"] = np.ascontiguousarray(np.broadcast_to(f(attn_sink)[0][None, :], (128, 8)))
    cw = f(conv_w)[0]
    shared["convw"] = np.ascontiguousarray(np.stack([_fm(cw[i]) for i in range(3)], axis=1))
    shared["convb"] = _fm(f(conv_b)[0])

    T = 1024
    row = np.repeat(np.arange(T // 64, dtype=np.float32), 64)
    col = np.tile(np.arange(64, dtype=np.float32), T // 64)
    inv = (10000.0 ** (-np.arange(32, dtype=np.float32) / 32)).astype(np.float32)
    ang = np.concatenate([row[None, :] * inv[:, None], col[None, :] * inv[:, None]], axis=0).astype(np.float32)
    cosA, sinA = np.cos(ang), np.sin(ang)
    ropeC_s = np.concatenate([cosA, cosA], 0)
    ropeS_s = np.concatenate([-sinA, sinA], 0)
    perm = np.array([(p % 64 // 32) * 64 + (p // 64) * 32 + p % 32 for p in range(128)])

    qi = np.arange(128)[:, None]
    kj = np.arange(384)[None, :] - 128
    NEG = np.float32(-1e30)
    in_maps = []
    c_f, cctx = f(c), f(c_ctx)
    ck_all, cv_all = f(cache_k), f(cache_v)
    for core in range(8):
        m = dict(shared)
        if core < 6:
            xa = x_prompt[5 * core:5 * core + 4].reshape(1024, D)
            xb = x_prompt[5 * core + 4]
            condA = cctx
        else:
            xa = x_sample[core - 6]
            xb = x_prompt[30 + core - 6]
            condA = c_f[core - 6]
        x = np.concatenate([xa, xb], 0)
        m["xT"] = np.ascontiguousarray(x.reshape(NTOK, 16, 128).transpose(2, 1, 0))
        m["cond"] = np.ascontiguousarray(np.stack([_fm(condA), _fm(cctx)], axis=2).reshape(128, 32))
        m["ssmp"] = ssmp[core]
        bm = np.ones((128, NJ), np.float32)
        bm[:, 128] = 0
        maskL = np.full((128, 10, 384), NEG, np.float32)
        cmask = np.full((128, 10), NEG, np.float32)
        bmL = np.ones((128, NTOK), np.float32)
        bmR = np.ones((128, NTOK), np.float32)
        rC = np.ones((128, NTOK), np.float32)
        rS = np.zeros((128, NTOK), np.float32)
        if core < 6:
            bm[:, [32, 64, 96]] = 0
            starts = [0, 256, 512, 768, 1024]
            for qt in range(10):
                if qt % 2 == 0:
                    maskL[:, qt, 128:384] = 0
                else:
                    maskL[:, qt, 0:256] = 0
            ck = np.zeros((128, 2, 256), np.float32)
            cv = np.zeros((128, 2, 256), np.float32)
        else:
            starts = [0, 1024]
            band = np.where(np.abs(qi - kj) <= 128, np.float32(0), NEG).astype(np.float32)
            for qt in range(8):
                mk = band.copy()
                if qt == 0:
                    mk[:, 0:128] = NEG
                if qt == 7:
                    mk[:, 256:384] = NEG
                maskL[:, qt] = mk
                cmask[:, qt] = 0
            maskL[:, 8, 128:384] = 0
            maskL[:, 9, 0:256] = 0
            rC[:, :1024] = ropeC_s
            rS[:, :1024] = ropeS_s
            b = core - 6
            ck = np.ascontiguousarray(ck_all[b, 0].transpose(2, 1, 0)[perm])
            cv = np.ascontiguousarray(cv_all[b, 0].reshape(2, 128, 256).transpose(1, 0, 2))
        for s0 in starts:
            bmL[:, s0] = 0
        for e0 in starts[1:] + [NTOK]:
            bmR[:, e0 - 1] = 0
        m.update(bm=bm, maskL=maskL, cmask=cmask, bmL=bmL, bmR=bmR, ropeC=rC, ropeS=rS, ckT=ck, cv=cv)
        in_maps.append(m)

    if _DEBUG_HOOK is not None:
        return _DEBUG_HOOK(in_maps)
    if "nc" not in _NC_CACHE:
        _NC_CACHE["nc"] = build()
    nc = _NC_CACHE["nc"]
    res = run_bass_kernel_spmd(nc, in_maps, core_ids=list(range(8)))
    R = res.results

    y_prompt = np.zeros((32, 256, D), np.float32)
    y_sample = np.zeros((2, 1024, D), np.float32)
    nk = np.zeros((32, 1, 256, 2, 128), np.float32)
    nv = np.zeros((32, 1, 256, 2, 128), np.float32)
    hre = np.zeros((32, 1, 2, 64, 64), np.float32)
    him = np.zeros((32, 1, 2, 64, 64), np.float32)
    for core in range(8):
        y = R[core]["yT"].transpose(2, 1, 0).reshape(NTOK, D)
        k = R[core]["kTo"].T.reshape(NTOK, 2, 128)
        v = R[core]["vo"].reshape(NTOK, 2, 128)
        hf = R[core]["hfin"]
        if core < 6:
            seqs = [(5 * core + i, i * 256) for i in range(4)] + [(5 * core + 4, 1024)]
        else:
            y_sample[core - 6] = y[:1024]
            seqs = [(30 + core - 6, 1024)]
        for (sidx_, t0) in seqs:
            y_prompt[sidx_] = y[t0:t0 + 256]
            nk[sidx_, 0] = k[t0:t0 + 256]
            nv[sidx_, 0] = v[t0:t0 + 256]
            si = t0 // 256
            for d in range(2):
                if si == 4:
                    q = 4
                else:
                    q = si if d == 0 else 3 - si
                hre[sidx_, 0, d] = hf[0:64, d, :, q].T
                him[sidx_, 0, d] = hf[64:128, d, :, q].T
    return (y_prompt, y_sample, nk, nv, hre, him)
```
